# Optimizing a Trainium2 kernel written in Bass

```python
import jax, jax.numpy as jnp
from jax import lax
import numpy as np

D_MODEL = 2048
BATCH = 8
SEQ = 4096
DEPTH = 4

CHUNK = 64
N_MIXERS = 4
EPS = 1e-6
N_MOD = 6

CONV_WIDTH = 31

GLA_HEADS = 4
GLA_DK = D_MODEL // (2 * GLA_HEADS)
GLA_DV = D_MODEL // GLA_HEADS
GLA_GATE_RANK = 16
GLA_GATE_TEMP = 16.0

RET_HEADS = 8
RET_DK = D_MODEL // RET_HEADS
RET_DV = 2 * RET_DK
ROPE_BASE = 10000.0

SGU_CHUNK = 128
SGU_HEADS = 8
SGU_WIDTH = 2 * D_MODEL
SGU_HEAD_DIM = SGU_WIDTH // SGU_HEADS

FFN_HIDDEN = 5632
FFN_CONV_WIDTH = 3

GLA_IN = 2 * GLA_HEADS * GLA_DK + 2 * GLA_HEADS * GLA_DV + GLA_GATE_RANK
RET_IN = 2 * RET_HEADS * RET_DK + 2 * RET_HEADS * RET_DV

kernel_name = 'hybrid_streaming_encoder_interleaved'


def _n_uses(m):
    return (DEPTH - m + N_MIXERS - 1) // N_MIXERS


def _rms(x):
    x32 = x.astype(jnp.float32)
    return x32 * lax.rsqrt(jnp.mean(jnp.square(x32), axis=-1, keepdims=True) + EPS)


def _rms_norm(x, g):
    return (_rms(x) * g.astype(jnp.float32)).astype(x.dtype)


def _layer_norm(x, g, b):
    x32 = x.astype(jnp.float32)
    mu = jnp.mean(x32, axis=-1, keepdims=True)
    var = jnp.mean(jnp.square(x32 - mu), axis=-1, keepdims=True)
    y = (x32 - mu) * lax.rsqrt(var + EPS) * g.astype(jnp.float32) + b.astype(jnp.float32)
    return y.astype(x.dtype)


def _causal_dwconv(x, w, b):
    k = w.shape[0]
    xp = jnp.pad(x, ((0, 0), (k - 1, 0), (0, 0)))
    y = lax.conv_general_dilated(xp, w[:, None, :].astype(x.dtype), (1,), 'VALID',
                                 dimension_numbers=('NWC', 'WIO', 'NWC'),
                                 feature_group_count=x.shape[-1])
    return y + b


def _to_chunks(t):
    B, L, H, d = t.shape
    return t.reshape(B, L // CHUNK, CHUNK, H, d).transpose(1, 0, 3, 2, 4)


def _from_chunks(t):
    nC, B, H, C, d = t.shape
    return t.transpose(1, 0, 3, 2, 4).reshape(B, nC * C, H, d)


def _rotary(t, pos):
    half = t.shape[-1] // 2
    inv_freq = ROPE_BASE ** (-jnp.arange(half, dtype=jnp.float32) / half)
    ang = pos.astype(jnp.float32)[:, None] * inv_freq
    cos = jnp.cos(ang)[None, :, None, :]
    sin = jnp.sin(ang)[None, :, None, :]
    t1, t2 = t[..., :half], t[..., half:]
    return jnp.concatenate([t1 * cos - t2 * sin, t1 * sin + t2 * cos], axis=-1)


def _conv_module(h, w_in, dw_w, dw_b, ln_g, ln_b, w_out):
    a, gate = jnp.split(h @ w_in, 2, axis=-1)
    z = _causal_dwconv(a * jax.nn.sigmoid(gate), dw_w, dw_b)
    z = jax.nn.silu(_layer_norm(z, ln_g, ln_b))
    return z @ w_out


def _gla_mixer(h, w_in, w_gate, b_gate, norm_g, w_out):
    B, L, _ = h.shape
    f32 = jnp.float32
    hk, hv = GLA_HEADS * GLA_DK, GLA_HEADS * GLA_DV
    q, k, v, r, g_low = jnp.split(h @ w_in, [hk, 2 * hk, 2 * hk + hv, 2 * hk + 2 * hv], axis=-1)
    log_a = jax.nn.log_sigmoid((g_low @ w_gate + b_gate).astype(f32)) / GLA_GATE_TEMP
    q = _to_chunks(q.astype(f32).reshape(B, L, GLA_HEADS, GLA_DK)) * GLA_DK ** -0.5
    k = _to_chunks(k.astype(f32).reshape(B, L, GLA_HEADS, GLA_DK))
    v = _to_chunks(v.astype(f32).reshape(B, L, GLA_HEADS, GLA_DV))
    b = jnp.cumsum(_to_chunks(log_a.reshape(B, L, GLA_HEADS, GLA_DK)), axis=3)
    b_last = b[..., -1:, :]
    e_pos, e_neg = jnp.exp(b), jnp.exp(-b)
    q_fwd, k_fwd = q * e_pos, k * e_neg
    q_bwd, k_bwd = q * e_neg, k * e_pos
    s_fwd = jnp.einsum('nbhtd,nbhsd->nbhts', q_fwd, k_fwd)
    s_bwd = jnp.einsum('nbhtd,nbhsd->nbhts', q_bwd, k_bwd)
    causal = jnp.tril(jnp.ones((CHUNK, CHUNK), dtype=bool))
    scores = jnp.where(causal, s_fwd, s_bwd)
    o_intra = jnp.einsum('nbhts,nbhsv->nbhtv', scores, v)
    k_state = k * jnp.exp(b_last - b)
    decay = jnp.exp(b_last[..., 0, :])

    def step(state, xs):
        q_c, k_c, v_c, d_c = xs
        o = jnp.einsum('bhtd,bhdv->bhtv', q_c, state)
        state = state * d_c[..., None] + jnp.einsum('bhsd,bhsv->bhdv', k_c, v_c)
        return state, o

    state0 = jnp.zeros((B, GLA_HEADS, GLA_DK, GLA_DV), f32)
    _, o_inter = lax.scan(step, state0, (q_fwd, k_state, v, decay))
    o = _rms(_from_chunks(o_intra + o_inter)) * norm_g.astype(f32)
    o = o.reshape(B, L, hv).astype(h.dtype) * jax.nn.silu(r)
    return o @ w_out


def _retention_mixer(h, w_in, w_out):
    B, L, _ = h.shape
    f32 = jnp.float32
    hk, hv = RET_HEADS * RET_DK, RET_HEADS * RET_DV
    q, k, v, g = jnp.split(h @ w_in, [hk, 2 * hk, 2 * hk + hv], axis=-1)
    pos = jnp.arange(L)
    q = _rotary(q.astype(f32).reshape(B, L, RET_HEADS, RET_DK), pos)
    k = _rotary(k.astype(f32).reshape(B, L, RET_HEADS, RET_DK), pos) * RET_DK ** -0.5
    v = v.astype(f32).reshape(B, L, RET_HEADS, RET_DV)
    q, k, v = _to_chunks(q), _to_chunks(k), _to_chunks(v)
    log_g = jnp.log1p(-jnp.exp2(-5.0 - jnp.arange(RET_HEADS, dtype=f32)))
    idx = jnp.arange(CHUNK, dtype=f32)
    d_intra = jnp.exp(log_g[:, None, None] * jnp.abs(idx[:, None] - idx[None, :]))
    q_in = q * jnp.exp(log_g[:, None] * (idx + 1.0))[:, :, None]
    k_in = k * jnp.exp(log_g[:, None] * (CHUNK - 1.0 - idx))[:, :, None]
    chunk_decay = jnp.exp(log_g * CHUNK)[None, :, None, None]
    scores = jnp.einsum('nbhtd,nbhsd->nbhts', q, k) * d_intra
    o_intra = jnp.einsum('nbhts,nbhsv->nbhtv', scores, v)

    def step(state, xs):
        q_c, k_c, v_c = xs
        o = jnp.einsum('bhtd,bhdv->bhtv', q_c, state)
        state = state * chunk_decay + jnp.einsum('bhsd,bhsv->bhdv', k_c, v_c)
        return state, o

    state0 = jnp.zeros((B, RET_HEADS, RET_DK, RET_DV), f32)
    _, o_inter = lax.scan(step, state0, (q_in, k_in, v))
    o = _rms(_from_chunks(o_intra + o_inter))
    o = o.reshape(B, L, hv).astype(h.dtype) * jax.nn.silu(g)
    return o @ w_out


def _sgu_mixer(h, w_in, ln_g, ln_b, w_s, b_s, w_out):
    B, L, _ = h.shape
    u, v = jnp.split(jax.nn.gelu(h @ w_in, approximate=False), 2, axis=-1)
    v = _layer_norm(v, ln_g, ln_b).reshape(B, L // SGU_CHUNK, SGU_CHUNK, SGU_HEADS, SGU_HEAD_DIM)
    p = jnp.arange(SGU_CHUNK)
    mask = (p[None, :] // CHUNK) <= (p[:, None] // CHUNK)
    w = jnp.where(mask[None], w_s, 0)
    sv = jnp.einsum('hij,bnjhd->bnihd', w, v) + b_s.T[None, None, :, :, None]
    return (u * sv.reshape(B, L, SGU_WIDTH)) @ w_out


def _conv_ffn(h, w_in, conv_w, conv_b, w_out):
    a, b = jnp.split(h @ w_in, 2, axis=-1)
    a = _causal_dwconv(a, conv_w, conv_b)
    return (jax.nn.silu(a) * b) @ w_out


def setup_inputs(seed: int = 0) -> dict:
    key = jax.random.key(seed)
    ks = iter(jax.random.split(key, 32))
    f32 = jnp.float32

    def nrm(shape, scale):
        return jax.random.normal(next(ks), shape, f32) * scale

    D, F = D_MODEL, FFN_HIDDEN
    nA, nB, nC, nD = (_n_uses(m) for m in range(N_MIXERS))
    return {
        'x': nrm((BATCH, SEQ, D), 1.0),
        'c': nrm((BATCH, D), 1.0),
        'ada_w': nrm((DEPTH, D, N_MOD * D), 0.5 * D ** -0.5),
        'ada_b': nrm((DEPTH, N_MOD * D), 0.02),
        'norm_g': 1.0 + nrm((DEPTH, 4, D), 0.05),
        'ffn_w_in': nrm((DEPTH, D, 2 * F), D ** -0.5),
        'ffn_conv_w': nrm((DEPTH, FFN_CONV_WIDTH, F), FFN_CONV_WIDTH ** -0.5),
        'ffn_conv_b': nrm((DEPTH, F), 0.02),
        'ffn_w_out': nrm((DEPTH, F, D), F ** -0.5),
        'cm_w_in': nrm((nA, D, 2 * D), D ** -0.5),
        'cm_dw_w': nrm((nA, CONV_WIDTH, D), CONV_WIDTH ** -0.5),
        'cm_dw_b': nrm((nA, D), 0.02),
        'cm_ln_g': 1.0 + nrm((nA, D), 0.05),
        'cm_ln_b': nrm((nA, D), 0.05),
        'cm_w_out': nrm((nA, D, D), D ** -0.5),
        'gla_w_in': nrm((nB, D, GLA_IN), D ** -0.5),
        'gla_w_gate': nrm((nB, GLA_GATE_RANK, GLA_HEADS * GLA_DK), GLA_GATE_RANK ** -0.5),
        'gla_b_gate': nrm((nB, GLA_HEADS * GLA_DK), 0.1),
        'gla_norm_g': 1.0 + nrm((nB, GLA_DV), 0.05),
        'gla_w_out': nrm((nB, GLA_HEADS * GLA_DV, D), (GLA_HEADS * GLA_DV) ** -0.5),
        'ret_w_in': nrm((nC, D, RET_IN), D ** -0.5),
        'ret_w_out': nrm((nC, RET_HEADS * RET_DV, D), (RET_HEADS * RET_DV) ** -0.5),
        'sgu_w_in': nrm((nD, D, 2 * SGU_WIDTH), D ** -0.5),
        'sgu_ln_g': 1.0 + nrm((nD, SGU_WIDTH), 0.05),
        'sgu_ln_b': nrm((nD, SGU_WIDTH), 0.05),
        'sgu_w_s': nrm((nD, SGU_HEADS, SGU_CHUNK, SGU_CHUNK), SGU_CHUNK ** -0.5),
        'sgu_b_s': 1.0 + nrm((nD, SGU_HEADS, SGU_CHUNK), 0.1),
        'sgu_w_out': nrm((nD, SGU_WIDTH, D), SGU_WIDTH ** -0.5),
    }


def reference(x, c, ada_w, ada_b, norm_g, ffn_w_in, ffn_conv_w, ffn_conv_b, ffn_w_out,
              cm_w_in, cm_dw_w, cm_dw_b, cm_ln_g, cm_ln_b, cm_w_out,
              gla_w_in, gla_w_gate, gla_b_gate, gla_norm_g, gla_w_out,
              ret_w_in, ret_w_out,
              sgu_w_in, sgu_ln_g, sgu_ln_b, sgu_w_s, sgu_b_s, sgu_w_out):
    B, L, D = x.shape
    c_act = jax.nn.silu(c)
    for i in range(DEPTH):
        m, j = i % N_MIXERS, i // N_MIXERS
        mod = (c_act @ ada_w[i] + ada_b[i]).reshape(B, N_MOD, 1, D)
        shift_m, scale_m, gate_m = mod[:, 0], mod[:, 1], mod[:, 2]
        shift_f, scale_f, gate_f = mod[:, 3], mod[:, 4], mod[:, 5]
        h = _rms_norm(x, norm_g[i, 0]) * (1.0 + scale_m) + shift_m
        if m == 0:
            y = _conv_module(h, cm_w_in[j], cm_dw_w[j], cm_dw_b[j], cm_ln_g[j], cm_ln_b[j], cm_w_out[j])
        elif m == 1:
            y = _gla_mixer(h, gla_w_in[j], gla_w_gate[j], gla_b_gate[j], gla_norm_g[j], gla_w_out[j])
        elif m == 2:
            y = _retention_mixer(h, ret_w_in[j], ret_w_out[j])
        else:
            y = _sgu_mixer(h, sgu_w_in[j], sgu_ln_g[j], sgu_ln_b[j], sgu_w_s[j], sgu_b_s[j], sgu_w_out[j])
        x = x + gate_m * _rms_norm(y, norm_g[i, 1])
        h = _rms_norm(x, norm_g[i, 2]) * (1.0 + scale_f) + shift_f
        y = _conv_ffn(h, ffn_w_in[i], ffn_conv_w[i], ffn_conv_b[i], ffn_w_out[i])
        x = x + gate_f * _rms_norm(y, norm_g[i, 3])
    return x
```

```python
import contextlib
import numpy as np
import concourse.bass as bass
import concourse.mybir as mybir
from concourse.bass_utils import run_bass_kernel_spmd

F32 = mybir.dt.float32
BF16 = mybir.dt.bfloat16
AF = mybir.ActivationFunctionType
ALU = mybir.AluOpType

D = 2048
L_SEQ = 4096
NB = 8
DEPTH = 4
TB = 512
NTB = L_SEQ // TB
KC = D // 128
FFH = 5632
FFC = FFH // 128
EPS = 1e-6
WSLOT = 4096
NWSLOT = 4
STRICT = True


class _Op:
    __slots__ = ("idx", "eng", "fn", "deps", "dma", "slot", "val", "sig", "cnt")


class Sched:
    ENGS = ("pe", "act", "dve", "pool", "sp")
    NSLOT = 12
    EPOCH = 30000

    def __init__(self):
        self.ops = []
        self.streams = {e: [] for e in self.ENGS}
        self.lastw = {}
        self.rdc = {}
        self.rdd = {}
        self.dcount = {e: 0 for e in self.ENGS}
        self.slot_last = {}

    def add(self, eng, fn, reads=(), writes=(), dma=False):
        op = _Op()
        op.idx = len(self.ops)
        op.eng = eng
        op.fn = fn
        op.dma = dma
        op.sig = False
        op.cnt = 0
        op.slot = -1
        op.val = 0
        deps = set()
        for k in reads:
            w = self.lastw.get(k)
            if w is not None:
                deps.add(w)
        for k in writes:
            w = self.lastw.get(k)
            if w is not None:
                deps.add(w)
            r = self.rdc.get(k)
            if r:
                deps.update(r.values())
            r = self.rdd.get(k)
            if r:
                deps.update(r)
        if dma:
            n = self.dcount[eng]
            self.dcount[eng] = n + 1
            op.slot = n % self.NSLOT
            op.val = 16 * (n // self.NSLOT + 1)
            prev = self.slot_last.get((eng, op.slot))
            if prev is not None:
                deps.add(prev)
            self.slot_last[(eng, op.slot)] = op.idx
        for k in reads:
            if dma:
                self.rdd.setdefault(k, []).append(op.idx)
            else:
                self.rdc.setdefault(k, {})[eng] = op.idx
        for k in writes:
            self.lastw[k] = op.idx
            self.rdc[k] = {}
            self.rdd[k] = []
        deps.discard(op.idx)
        op.deps = deps
        self.ops.append(op)
        self.streams[eng].append(op)
        return op

    def _needs_wait(self, op, d):
        if d.dma:
            return True
        if d.eng == op.eng:
            if op.eng == "pe":
                return False
            return STRICT
        return True

    def finalize(self):
        for op in self.ops:
            for di in op.deps:
                d = self.ops[di]
                if not d.dma and self._needs_wait(op, d):
                    d.sig = True
        self.nepoch = {}
        for e in self.ENGS:
            c = 0
            for op in self.streams[e]:
                if op.sig and not op.dma:
                    c += 1
                    op.cnt = c
            self.nepoch[e] = max(1, (c + self.EPOCH - 1) // self.EPOCH)

    def emit(self, nc):
        self.finalize()
        with contextlib.ExitStack() as st:
            csem = {}
            for e in self.ENGS:
                csem[e] = [st.enter_context(nc.semaphore(f"c_{e}_{i}")) for i in range(self.nepoch[e])]
            dsem = {}
            for e in self.ENGS:
                if self.dcount[e]:
                    dsem[e] = [st.enter_context(nc.semaphore(f"d_{e}_{i}")) for i in range(self.NSLOT)]
            block = st.enter_context(nc.Block())
            ops = self.ops
            EP = self.EPOCH

            def run(ename, eng):
                known_c = {}
                known_d = {}
                for op in self.streams[ename]:
                    need_c = {}
                    need_d = {}
                    for di in op.deps:
                        d = ops[di]
                        if not self._needs_wait(op, d):
                            continue
                        if d.dma:
                            k = (d.eng, d.slot)
                            if need_d.get(k, 0) < d.val:
                                need_d[k] = d.val
                        else:
                            if need_c.get(d.eng, 0) < d.cnt:
                                need_c[d.eng] = d.cnt
                    for e2, c in need_c.items():
                        if known_c.get(e2, 0) >= c:
                            continue
                        known_c[e2] = c
                        eng.wait_ge(csem[e2][(c - 1) // EP], (c - 1) % EP + 1)
                    for k, v in need_d.items():
                        if known_d.get(k, 0) >= v:
                            continue
                        known_d[k] = v
                        eng.wait_ge(dsem[k[0]][k[1]], v)
                    ins = op.fn(eng)
                    if op.dma:
                        ins.then_inc(dsem[ename][op.slot], 16)
                    elif op.sig:
                        c = op.cnt
                        ins.then_inc(csem[ename][(c - 1) // EP], 1)

            @block.tensor
            def _(e):
                run("pe", e)

            @block.scalar
            def _(e):
                run("act", e)

            @block.vector
            def _(e):
                run("dve", e)

            @block.gpsimd
            def _(e):
                run("pool", e)

            @block.sync
            def _(e):
                run("sp", e)


def _tile_weight(W, nk, fw, col_groups=None):
    K, F = W.shape
    nkt = K // (128 * nk)
    if col_groups is None:
        col_groups = [(c, fw) for c in range(0, F, fw)]
    tiles = []
    for (c0, w) in col_groups:
        blk = W[:, c0:c0 + w]
        if w < fw:
            blk = np.concatenate([blk, np.zeros((K, fw - w), W.dtype)], axis=1)
        for kt in range(nkt):
            t = blk[kt * nk * 128:(kt + 1) * nk * 128].reshape(nk, 128, fw).transpose(1, 0, 2)
            tiles.append(t.reshape(128, nk * fw))
    return np.ascontiguousarray(np.stack(tiles, 0))


def _pvec(v):
    v = np.asarray(v, np.float32).reshape(-1, 128)
    return np.ascontiguousarray(v.T)


def _ret_consts():
    f32 = np.float32
    half = 128
    inv_freq = (np.float32(10000.0) ** (-np.arange(half, dtype=f32) / f32(half))).astype(f32)
    ang = (np.arange(L_SEQ, dtype=f32)[None, :] * inv_freq[:, None]).astype(f32)
    H = 8
    lg = np.log1p(-np.exp2(-5.0 - np.arange(H, dtype=f32))).astype(f32)
    idx = np.arange(64, dtype=f32)
    gq = np.exp(lg[:, None] * (idx + 1.0)).astype(f32)
    gk = (np.exp(lg[:, None] * (63.0 - idx)) * f32(256.0 ** -0.5)).astype(f32)
    s_ = idx[:, None]
    t_ = idx[None, :]
    ex = np.where(s_ <= t_, -64.0, 2.0 * (s_ - t_) - 64.0).astype(f32)
    mt = np.exp(lg[:, None, None] * ex[None]).astype(f32)
    mt = np.concatenate([mt, mt], axis=1).transpose(1, 0, 2)
    rep = lambda a: np.ascontiguousarray(np.broadcast_to(a.reshape(1, -1), (128, a.size)).astype(f32))
    return {
        "rot_cos": np.ascontiguousarray(np.cos(ang).astype(f32)),
        "rot_sin": np.ascontiguousarray(np.sin(ang).astype(f32)),
        "ret_c": np.ascontiguousarray(np.concatenate([rep(gq), rep(gk), mt.reshape(128, H * 64)], axis=1)),
        "ident": np.eye(128, dtype=f32),
    }


def _gla_consts():
    f32 = np.float32
    s_ = np.arange(128)[:, None]
    t_ = np.arange(128)[None, :]
    same = (s_ // 64) == (t_ // 64)
    tri = np.where(same & (s_ <= t_), -1.0 / 16.0, 0.0).astype(f32)
    tri2 = np.where(same & (s_ > t_), -1.0 / 16.0, 0.0).astype(f32)
    sl = (np.arange(128) % 64)[:, None]
    tl = np.arange(64)[None, :]
    um = (sl <= tl).astype(f32)
    lm = (sl > tl).astype(f32)
    return np.ascontiguousarray(np.concatenate([tri, tri2, um, lm], axis=1))


class _VecPack:
    def __init__(self):
        self.cols = []
        self.off = {}
        self.n = 0

    def add(self, name, arr):
        self.off[name] = (self.n, arr.shape[1])
        self.cols.append(arr)
        self.n += arr.shape[1]

    def build(self):
        return np.ascontiguousarray(np.concatenate(self.cols, axis=1).astype(np.float32))


def weight_specs():
    specs = {}
    for i in range(DEPTH):
        specs[f"ada{i}"] = dict(K=D, F=6 * D, nk=16, fw=256)
        specs[f"ffin{i}"] = dict(K=D, F=2 * FFH, nk=16, fw=256)
        specs[f"ffout{i}"] = dict(K=FFH, F=D, nk=22, fw=128)
    specs["cmin"] = dict(K=D, F=2 * D, nk=16, fw=256)
    specs["cmout"] = dict(K=D, F=D, nk=16, fw=256)
    specs["sguin"] = dict(K=D, F=4 * D, nk=16, fw=256)
    specs["retin"] = dict(K=D, F=6 * D, nk=16, fw=256)
    specs["glain"] = dict(K=D, F=6144, nk=16, fw=256)
    specs["glag"] = dict(K=D, F=16, nk=16, fw=16)
    specs["glaout"] = dict(K=D, F=D, nk=16, fw=256)
    specs["retout"] = dict(K=2 * D, F=D, nk=16, fw=256)
    specs["sguout"] = dict(K=2 * D, F=D, nk=16, fw=256)
    for s in specs.values():
        s["nkt"] = s["K"] // (128 * s["nk"])
        s["ng"] = s["F"] // s["fw"]
        s["nt"] = s["nkt"] * s["ng"]
        s["E"] = s["nk"] * s["fw"]
    return specs


def prep_inputs(inp, cfg):
    shared = {}
    for i in cfg["layers"]:
        shared[f"ada{i}"] = _tile_weight(inp["ada_w"][i], 16, 256)
        shared[f"ffin{i}"] = _tile_weight(inp["ffn_w_in"][i], 16, 256)
        shared[f"ffout{i}"] = _tile_weight(inp["ffn_w_out"][i], 22, 128)
    if 0 in cfg["layers"] and cfg.get("mixer", True):
        shared["cmin"] = _tile_weight(inp["cm_w_in"][0], 16, 256)
        shared["cmout"] = _tile_weight(inp["cm_w_out"][0], 16, 256)
    if 3 in cfg["layers"] and cfg.get("mixer", True):
        shared["sguin"] = _tile_weight(inp["sgu_w_in"][0], 16, 256)
        shared["sguout"] = _tile_weight(inp["sgu_w_out"][0], 16, 256)
    if 2 in cfg["layers"] and cfg.get("mixer", True):
        shared["retin"] = _tile_weight(inp["ret_w_in"][0], 16, 256)
        shared["retout"] = _tile_weight(inp["ret_w_out"][0], 16, 256)
    shared.update(_ret_consts())
    if 1 in cfg["layers"] and cfg.get("mixer", True):
        shared["glain"] = _tile_weight(inp["gla_w_in"][0][:, :6144], 16, 256)
        shared["glag"] = _tile_weight(inp["gla_w_in"][0][:, 6144:6160], 16, 16)
        shared["glaout"] = _tile_weight(inp["gla_w_out"][0], 16, 256)
    wg = np.zeros((32, 1024), np.float32)
    wg[0:16] = inp["gla_w_gate"][0]
    wg[16] = inp["gla_b_gate"][0]
    shared["gla_wg"] = wg
    shared["gla_c"] = _gla_consts()
    shared["sgu_wsT"] = np.ascontiguousarray(inp["sgu_w_s"][0].transpose(2, 0, 1).reshape(128, 1024).astype(np.float32))
    shared["sgu_bsb"] = np.ascontiguousarray(np.broadcast_to(inp["sgu_b_s"][0].reshape(1, 1024), (128, 1024)).astype(np.float32))
    vp = _VecPack()
    for i in range(DEPTH):
        vp.add(f"adab{i}", _pvec(inp["ada_b"][i]))
        for j in range(4):
            vp.add(f"ng{i}_{j}", _pvec(inp["norm_g"][i, j]))
        for k in range(3):
            vp.add(f"fcw{i}_{k}", _pvec(inp["ffn_conv_w"][i, k]))
        vp.add(f"fcb{i}", _pvec(inp["ffn_conv_b"][i]))
    for k in range(31):
        vp.add(f"cmw{k}", _pvec(inp["cm_dw_w"][0, k]))
    vp.add("cmb", _pvec(inp["cm_dw_b"][0]))
    vp.add("cmlg", _pvec(inp["cm_ln_g"][0]))
    vp.add("cmlb", _pvec(inp["cm_ln_b"][0]))
    vp.add("sglg", _pvec(inp["sgu_ln_g"][0]))
    vp.add("glng", _pvec(inp["gla_norm_g"][0]))
    vp.add("sglb", _pvec(inp["sgu_ln_b"][0]))
    vecs_common = vp
    percore = []
    for b in range(NB):
        d = {}
        d["xT"] = np.ascontiguousarray(inp["x"][b].T)
        d["cvec"] = _pvec(inp["c"][b])
        percore.append(d)
    shared["vecs"] = vp.build()
    return shared, percore, vp.off


class Builder:
    def __init__(self, cfg, vec_off, nvec):
        self.cfg = cfg
        self.voff = vec_off
        self.nvec = nvec
        self.nc = bass.Bass("TRN2", target_bir_lowering=False)
        self.S = Sched()
        self.st = contextlib.ExitStack()
        self.specs = weight_specs()
        self.wslot_i = 0
        self.dbg_done = set()
        self.ps_i = 0
        self.uid = 0

    def sb(self, name, shape, dt):
        return self.st.enter_context(self.nc.sbuf_tensor(name, list(shape), dt))

    def dram(self, name, shape, dt, kind):
        return self.nc.dram_tensor(name, list(shape), dt, kind=kind).ap()

    def pe(self, fn, r=(), w=()):
        return self.S.add("pe", fn, r, w)

    def act(self, fn, r=(), w=()):
        return self.S.add("act", fn, r, w)

    def dve(self, fn, r=(), w=()):
        return self.S.add("dve", fn, r, w)

    def pool(self, fn, r=(), w=()):
        return self.S.add("pool", fn, r, w)

    def dma(self, q, out, in_, r=(), w=()):
        return self.S.add(q, lambda e: e.dma_start(out=out, in_=in_), r, w, dma=True)

    def dbg(self, name, ap, keys, dt=F32):
        if not self.cfg.get("debug") or name in self.dbg_done:
            return
        self.dbg_done.add(name)
        d = self.dram("dbg_" + name, list(ap.shape), dt, "ExternalOutput")
        self.dma("pool", d, ap, r=list(keys), w=[("dbg", name)])

    def psum(self, i=None):
        if i is None:
            i = self.ps_i % 8
            self.ps_i += 1
        return self.PS[i], ("PS", i)

    def ring(self, name):
        bufs, st = self.rings[name]
        i = st[0] % len(bufs)
        st[0] += 1
        return bufs[i], (name, i)

    def mkring(self, name, n, shape, dt):
        self.rings[name] = ([self.sb(f"{name}{i}", shape, dt) for i in range(n)], [0])

    def wload(self, mat, t):
        sp = self.specs[mat]
        slot = self.wslot_i % NWSLOT
        self.wslot_i += 1
        buf = self.WR[slot]
        E = sp["E"]
        src = self.wb[mat][t]
        key = ("WR", slot)
        self.dma("sp", buf[:, 0:E], src, r=[("wb", mat, t // 16)], w=[key])
        view = buf[:, 0:E].rearrange("p (k f) -> p k f", f=sp["fw"])
        return view, key

    def cast_weights(self, mat):
        sp = self.specs[mat]
        nt, E = sp["nt"], sp["E"]
        a = 1
        while E // a > 2048 or E % a:
            a += 1
        src = self.wf[mat].rearrange("t p (a b) -> t p a b", a=a)
        dst = self.wb[mat].rearrange("t p (a b) -> t p a b", a=a)
        for g in range(0, nt, 16):
            n = min(16, nt - g)
            self.dma("pool", dst[g:g + n], src[g:g + n], w=[("wb", mat, g // 16)])

    def vec(self, name, c0=0, n=None):
        o, w = self.voff[name]
        if n is None:
            n = w - c0
        return self.VEC[:, o + c0:o + c0 + n]

    def build(self):
        nc, cfg = self.nc, self.cfg
        layers = cfg["layers"]
        ntb = cfg.get("ntb", NTB)
        self.ntb = ntb
        self.xT = self.dram("xT", [D, L_SEQ], F32, "ExternalInput")
        self.cvec = self.dram("cvec", [128, KC], F32, "ExternalInput")
        self.vecs = self.dram("vecs", [128, self.nvec], F32, "ExternalInput")
        self.outT = self.dram("outT", [D, L_SEQ], F32, "ExternalOutput")
        self.xs = [self.dram(f"xs{i}", [D, L_SEQ], F32, "Internal") for i in range(2)]
        self.sgu_wsT = self.dram("sgu_wsT", [128, 1024], F32, "ExternalInput")
        self.rot_cos = self.dram("rot_cos", [128, L_SEQ], F32, "ExternalInput")
        self.gla_wg = self.dram("gla_wg", [32, 1024], F32, "ExternalInput")
        self.gla_c = self.dram("gla_c", [128, 384], F32, "ExternalInput")
        self.rot_sin = self.dram("rot_sin", [128, L_SEQ], F32, "ExternalInput")
        self.ret_c = self.dram("ret_c", [128, 1536], F32, "ExternalInput")
        self.ident_d = self.dram("ident", [128, 128], F32, "ExternalInput")
        self.sgu_bsb = self.dram("sgu_bsb", [128, 1024], F32, "ExternalInput")
        self.wf, self.wb = {}, {}
        mats = []
        for i in layers:
            mats.append(f"ada{i}")
        for i in layers:
            if cfg.get("mixer", True):
                mats += {0: ["cmin", "cmout"], 1: ["glain", "glag", "glaout"], 2: ["retin", "retout"], 3: ["sguin", "sguout"]}[i]
            mats += [f"ffin{i}", f"ffout{i}"]
        for m in mats:
            sp = self.specs[m]
            self.wf[m] = self.dram(m, [sp["nt"], 128, sp["E"]], F32, "ExternalInput")
            self.wb[m] = self.dram(m + "_b", [sp["nt"], 128, sp["E"]], BF16, "Internal")
        self.rings = {}
        self.VEC = self.sb("VEC", [128, self.nvec], F32)
        self.CV = self.sb("CV", [128, KC], F32)
        self.CA = self.sb("CA", [128, KC], BF16)
        self.MOD = self.sb("MOD", [128, 96], F32)
        self.PV = self.sb("PV", [128, 96], F32)
        self.ONES = self.sb("ONES", [128, 128], BF16)
        self.IDENT = self.sb("IDENT", [128, 128], BF16)
        self.XB = self.sb("XB", [128, KC, TB], F32)
        self.YB = self.sb("YB", [128, KC, TB], F32)
        self.HB = self.sb("HB", [128, KC, TB], BF16)
        self.MX = self.sb("MX", [128, 32 * TB // 2], F32)
        self.UB = self.MX[:, :].bitcast(BF16).rearrange("p (c t) -> p c t", t=TB)
        self.WR = [self.sb(f"WR{i}", [128, WSLOT], BF16) for i in range(NWSLOT)]
        self.RSTD = self.sb("RSTD", [128, TB], F32)
        self.LS = self.sb("LS", [128, 8192], F32)
        self.CTAIL = self.LS[:, 0:KC * 30].rearrange("p (c t) -> p c t", t=30)
        self.MU = self.LS[:, 512:1024]
        self.ST = self.sb("ST", [128, 256], F32)
        self.FTAIL = self.sb("FTAIL", [128, FFC, 2], F32)
        self.mkring("sq", 2, [128, TB], BF16)
        self.mkring("t32", 2, [128, TB], F32)
        self.mkring("ab", 2, [128, TB + 2], F32)
        self.mkring("acc", 2, [128, TB], F32)
        self.mkring("so", 2, [128, TB], F32)
        self.PS = [self.st.enter_context(nc.psum_tensor(f"PS{i}", [128, TB], F32)) for i in range(8)]

        for m in mats:
            self.cast_weights(m)
        self.dma("sp", self.VEC[:], self.vecs, w=["VEC"])
        self.dma("sp", self.CV[:], self.cvec, w=["CV"])
        self.dve(lambda e: e.memset(self.ONES[:], 1.0), w=["ONES"])
        it, itk = self.ring("t32")
        self.dma("sp", it[:, 0:128], self.ident_d, w=[itk])
        self.act(lambda e: e.activation(out=self.IDENT[:], in_=it[:, 0:128], func=AF.Copy), r=[itk], w=["IDENT"])
        self.act(lambda e: e.activation(out=self.CA[:], in_=self.CV[:], func=AF.Silu), r=["CV"], w=["CA"])

        xsrc = self.xT
        for li, i in enumerate(layers):
            last = li == len(layers) - 1
            xdst = self.outT if last else self.xs[li % 2]
            self.ada(i)
            if cfg.get("mixer", True) and i == 3:
                self.sgu_setup()
            if cfg.get("mixer", True) and i == 1:
                self.gla_setup()
            if cfg.get("mixer", True) and i == 2:
                self.pool(lambda e: e.memset(self.LS[:], 0.0), w=[("STATE", j) for j in range(16)])
            self.dve(lambda e: e.memset(self.FTAIL[:], 0.0), w=[("FT", fc) for fc in range(FFC)])
            for tb in range(ntb):
                self.load_x(xsrc, tb)
                if cfg.get("mixer", True):
                    self.prenorm(0)
                    self.mixer(i, tb)
                    self.postnorm(1, None, tb)
                self.prenorm(3)
                self.ffn(i, tb)
                self.postnorm(4, xdst, tb)
            xsrc = xdst
        self.S.add("sp", lambda e: e.nop(), [("xd", id(xdst), tb, c) for tb in range(ntb) for c in range(KC)]
                   + [("dbg", n) for n in self.dbg_done], [])
        self.S.emit(nc)
        return nc

    def ada(self, i):
        ps, pk = self.psum()
        mat = f"ada{i}"
        for g in range(48):
            wv, wk = self.wload(mat, g)
            for j in range(2):
                col = g * 2 + j
                for kc in range(KC):
                    self.pe(lambda e, wv=wv, j=j, kc=kc, col=col: e.matmul(
                        ps[:, col:col + 1], wv[:, kc, j * 128:(j + 1) * 128], self.CA[:, kc:kc + 1],
                        start=(kc == 0), stop=(kc == KC - 1)),
                        r=[wk, "CA"], w=[pk])
        self.dve(lambda e: e.tensor_tensor(out=self.MOD[:], in0=ps[:, 0:96], in1=self.vec(f"adab{i}"), op=ALU.add),
                 r=[pk, "VEC"], w=["MOD"])
        M, P = self.MOD, self.PV

        def sl(t, j):
            return t[:, j * 16:(j + 1) * 16]
        self.dve(lambda e: e.scalar_tensor_tensor(out=sl(P, 0), in0=sl(M, 1), scalar=1.0, in1=self.vec(f"ng{i}_0"),
                                                  op0=ALU.add, op1=ALU.mult), r=["MOD", "VEC"], w=["PV0"])
        self.dve(lambda e: e.tensor_copy(out=sl(P, 1), in_=sl(M, 0)), r=["MOD"], w=["PV1"])
        self.dve(lambda e: e.tensor_tensor(out=sl(P, 2), in0=sl(M, 2), in1=self.vec(f"ng{i}_1"), op=ALU.mult),
                 r=["MOD", "VEC"], w=["PV2"])
        self.dve(lambda e: e.scalar_tensor_tensor(out=sl(P, 3), in0=sl(M, 4), scalar=1.0, in1=self.vec(f"ng{i}_2"),
                                                  op0=ALU.add, op1=ALU.mult), r=["MOD", "VEC"], w=["PV3"])
        self.dve(lambda e: e.tensor_copy(out=sl(P, 4), in_=sl(M, 3)), r=["MOD"], w=["PV4"])
        self.dve(lambda e: e.tensor_tensor(out=sl(P, 5), in0=sl(M, 5), in1=self.vec(f"ng{i}_3"), op=ALU.mult),
                 r=["MOD", "VEC"], w=["PV5"])

    def load_x(self, xsrc, tb):
        src = xsrc.rearrange("(c p) t -> p c t", p=128)
        for q in range(4):
            self.dma("sp", self.XB[:, q * 4:(q + 1) * 4, :], src[:, q * 4:(q + 1) * 4, tb * TB:(tb + 1) * TB],
                     r=[("xd", id(xsrc), tb, c) for c in range(q * 4, q * 4 + 4)],
                     w=[("XB", c) for c in range(q * 4, q * 4 + 4)])

    def rstd_from(self, srcbuf, skey):
        ps, pk = self.psum()
        for c in range(KC):
            sq, sk = self.ring("sq")
            self.act(lambda e, sq=sq, c=c: e.activation(out=sq[:], in_=srcbuf[:, c, :], func=AF.Square),
                     r=[(skey, c)], w=[sk])
            self.pe(lambda e, sq=sq, c=c: e.matmul(ps[:], self.ONES[:], sq[:], start=(c == 0), stop=(c == KC - 1)),
                    r=[sk, "ONES"], w=[pk])
        self.act(lambda e: e.activation(out=self.RSTD[:], in_=ps[:], func=AF.Sqrt, bias=self.EPSC[:, 0:1], scale=1.0 / D),
                 r=[pk, "EPSC"], w=["RSTD"])
        self.dve(lambda e: e.reciprocal(out=self.RSTD[:], in_=self.RSTD[:]), r=["RSTD"], w=["RSTD"])

    def prenorm(self, j):
        self.rstd_from(self.XB, "XB")
        A = self.PV[:, j * 16:(j + 1) * 16]
        Bv = self.PV[:, (j + 1) * 16:(j + 2) * 16]
        for c in range(KC):
            t, tk = self.ring("t32")
            self.dve(lambda e, t=t, c=c: e.tensor_tensor(out=t[:], in0=self.XB[:, c, :], in1=self.RSTD[:], op=ALU.mult),
                     r=[("XB", c), "RSTD"], w=[tk])
            self.act(lambda e, t=t, c=c: e.activation(out=self.HB[:, c, :], in_=t[:], func=AF.Identity,
                                                      scale=A[:, c:c + 1], bias=Bv[:, c:c + 1]),
                     r=[tk, f"PV{j}", f"PV{j + 1}"], w=[("HB", c)])

    def postnorm(self, j, xdst, tb):
        self.rstd_from(self.YB, "YB")
        G = self.PV[:, (j + 1) * 16:(j + 2) * 16]
        gk = f"PV{j + 1}"
        for c in range(KC):
            t, tk = self.ring("t32")
            self.dve(lambda e, t=t, c=c: e.tensor_tensor(out=t[:], in0=self.YB[:, c, :], in1=self.RSTD[:], op=ALU.mult),
                     r=[("YB", c), "RSTD"], w=[tk])
            if xdst is None:
                self.dve(lambda e, t=t, c=c: e.scalar_tensor_tensor(out=self.XB[:, c, :], in0=t[:], scalar=G[:, c:c + 1],
                                                                    in1=self.XB[:, c, :], op0=ALU.mult, op1=ALU.add),
                         r=[tk, gk, ("XB", c)], w=[("XB", c)])
            else:
                so, sk = self.ring("so")
                self.dve(lambda e, t=t, c=c, so=so: e.scalar_tensor_tensor(out=so[:], in0=t[:], scalar=G[:, c:c + 1],
                                                                           in1=self.XB[:, c, :], op0=ALU.mult, op1=ALU.add),
                         r=[tk, gk, ("XB", c)], w=[sk])
                self.dma("pool", xdst[c * 128:(c + 1) * 128, tb * TB:(tb + 1) * TB], so[:],
                         r=[sk], w=[("xd", id(xdst), tb, c)])

    def ffn(self, i, tb):
        mi, mo = f"ffin{i}", f"ffout{i}"
        hk = [("HB", c) for c in range(KC)]
        for half in range(2):
          for p in range(11 * half, 11 * half + 11):
                wa, wak = self.wload(mi, p)
                wb_, wbk = self.wload(mi, 22 + p)
                chunks = []
                for j in range(2):
                    fc = 2 * p + j
                    pa, pak = self.psum()
                    pb, pbk = self.psum()
                    for kc in range(KC):
                        self.pe(lambda e, pa=pa, j=j, kc=kc, wa=wa: e.matmul(pa[:], wa[:, kc, j * 128:(j + 1) * 128], self.HB[:, kc, :],
                                                                            start=(kc == 0), stop=(kc == KC - 1)),
                                r=[wak] + hk, w=[pak])
                    for kc in range(KC):
                        self.pe(lambda e, pb=pb, j=j, kc=kc, wb_=wb_: e.matmul(pb[:], wb_[:, kc, j * 128:(j + 1) * 128], self.HB[:, kc, :],
                                                                               start=(kc == 0), stop=(kc == KC - 1)),
                                r=[wbk] + hk, w=[pbk])
                    chunks.append((fc, pa, pak, pb, pbk))
                st = []
                for (fc, pa, pak, pb, pbk) in chunks:
                    ab, abk = self.ring("ab")
                    acc, ack = self.ring("acc")
                    self.pool(lambda e, ab=ab, fc=fc: e.tensor_copy(out=ab[:, 0:2], in_=self.FTAIL[:, fc, :]),
                              r=[("FT", fc)], w=[(abk, "t")])
                    self.act(lambda e, ab=ab, pa=pa: e.activation(out=ab[:, 2:TB + 2], in_=pa[:], func=AF.Copy),
                             r=[pak], w=[abk])
                    self.pool(lambda e, ab=ab, fc=fc: e.tensor_copy(out=self.FTAIL[:, fc, :], in_=ab[:, TB:TB + 2]),
                              r=[abk], w=[("FT", fc)])
                    st.append((fc, ab, abk, acc, ack, pb, pbk))
                w0, w1, w2, cb = (self.vec(f"fcw{i}_0"), self.vec(f"fcw{i}_1"), self.vec(f"fcw{i}_2"), self.vec(f"fcb{i}"))
                for (fc, ab, abk, acc, ack, pb, pbk) in st:
                    self.dve(lambda e, ab=ab, acc=acc, fc=fc: e.tensor_scalar(out=acc[:], in0=ab[:, 2:TB + 2], scalar1=w2[:, fc:fc + 1],
                                                                             scalar2=cb[:, fc:fc + 1], op0=ALU.mult, op1=ALU.add),
                             r=[abk, "VEC"], w=[ack])
                for (fc, ab, abk, acc, ack, pb, pbk) in st:
                    self.dve(lambda e, ab=ab, acc=acc, fc=fc: e.scalar_tensor_tensor(out=acc[:], in0=ab[:, 1:TB + 1], scalar=w1[:, fc:fc + 1],
                                                                                    in1=acc[:], op0=ALU.mult, op1=ALU.add),
                             r=[abk, (abk, "t"), ack, "VEC"], w=[ack])
                for (fc, ab, abk, acc, ack, pb, pbk) in st:
                    self.dve(lambda e, ab=ab, acc=acc, fc=fc: e.scalar_tensor_tensor(out=acc[:], in0=ab[:, 0:TB], scalar=w0[:, fc:fc + 1],
                                                                                    in1=acc[:], op0=ALU.mult, op1=ALU.add),
                             r=[abk, (abk, "t"), ack, "VEC"], w=[ack])
                for (fc, ab, abk, acc, ack, pb, pbk) in st:
                    self.act(lambda e, acc=acc: e.activation(out=acc[:], in_=acc[:], func=AF.Silu), r=[ack], w=[ack])
                for (fc, ab, abk, acc, ack, pb, pbk) in st:
                    self.dve(lambda e, acc=acc, pb=pb, fc=fc, half=half: e.tensor_tensor(out=self.UB[:, fc - 22 * half, :], in0=acc[:], in1=pb[:], op=ALU.mult),
                             r=[ack, pbk], w=[("UB", fc - 22 * half)])
          uk = [("UB", c) for c in range(22)]
          for oc in range(KC):
              ps, pk = self.psum()
              wv, wk = self.wload(mo, oc * 2 + half)
              for kc in range(22):
                  self.pe(lambda e, ps=ps, wv=wv, kc=kc: e.matmul(ps[:], wv[:, kc, :], self.UB[:, kc, :],
                                                                 start=(kc == 0), stop=(kc == 21)), r=[wk] + uk, w=[pk])
              if half == 0:
                  self.act(lambda e, ps=ps, oc=oc: e.activation(out=self.YB[:, oc, :], in_=ps[:], func=AF.Copy),
                           r=[pk], w=[("YB", oc)])
              else:
                  self.dve(lambda e, ps=ps, oc=oc: e.tensor_tensor(out=self.YB[:, oc, :], in0=self.YB[:, oc, :], in1=ps[:], op=ALU.add),
                           r=[pk, ("YB", oc)], w=[("YB", oc)])

    def mixer(self, i, tb):
        if i == 0:
            self.conv_mixer(tb)
        elif i == 3:
            self.sgu_mixer(tb)
        elif i == 2:
            self.ret_mixer(tb)
        elif i == 1:
            self.gla_mixer(tb)
        else:
            raise NotImplementedError

    def ysc(self, off, n, dt=None):
        v = self.YB[:, :, :].rearrange("p c t -> p (c t)")[:, off:off + n]
        if dt is not None:
            v = v.bitcast(dt)
        return v

    def out_proj32(self, mat, src, skeys):
        for g in range(8):
            pss = [self.psum() for _ in range(2)]
            for kt in range(2):
                wv, wk = self.wload(mat, g * 2 + kt)
                for jj in range(2):
                    ps, pk = pss[jj]
                    for kc in range(KC):
                        self.pe(lambda e, ps=ps, wv=wv, kc=kc, jj=jj, kt=kt: e.matmul(
                            ps[:], wv[:, kc, jj * 128:(jj + 1) * 128], src[:, kt * 16 + kc, :],
                            start=(kt == 0 and kc == 0), stop=(kt == 1 and kc == KC - 1)), r=[wk] + skeys[kt * 16:(kt + 1) * 16], w=[pk])
            for jj in range(2):
                ps, pk = pss[jj]
                oc = 2 * g + jj
                self.act(lambda e, ps=ps, oc=oc: e.activation(out=self.YB[:, oc, :], in_=ps[:], func=AF.Copy), r=[pk], w=[("YB", oc)])

    def ret_mixer(self, tb):
        H = 8
        g64 = [float(np.exp(np.float32(64.0) * np.log1p(-np.exp2(np.float32(-5.0 - h))))) for h in range(H)]
        STATE = self.LS[:, :].rearrange("p (j v) -> p j v", v=512)
        OALL = self.UB
        COS = self.ysc(0, 512)
        SIN = self.ysc(512, 512)
        QI = self.ysc(1024, 512, BF16).rearrange("p (c t) -> p c t", t=TB)
        KI = self.ysc(1536, 512, BF16).rearrange("p (c t) -> p c t", t=TB)
        KT = self.ysc(2048, 512, BF16).rearrange("p (g d) -> p g d", d=256)
        VT = self.ysc(2560, 1024, BF16).rearrange("p (g v) -> p g v", v=512)
        SG = self.ysc(3584, 1024, BF16).rearrange("p (c t) -> p c t", t=TB)
        SB = self.ysc(4608, 512, BF16).rearrange("p (c v) -> p c v", v=512)
        RC = self.ysc(5120, 1536)
        GQ = RC[:, 0:512].rearrange("p (h t) -> p h t", t=64)
        GK = RC[:, 512:1024].rearrange("p (h t) -> p h t", t=64)
        MT = RC[:, 1024:1536].rearrange("p (h t) -> p h t", t=64)
        SCT = self.ysc(6656, 32, BF16)
        hk = [("HB", c) for c in range(KC)]
        self.dma("sp", COS, self.rot_cos[:, tb * TB:(tb + 1) * TB], w=["COS"])
        self.dma("sp", SIN, self.rot_sin[:, tb * TB:(tb + 1) * TB], w=["SIN"])
        self.dma("sp", RC, self.ret_c, w=["RC"])

        def proj_fm(wv, wk, j, bank):
            ps, pk = self.psum(bank)
            for kc in range(KC):
                self.pe(lambda e, ps=ps, wv=wv, kc=kc, j=j: e.matmul(ps[:], wv[:, kc, j * 128:(j + 1) * 128], self.HB[:, kc, :],
                                                                    start=(kc == 0), stop=(kc == KC - 1)), r=[wk] + hk, w=[pk])
            return ps, pk

        def rotary(p1, k1, p2, k2, G, h, DST, dkey):
            a, ak = self.ring("t32")
            b, bk = self.ring("t32")
            c, ck = self.ring("acc")
            d, dk_ = self.ring("acc")
            gb = G[:, h, :].unsqueeze(1).to_broadcast([128, 8, 64])
            v3 = lambda t: t[:].rearrange("p (n t) -> p n t", t=64)
            self.dve(lambda e: e.tensor_tensor(out=a[:], in0=p1[:], in1=COS, op=ALU.mult), r=[k1, "COS"], w=[ak])
            self.dve(lambda e: e.tensor_tensor(out=b[:], in0=p2[:], in1=SIN, op=ALU.mult), r=[k2, "SIN"], w=[bk])
            self.dve(lambda e: e.tensor_tensor(out=c[:], in0=p1[:], in1=SIN, op=ALU.mult), r=[k1, "SIN"], w=[ck])
            self.dve(lambda e: e.tensor_tensor(out=d[:], in0=p2[:], in1=COS, op=ALU.mult), r=[k2, "COS"], w=[dk_])
            self.dve(lambda e: e.tensor_tensor(out=a[:], in0=a[:], in1=b[:], op=ALU.subtract), r=[ak, bk], w=[ak])
            self.dve(lambda e: e.tensor_tensor(out=c[:], in0=c[:], in1=d[:], op=ALU.add), r=[ck, dk_], w=[ck])
            self.dve(lambda e: e.tensor_tensor(out=DST[:, 0, :].rearrange("p (n t) -> p n t", t=64), in0=v3(a), in1=gb, op=ALU.mult),
                     r=[ak, "RC"], w=[(dkey, 0)])
            self.dve(lambda e: e.tensor_tensor(out=DST[:, 1, :].rearrange("p (n t) -> p n t", t=64), in0=v3(c), in1=gb, op=ALU.mult),
                     r=[ck, "RC"], w=[(dkey, 1)])

        for h in range(H):
            wq, wqk = self.wload("retin", h)
            p1, k1 = proj_fm(wq, wqk, 0, 4)
            p2, k2 = proj_fm(wq, wqk, 1, 5)
            rotary(p1, k1, p2, k2, GQ, h, QI, "QI")
            wkk, wkkk = self.wload("retin", 8 + h)
            p1, k1 = proj_fm(wkk, wkkk, 0, 6)
            p2, k2 = proj_fm(wkk, wkkk, 1, 7)
            rotary(p1, k1, p2, k2, GK, h, KI, "KI")
            pt, ptk = self.psum(4)
            ptb = pt[:].bitcast(BF16)
            for tt in range(4):
                for dc in range(2):
                    self.pe(lambda e, tt=tt, dc=dc: e.transpose(ptb[:, (tt * 2 + dc) * 128:(tt * 2 + dc + 1) * 128],
                                                                KI[:, dc, tt * 128:(tt + 1) * 128], self.IDENT[:]),
                            r=[("KI", dc), "IDENT"], w=[ptk])
            self.act(lambda e: e.activation(out=KT.rearrange("p g d -> p (g d)"), in_=ptb, func=AF.Copy), r=[ptk], w=["KT"])
            for half in range(2):
                wv, wvk = self.wload("retin", 16 + 2 * h + half)
                for tt in range(4):
                    ps, pk = self.psum(5 + (tt % 2))
                    for kc in range(KC):
                        self.pe(lambda e, ps=ps, wv=wv, kc=kc, tt=tt: e.matmul(ps[:, 0:256], self.HB[:, kc, tt * 128:(tt + 1) * 128], wv[:, kc, :],
                                                                               start=(kc == 0), stop=(kc == KC - 1)), r=[wvk] + hk, w=[pk])
                    self.act(lambda e, ps=ps, tt=tt, half=half: e.activation(out=VT[:, tt, half * 256:(half + 1) * 256], in_=ps[:, 0:256], func=AF.Copy),
                             r=[pk], w=[("VT", tt)])
            for half in range(2):
                wg, wgk = self.wload("retin", 32 + 2 * h + half)
                for j in range(2):
                    ps, pk = proj_fm(wg, wgk, j, 6 + j)
                    vc = half * 2 + j
                    self.act(lambda e, ps=ps, vc=vc: e.activation(out=SG[:, vc, :], in_=ps[:], func=AF.Silu), r=[pk], w=[("SG", vc)])
            for dc in range(2):
                self.act(lambda e, dc=dc, h=h: e.activation(out=SB[:, dc, :], in_=STATE[:, h * 2 + dc, :], func=AF.Copy),
                         r=[("STATE", h * 2 + dc)], w=[("SB", dc)])
            po = [self.psum(vc) for vc in range(4)]
            for n in range(8):
                tt, r0 = n // 2, (n % 2) * 64
                pss, psk = self.psum(4)
                for dc in range(2):
                    self.pe(lambda e, dc=dc, n=n, r0=r0, pss=pss: e.matmul(pss[r0:r0 + 64, 0:64], KI[:, dc, n * 64:(n + 1) * 64], QI[:, dc, n * 64:(n + 1) * 64],
                                                                           start=(dc == 0), stop=(dc == 1)), r=[("KI", dc), ("QI", dc)], w=[psk])
                self.dve(lambda e, r0=r0, h=h, pss=pss: e.tensor_tensor(out=SCT[r0:r0 + 64, :], in0=pss[r0:r0 + 64, 0:64], in1=MT[r0:r0 + 64, h, :], op=ALU.mult),
                         r=[psk, "RC"], w=["SCT"])
                for vc in range(4):
                    ps, pk = po[vc]
                    self.pe(lambda e, ps=ps, vc=vc, n=n, tt=tt, r0=r0: e.matmul(ps[:, n * 64:(n + 1) * 64], VT[r0:r0 + 64, tt, vc * 128:(vc + 1) * 128], SCT[r0:r0 + 64, :],
                                                                                start=True, stop=False), r=[("VT", tt), "SCT"], w=[pk])
                    for dc in range(2):
                        self.pe(lambda e, ps=ps, vc=vc, n=n, dc=dc: e.matmul(ps[:, n * 64:(n + 1) * 64], SB[:, dc, vc * 128:(vc + 1) * 128], QI[:, dc, n * 64:(n + 1) * 64],
                                                                             start=False, stop=(dc == 1)), r=[("SB", dc), ("QI", dc)], w=[pk])
                for dc in range(2):
                    ps, pk = self.psum(5 + dc)
                    self.pe(lambda e, ps=ps, dc=dc, tt=tt, r0=r0: e.matmul(ps[:], KT[r0:r0 + 64, tt, dc * 128:(dc + 1) * 128], VT[r0:r0 + 64, tt, :],
                                                                           start=True, stop=True), r=["KT", ("VT", tt)], w=[pk])
                    sk = ("STATE", h * 2 + dc)
                    self.dve(lambda e, ps=ps, dc=dc, h=h: e.scalar_tensor_tensor(out=STATE[:, h * 2 + dc, :], in0=STATE[:, h * 2 + dc, :], scalar=g64[h], in1=ps[:],
                                                                                op0=ALU.mult, op1=ALU.add), r=[pk, sk], w=[sk])
                    if n < 7:
                        self.act(lambda e, dc=dc, h=h: e.activation(out=SB[:, dc, :], in_=STATE[:, h * 2 + dc, :], func=AF.Copy), r=[sk], w=[("SB", dc)])
            pn, pnk = self.psum(7)
            for vc in range(4):
                ps, pk = po[vc]
                sq, sqk = self.ring("sq")
                self.act(lambda e, ps=ps, sq=sq: e.activation(out=sq[:], in_=ps[:], func=AF.Square), r=[pk], w=[sqk])
                self.pe(lambda e, sq=sq, vc=vc: e.matmul(pn[:], self.ONES[:], sq[:], start=(vc == 0), stop=(vc == 3)), r=[sqk, "ONES"], w=[pnk])
            self.act(lambda e: e.activation(out=self.RSTD[:], in_=pn[:], func=AF.Sqrt, bias=self.EPSC[:, 0:1], scale=1.0 / 512), r=[pnk, "EPSC"], w=["RSTD"])
            self.dve(lambda e: e.reciprocal(out=self.RSTD[:], in_=self.RSTD[:]), r=["RSTD"], w=["RSTD"])
            for vc in range(4):
                ps, pk = po[vc]
                t, tk = self.ring("t32")
                self.dve(lambda e, ps=ps, t=t: e.tensor_tensor(out=t[:], in0=ps[:], in1=self.RSTD[:], op=ALU.mult), r=[pk, "RSTD"], w=[tk])
                self.dve(lambda e, t=t, vc=vc, h=h: e.tensor_tensor(out=OALL[:, h * 4 + vc, :], in0=t[:], in1=SG[:, vc, :], op=ALU.mult),
                         r=[tk, ("SG", vc)], w=[("UB", h * 4 + vc)])
        self.out_proj32("retout", OALL, [("UB", c) for c in range(32)])

    def gla_setup(self):
        WG = self.LS[:, 4096:4608].bitcast(BF16)
        self.pool(lambda e: e.memset(self.LS[:, 0:4096], 0.0), w=[("STATE", j) for j in range(8)])
        tmp = self.YB[:, 14:16, :].rearrange("p c t -> p (c t)")
        yk = [("YB", 14), ("YB", 15)]
        self.dma("sp", tmp[0:32, :], self.gla_wg, w=yk)
        self.act(lambda e: e.activation(out=WG[0:32, :], in_=tmp[0:32, :], func=AF.Copy), r=yk, w=["WG"])

    def gla_mixer(self, tb):
        H = 4
        STATE = self.LS[:, 0:4096].rearrange("p (j v) -> p j v", v=512)
        WG = self.LS[:, 4096:4608].bitcast(BF16)
        OALL = self.UB
        mx = self.MX
        VT = mx[:, 4096:5120].bitcast(BF16).rearrange("p (g v) -> p g v", v=512)
        SG = mx[:, 5120:6144].bitcast(BF16).rearrange("p (c t) -> p c t", t=TB)
        KS = mx[:, 6144:6656].bitcast(BF16).rearrange("p (g d) -> p g d", d=256)
        SB = mx[:, 6656:7168].bitcast(BF16).rearrange("p (c v) -> p c v", v=512)
        GLOW = self.ysc(0, 256, BF16)
        LA = self.ysc(256, 1024).rearrange("p (g f) -> p g f", f=256)
        EP = self.ysc(1280, 1024).rearrange("p (c t) -> p c t", t=TB)
        EN = self.ysc(2304, 1024).rearrange("p (c t) -> p c t", t=TB)
        E2 = self.ysc(3328, 1024).rearrange("p (g f) -> p g f", f=256)
        QF = self.ysc(4352, 512, BF16).rearrange("p (c t) -> p c t", t=TB)
        QB = self.ysc(4864, 512, BF16).rearrange("p (c t) -> p c t", t=TB)
        KF = self.ysc(5376, 512, BF16).rearrange("p (c t) -> p c t", t=TB)
        KB = self.ysc(5888, 512, BF16).rearrange("p (c t) -> p c t", t=TB)
        GC = self.ysc(6400, 384)
        TRI, TRI2, UM, LM = GC[:, 0:128], GC[:, 128:256], GC[:, 256:320], GC[:, 320:384]
        SCT = self.ysc(6784, 32, BF16)
        T1 = self.ysc(6816, 64)
        T2 = self.ysc(6880, 64)
        hk = [("HB", c) for c in range(KC)]
        self.dma("sp", GC, self.gla_c, w=["GC"])
        self.pool(lambda e: e.memset(GLOW[0:32, :], 1.0), w=["GLOW"])
        wgl, wglk = self.wload("glag", 0)
        pg, pgk = self.psum(7)
        for kc in range(KC):
            self.pe(lambda e, kc=kc: e.matmul(pg[0:16, :], wgl[:, kc, :], self.HB[:, kc, :], start=(kc == 0), stop=(kc == KC - 1)),
                    r=[wglk] + hk, w=[pgk])
        self.act(lambda e: e.activation(out=GLOW[0:16, :], in_=pg[0:16, :], func=AF.Copy), r=[pgk, "GLOW"], w=["GLOW"])

        def proj_fm(wv, wk, j, bank):
            ps, pk = self.psum(bank)
            for kc in range(KC):
                self.pe(lambda e, ps=ps, wv=wv, kc=kc, j=j: e.matmul(ps[:], wv[:, kc, j * 128:(j + 1) * 128], self.HB[:, kc, :],
                                                                    start=(kc == 0), stop=(kc == KC - 1)), r=[wk] + hk, w=[pk])
            return ps, pk

        for h in range(H):
            for tt in range(4):
                ps, pk = self.psum(5 + tt % 2)
                self.pe(lambda e, ps=ps, tt=tt, h=h: e.matmul(ps[:, 0:256], GLOW[0:17, tt * 128:(tt + 1) * 128], WG[0:17, h * 256:(h + 1) * 256],
                                                              start=True, stop=True), r=["GLOW", "WG"], w=[pk])
                self.act(lambda e, ps=ps, tt=tt: e.activation(out=LA[:, tt, :], in_=ps[:, 0:256], func=AF.Exp, scale=-1.0), r=[pk], w=[("LA", tt)])
                self.act(lambda e, tt=tt: e.activation(out=LA[:, tt, :], in_=LA[:, tt, :], func=AF.Ln, bias=1.0), r=[("LA", tt)], w=[("LA", tt)])
            for dc in range(2):
                ps, pk = self.psum(5 + dc)
                for tt in range(4):
                    self.pe(lambda e, ps=ps, tt=tt, dc=dc: e.matmul(ps[:, tt * 128:(tt + 1) * 128], LA[:, tt, dc * 128:(dc + 1) * 128], TRI,
                                                                    start=True, stop=True), r=[("LA", tt), "GC"], w=[pk])
                self.act(lambda e, ps=ps, dc=dc: e.activation(out=EP[:, dc, :], in_=ps[:], func=AF.Exp), r=[pk], w=[("EP", dc)])
                self.act(lambda e, ps=ps, dc=dc: e.activation(out=EN[:, dc, :], in_=ps[:], func=AF.Exp, scale=-1.0), r=[pk], w=[("EN", dc)])
            for tt in range(4):
                ps, pk = self.psum(7)
                self.pe(lambda e, ps=ps, tt=tt: e.matmul(ps[:, 0:256], TRI2, LA[:, tt, :], start=True, stop=True), r=[("LA", tt), "GC"], w=[pk])
                self.act(lambda e, ps=ps, tt=tt: e.activation(out=E2[:, tt, :], in_=ps[:, 0:256], func=AF.Exp), r=[pk], w=[("E2", tt)])
            wq, wqk = self.wload("glain", h)
            for dc in range(2):
                ps, pk = proj_fm(wq, wqk, dc, 5 + dc)
                self.dve(lambda e, ps=ps, dc=dc: e.scalar_tensor_tensor(out=QF[:, dc, :], in0=ps[:], scalar=0.0625, in1=EP[:, dc, :], op0=ALU.mult, op1=ALU.mult),
                         r=[pk, ("EP", dc)], w=[("QF", dc)])
                self.dve(lambda e, ps=ps, dc=dc: e.scalar_tensor_tensor(out=QB[:, dc, :], in0=ps[:], scalar=0.0625, in1=EN[:, dc, :], op0=ALU.mult, op1=ALU.mult),
                         r=[pk, ("EN", dc)], w=[("QB", dc)])
            wkk, wkkk = self.wload("glain", 4 + h)
            for dc in range(2):
                ps, pk = proj_fm(wkk, wkkk, dc, 5 + dc)
                self.dve(lambda e, ps=ps, dc=dc: e.tensor_tensor(out=KF[:, dc, :], in0=ps[:], in1=EN[:, dc, :], op=ALU.mult), r=[pk, ("EN", dc)], w=[("KF", dc)])
                self.dve(lambda e, ps=ps, dc=dc: e.tensor_tensor(out=KB[:, dc, :], in0=ps[:], in1=EP[:, dc, :], op=ALU.mult), r=[pk, ("EP", dc)], w=[("KB", dc)])
            for tt in range(4):
                ps, pk = self.psum(5 + tt % 2)
                for kc in range(KC):
                    self.pe(lambda e, ps=ps, kc=kc, tt=tt, wkk=wkk: e.matmul(ps[:, 0:256], self.HB[:, kc, tt * 128:(tt + 1) * 128], wkk[:, kc, :],
                                                                    start=(kc == 0), stop=(kc == KC - 1)), r=[wkkk] + hk, w=[pk])
                self.dve(lambda e, ps=ps, tt=tt: e.tensor_tensor(out=KS[:, tt, :], in0=ps[:, 0:256], in1=E2[:, tt, :], op=ALU.mult), r=[pk, ("E2", tt)], w=[("KS", tt)])
            for half in range(2):
                wv, wvk = self.wload("glain", 8 + 2 * h + half)
                for tt in range(4):
                    ps, pk = self.psum(5 + tt % 2)
                    for kc in range(KC):
                        self.pe(lambda e, ps=ps, wv=wv, kc=kc, tt=tt: e.matmul(ps[:, 0:256], self.HB[:, kc, tt * 128:(tt + 1) * 128], wv[:, kc, :],
                                                                               start=(kc == 0), stop=(kc == KC - 1)), r=[wvk] + hk, w=[pk])
                    self.act(lambda e, ps=ps, tt=tt, half=half: e.activation(out=VT[:, tt, half * 256:(half + 1) * 256], in_=ps[:, 0:256], func=AF.Copy),
                             r=[pk], w=[("VT", tt)])
            for half in range(2):
                wr, wrk = self.wload("glain", 16 + 2 * h + half)
                for j in range(2):
                    ps, pk = proj_fm(wr, wrk, j, 5 + j)
                    vc = half * 2 + j
                    self.act(lambda e, ps=ps, vc=vc: e.activation(out=SG[:, vc, :], in_=ps[:], func=AF.Silu), r=[pk], w=[("SG", vc)])
            for dc in range(2):
                self.act(lambda e, dc=dc, h=h: e.activation(out=SB[:, dc, :], in_=STATE[:, h * 2 + dc, :], func=AF.Copy),
                         r=[("STATE", h * 2 + dc)], w=[("SB", dc)])
            if h == 0 and tb == 0:
                self.dbg("GLOW", GLOW[0:32, :], ["GLOW"], BF16)
                self.dbg("LA", LA, [("LA", t_) for t_ in range(4)])
                self.dbg("EP", EP, [("EP", 0), ("EP", 1)])
                self.dbg("EN", EN, [("EN", 0), ("EN", 1)])
                self.dbg("E2", E2, [("E2", t_) for t_ in range(4)])
                self.dbg("QF", QF, [("QF", 0), ("QF", 1)], BF16)
                self.dbg("KB", KB, [("KB", 0), ("KB", 1)], BF16)
                self.dbg("KS", KS, [("KS", t_) for t_ in range(4)], BF16)
                self.dbg("VT", VT, [("VT", t_) for t_ in range(4)], BF16)
                self.dbg("SG", SG, [("SG", t_) for t_ in range(4)], BF16)
            po = [self.psum(vc) for vc in range(4)]
            for n in range(8):
                tt, r0 = n // 2, (n % 2) * 64
                pss, psk = self.psum(4)
                cs = slice(n * 64, (n + 1) * 64)
                for dc in range(2):
                    self.pe(lambda e, dc=dc, cs=cs, r0=r0, pss=pss: e.matmul(pss[r0:r0 + 64, 0:64], KF[:, dc, cs], QF[:, dc, cs],
                                                                             start=(dc == 0), stop=(dc == 1)), r=[("KF", dc), ("QF", dc)], w=[psk])
                for dc in range(2):
                    self.pe(lambda e, dc=dc, cs=cs, r0=r0, pss=pss: e.matmul(pss[r0:r0 + 64, 64:128], KB[:, dc, cs], QB[:, dc, cs],
                                                                             start=(dc == 0), stop=(dc == 1)), r=[("KB", dc), ("QB", dc)], w=[psk])
                self.dve(lambda e, r0=r0, pss=pss: e.tensor_tensor(out=T1[r0:r0 + 64, :], in0=pss[r0:r0 + 64, 0:64], in1=UM[r0:r0 + 64, :], op=ALU.mult),
                         r=[psk, "GC"], w=["T1"])
                self.dve(lambda e, r0=r0, pss=pss: e.tensor_tensor(out=T2[r0:r0 + 64, :], in0=pss[r0:r0 + 64, 64:128], in1=LM[r0:r0 + 64, :], op=ALU.mult),
                         r=[psk, "GC"], w=["T2"])
                self.dve(lambda e, r0=r0: e.tensor_tensor(out=SCT[r0:r0 + 64, :], in0=T1[r0:r0 + 64, :], in1=T2[r0:r0 + 64, :], op=ALU.add),
                         r=["T1", "T2"], w=["SCT"])
                for vc in range(4):
                    ps, pk = po[vc]
                    self.pe(lambda e, ps=ps, vc=vc, cs=cs, tt=tt, r0=r0: e.matmul(ps[:, cs], VT[r0:r0 + 64, tt, vc * 128:(vc + 1) * 128], SCT[r0:r0 + 64, :],
                                                                                  start=True, stop=False), r=[("VT", tt), "SCT"], w=[pk])
                    for dc in range(2):
                        self.pe(lambda e, ps=ps, vc=vc, cs=cs, dc=dc: e.matmul(ps[:, cs], SB[:, dc, vc * 128:(vc + 1) * 128], QF[:, dc, cs],
                                                                               start=False, stop=(dc == 1)), r=[("SB", dc), ("QF", dc)], w=[pk])
                for dc in range(2):
                    ps, pk = self.psum(5 + dc)
                    self.pe(lambda e, ps=ps, dc=dc, tt=tt, r0=r0: e.matmul(ps[:], KS[r0:r0 + 64, tt, dc * 128:(dc + 1) * 128], VT[r0:r0 + 64, tt, :],
                                                                           start=True, stop=True), r=[("KS", tt), ("VT", tt)], w=[pk])
                    sk = ("STATE", h * 2 + dc)
                    self.dve(lambda e, ps=ps, dc=dc, h=h, n=n: e.scalar_tensor_tensor(
                        out=STATE[:, h * 2 + dc, :], in0=STATE[:, h * 2 + dc, :], scalar=EP[:, dc, n * 64 + 63:n * 64 + 64], in1=ps[:],
                        op0=ALU.mult, op1=ALU.add), r=[pk, sk, ("EP", dc)], w=[sk])
                    if n < 7:
                        self.act(lambda e, dc=dc, h=h: e.activation(out=SB[:, dc, :], in_=STATE[:, h * 2 + dc, :], func=AF.Copy), r=[sk], w=[("SB", dc)])
            pn, pnk = self.psum(7)
            for vc in range(4):
                ps, pk = po[vc]
                sq, sqk = self.ring("sq")
                self.act(lambda e, ps=ps, sq=sq: e.activation(out=sq[:], in_=ps[:], func=AF.Square), r=[pk], w=[sqk])
                self.pe(lambda e, sq=sq, vc=vc: e.matmul(pn[:], self.ONES[:], sq[:], start=(vc == 0), stop=(vc == 3)), r=[sqk, "ONES"], w=[pnk])
            self.act(lambda e: e.activation(out=self.RSTD[:], in_=pn[:], func=AF.Sqrt, bias=self.EPSC[:, 0:1], scale=1.0 / 512), r=[pnk, "EPSC"], w=["RSTD"])
            self.dve(lambda e: e.reciprocal(out=self.RSTD[:], in_=self.RSTD[:]), r=["RSTD"], w=["RSTD"])
            ng = self.vec("glng")
            for vc in range(4):
                ps, pk = po[vc]
                t, tk = self.ring("t32")
                self.dve(lambda e, ps=ps, t=t: e.tensor_tensor(out=t[:], in0=ps[:], in1=self.RSTD[:], op=ALU.mult), r=[pk, "RSTD"], w=[tk])
                self.dve(lambda e, t=t, vc=vc, h=h: e.scalar_tensor_tensor(out=OALL[:, h * 4 + vc, :], in0=t[:], scalar=ng[:, vc:vc + 1], in1=SG[:, vc, :],
                                                                           op0=ALU.mult, op1=ALU.mult), r=[tk, ("SG", vc), "VEC"], w=[("UB", h * 4 + vc)])
        if tb == 0:
            self.dbg("OALL", OALL[:, 0:16, :], [("UB", c) for c in range(16)], BF16)
        ok_ = [("UB", c) for c in range(KC)]
        for p in range(8):
            wv, wk = self.wload("glaout", p)
            for j in range(2):
                oc = 2 * p + j
                ps, pk = self.psum(5 + j)
                for kc in range(KC):
                    self.pe(lambda e, ps=ps, wv=wv, kc=kc, j=j: e.matmul(ps[:], wv[:, kc, j * 128:(j + 1) * 128], OALL[:, kc, :],
                                                                        start=(kc == 0), stop=(kc == KC - 1)), r=[wk] + ok_, w=[pk])
                self.act(lambda e, ps=ps, oc=oc: e.activation(out=self.YB[:, oc, :], in_=ps[:], func=AF.Copy), r=[pk], w=[("YB", oc)])

    def sgu_views(self):
        RWB = self.LS[:, 0:1024].rearrange("p (h i) -> p h i", i=128)
        BSB = self.LS[:, 1024:2048].rearrange("p (h i) -> p h i", i=128)
        WmT = self.LS[:, 2048:2560].bitcast(BF16).rearrange("p (h i) -> p h i", i=128)
        return RWB, BSB, WmT

    def sgu_setup(self):
        RWB, BSB, WmT = self.sgu_views()
        WS = self.YB[:, 0:2, :].rearrange("p c t -> p (c t)").rearrange("p (h i) -> p h i", i=128)
        yk = [("YB", 0), ("YB", 1)]
        self.dma("sp", self.YB[:, 0:2, :].rearrange("p c t -> p (c t)"), self.sgu_wsT, w=yk)
        self.dma("sp", self.LS[:, 1024:2048], self.sgu_bsb, w=["BSB"])
        self.pool(lambda e: e.memset(WS[64:128, :, 0:64], 0.0), r=yk, w=yk)
        self.act(lambda e: e.activation(out=WmT, in_=WS, func=AF.Copy), r=yk, w=["WmT"])
        for half in range(2):
            ps, pk = self.psum()
            self.pe(lambda e, ps=ps, half=half: e.matmul(ps[:], self.ONES[:], WmT[:, half * 4:(half + 1) * 4, :], start=True, stop=True),
                    r=["WmT", "ONES"], w=[pk])
            self.act(lambda e, ps=ps, half=half: e.activation(out=RWB[:, half * 4:(half + 1) * 4, :], in_=ps[:].rearrange("p (h i) -> p h i", i=128),
                                                              func=AF.Copy), r=[pk], w=["RWB"])

    def sgu_mixer(self, tb):
        RWB, BSB, WmT = self.sgu_views()
        U = self.UB
        Vb = self.YB[:, :, :].rearrange("p c t -> p (c t)").bitcast(BF16).rearrange("p (g f) -> p g f", f=4096)
        S1 = self.ST[:, 0:64].rearrange("p (g j) -> p g j", j=16)
        S2 = self.ST[:, 64:128].rearrange("p (g j) -> p g j", j=16)
        M1 = self.ST[:, 128:132]
        M2 = self.ST[:, 132:136]
        RS = self.ST[:, 136:140]
        NM = self.ST[:, 140:144]
        hk = [("HB", c) for c in range(KC)]

        def vbk(tt, j):
            return ("YB", 4 * tt + j // 4)
        for j in range(16):
            wv, wk = self.wload("sguin", 16 + j)
            for tt in range(4):
                ps, pk = self.psum()
                for kc in range(KC):
                    self.pe(lambda e, ps=ps, wv=wv, kc=kc, tt=tt: e.matmul(ps[:, 0:256], self.HB[:, kc, tt * 128:(tt + 1) * 128], wv[:, kc, :],
                                                                           start=(kc == 0), stop=(kc == KC - 1)), r=[wk] + hk, w=[pk])
                self.act(lambda e, ps=ps, tt=tt, j=j: e.activation(out=Vb[:, tt, j * 256:(j + 1) * 256], in_=ps[:, 0:256], func=AF.Gelu,
                                                                   accum_out=S1[:, tt, j:j + 1]), r=[pk], w=[vbk(tt, j), ("S1", tt, j)])
                jk, jkk = self.ring("sq")
                self.act(lambda e, jk=jk, tt=tt, j=j: e.activation(out=jk[:, 0:256], in_=Vb[:, tt, j * 256:(j + 1) * 256], func=AF.Square,
                                                                   accum_out=S2[:, tt, j:j + 1]), r=[vbk(tt, j)], w=[jkk, ("S2", tt, j)])
        for j in range(16):
            wu, wk = self.wload("sguin", j)
            for jj in range(2):
                c = 2 * j + jj
                ps, pk = self.psum()
                for kc in range(KC):
                    self.pe(lambda e, ps=ps, wu=wu, kc=kc, jj=jj: e.matmul(ps[:], wu[:, kc, jj * 128:(jj + 1) * 128], self.HB[:, kc, :],
                                                                           start=(kc == 0), stop=(kc == KC - 1)), r=[wk] + hk, w=[pk])
                self.act(lambda e, ps=ps, c=c: e.activation(out=U[:, c, :], in_=ps[:], func=AF.Gelu), r=[pk], w=[("UB", c)])
        s1k = [("S1", tt, j) for tt in range(4) for j in range(16)]
        s2k = [("S2", tt, j) for tt in range(4) for j in range(16)]
        X = mybir.AxisListType.X
        self.dve(lambda e: e.tensor_reduce(out=M1, in_=S1, axis=X, op=ALU.add), r=s1k, w=["M1"])
        self.dve(lambda e: e.tensor_reduce(out=M2, in_=S2, axis=X, op=ALU.add), r=s2k, w=["M2"])
        self.dve(lambda e: e.tensor_scalar(out=M1, in0=M1, scalar1=1.0 / 4096, scalar2=None, op0=ALU.mult), r=["M1"], w=["M1"])
        self.dve(lambda e: e.tensor_tensor(out=NM, in0=M1, in1=M1, op=ALU.mult), r=["M1"], w=["NM"])
        self.dve(lambda e: e.scalar_tensor_tensor(out=M2, in0=M2, scalar=1.0 / 4096, in1=NM, op0=ALU.mult, op1=ALU.subtract),
                 r=["M2", "NM"], w=["M2"])
        self.act(lambda e: e.activation(out=RS, in_=M2, func=AF.Sqrt, bias=self.EPSC[:, 0:1], scale=1.0), r=["M2", "EPSC"], w=["RS"])
        self.dve(lambda e: e.reciprocal(out=RS, in_=RS), r=["RS"], w=["RS"])
        self.dve(lambda e: e.scalar_tensor_tensor(out=NM, in0=M1, scalar=-1.0, in1=RS, op0=ALU.mult, op1=ALU.mult), r=["M1", "RS"], w=["NM"])
        for tt in range(4):
            ks = [("YB", 4 * tt + q) for q in range(4)]
            self.dve(lambda e, tt=tt: e.tensor_scalar(out=Vb[:, tt, :], in0=Vb[:, tt, :], scalar1=RS[:, tt:tt + 1], scalar2=NM[:, tt:tt + 1],
                                                      op0=ALU.mult, op1=ALU.add), r=ks + ["RS", "NM"], w=ks)
        lg, lb = self.vec("sglg"), self.vec("sglb")
        for c in range(32):
            h = c // 4
            ps, pk = self.psum()
            for tt in range(4):
                self.pe(lambda e, ps=ps, tt=tt, c=c, h=h: e.matmul(ps[:, tt * 128:(tt + 1) * 128], Vb[:, tt, c * 128:(c + 1) * 128], WmT[:, h, :],
                                                                    start=True, stop=True), r=[("YB", 4 * tt + c // 8), "WmT"], w=[pk])
            cb, cbk = self.ring("t32")
            self.dve(lambda e, cb=cb, c=c, h=h: e.scalar_tensor_tensor(out=cb[:, 0:128], in0=RWB[:, h, :], scalar=lb[:, c:c + 1], in1=BSB[:, h, :],
                                                                       op0=ALU.mult, op1=ALU.add), r=["RWB", "BSB", "VEC"], w=[cbk])
            sv, svk = self.ring("acc")
            self.dve(lambda e, cb=cb, sv=sv, ps=ps, c=c: e.scalar_tensor_tensor(
                out=sv[:].rearrange("p (g i) -> p g i", i=128), in0=ps[:].rearrange("p (g i) -> p g i", i=128), scalar=lg[:, c:c + 1],
                in1=cb[:, 0:128].unsqueeze(1).to_broadcast([128, 4, 128]), op0=ALU.mult, op1=ALU.add), r=[pk, cbk, "VEC"], w=[svk])
            self.dve(lambda e, sv=sv, c=c: e.tensor_tensor(out=U[:, c, :], in0=U[:, c, :], in1=sv[:], op=ALU.mult),
                     r=[svk, ("UB", c)], w=[("UB", c)])
        uk = [("UB", c) for c in range(32)]
        for g in range(8):
            pss = [self.psum() for _ in range(2)]
            for kt in range(2):
                wv, wk = self.wload("sguout", g * 2 + kt)
                for jj in range(2):
                    ps, pk = pss[jj]
                    for kc in range(KC):
                        self.pe(lambda e, ps=ps, wv=wv, kc=kc, jj=jj, kt=kt: e.matmul(
                            ps[:], wv[:, kc, jj * 128:(jj + 1) * 128], U[:, kt * 16 + kc, :],
                            start=(kt == 0 and kc == 0), stop=(kt == 1 and kc == KC - 1)), r=[wk] + uk[kt * 16:(kt + 1) * 16], w=[pk])
            for jj in range(2):
                ps, pk = pss[jj]
                oc = 2 * g + jj
                self.act(lambda e, ps=ps, oc=oc: e.activation(out=self.YB[:, oc, :], in_=ps[:], func=AF.Copy), r=[pk], w=[("YB", oc)])

    def ln_stats(self, src, skey, nch, nfeat):
        ps1, pk1 = self.psum()
        ps2, pk2 = self.psum()
        for c in range(nch):
            zb, zk = self.ring("sq")
            sq, sk = self.ring("sq")
            self.act(lambda e, zb=zb, c=c: e.activation(out=zb[:], in_=src[:, c, :], func=AF.Copy), r=[(skey, c)], w=[zk])
            self.act(lambda e, sq=sq, c=c: e.activation(out=sq[:], in_=src[:, c, :], func=AF.Square), r=[(skey, c)], w=[sk])
            self.pe(lambda e, zb=zb, c=c: e.matmul(ps1[:], self.ONES[:], zb[:], start=(c == 0), stop=(c == nch - 1)),
                    r=[zk, "ONES"], w=[pk1])
            self.pe(lambda e, sq=sq, c=c: e.matmul(ps2[:], self.ONES[:], sq[:], start=(c == 0), stop=(c == nch - 1)),
                    r=[sk, "ONES"], w=[pk2])
        MU, RS = self.MU, self.RSTD
        self.act(lambda e: e.activation(out=MU, in_=ps1[:], func=AF.Copy, scale=1.0 / nfeat), r=[pk1], w=["MU"])
        t, tk = self.ring("t32")
        self.dve(lambda e, t=t: e.tensor_tensor(out=t[:], in0=MU, in1=MU, op=ALU.mult), r=["MU"], w=[tk])
        self.dve(lambda e, t=t: e.scalar_tensor_tensor(out=t[:], in0=ps2[:], scalar=1.0 / nfeat, in1=t[:],
                                                       op0=ALU.mult, op1=ALU.subtract), r=[pk2, tk], w=[tk])
        self.act(lambda e, t=t: e.activation(out=RS[:], in_=t[:], func=AF.Sqrt, bias=self.EPSC[:, 0:1], scale=1.0),
                 r=[tk, "EPSC"], w=["RSTD"])
        self.dve(lambda e: e.reciprocal(out=RS[:], in_=RS[:]), r=["RSTD"], w=["RSTD"])

    def conv_mixer(self, tb):
        W_ = TB + 30

        class _GL:
            def __getitem__(s_, idx):
                p, fc, t = idx
                base = self.MX[:, fc * W_:(fc + 1) * W_] if fc < 15 else self.LS[:, 1024:1024 + W_]
                return base[p, t]
        GL = _GL()
        hk = [("HB", c) for c in range(KC)]
        for p in range(8):
            wa, wak = self.wload("cmin", p)
            wg, wgk = self.wload("cmin", 8 + p)
            for j in range(2):
                fc = 2 * p + j
                pa, pak = self.psum()
                pg, pgk = self.psum()
                for kc in range(KC):
                    self.pe(lambda e, pa=pa, j=j, kc=kc, wa=wa: e.matmul(pa[:], wa[:, kc, j * 128:(j + 1) * 128], self.HB[:, kc, :],
                                                                        start=(kc == 0), stop=(kc == KC - 1)), r=[wak] + hk, w=[pak])
                for kc in range(KC):
                    self.pe(lambda e, pg=pg, j=j, kc=kc, wg=wg: e.matmul(pg[:], wg[:, kc, j * 128:(j + 1) * 128], self.HB[:, kc, :],
                                                                        start=(kc == 0), stop=(kc == KC - 1)), r=[wgk] + hk, w=[pgk])
                sg, sgk = self.ring("t32")
                self.act(lambda e, sg=sg, pg=pg: e.activation(out=sg[:], in_=pg[:], func=AF.Sigmoid), r=[pgk], w=[sgk])
                if tb == 0:
                    self.pool(lambda e, fc=fc: e.memset(GL[:, fc, 0:30], 0.0), w=[("GLt", fc)])
                else:
                    self.pool(lambda e, fc=fc: e.tensor_copy(out=GL[:, fc, 0:30], in_=self.CTAIL[:, fc, :]),
                              r=[("CT", fc)], w=[("GLt", fc)])
                self.dve(lambda e, fc=fc, pa=pa, sg=sg: e.tensor_tensor(out=GL[:, fc, 30:TB + 30], in0=pa[:], in1=sg[:], op=ALU.mult),
                         r=[pak, sgk, ("GLt", fc)], w=[("GL", fc)])
                self.pool(lambda e, fc=fc: e.tensor_copy(out=self.CTAIL[:, fc, :], in_=GL[:, fc, TB:TB + 30]),
                          r=[("GL", fc)], w=[("CT", fc)])
        for fc in range(KC):
            rk = [("GL", fc), ("GLt", fc), "VEC"]
            self.dve(lambda e, fc=fc: e.tensor_scalar(out=self.YB[:, fc, :], in0=GL[:, fc, 30:TB + 30],
                                                      scalar1=self.vec("cmw30")[:, fc:fc + 1], scalar2=self.vec("cmb")[:, fc:fc + 1],
                                                      op0=ALU.mult, op1=ALU.add), r=rk, w=[("YB", fc)])
            for k in range(30):
                self.dve(lambda e, fc=fc, k=k: e.scalar_tensor_tensor(out=self.YB[:, fc, :], in0=GL[:, fc, k:k + TB],
                                                                      scalar=self.vec(f"cmw{k}")[:, fc:fc + 1], in1=self.YB[:, fc, :],
                                                                      op0=ALU.mult, op1=ALU.add), r=rk + [("YB", fc)], w=[("YB", fc)])
        self.ln_stats(self.YB, "YB", KC, D)
        lg, lb = self.vec("cmlg"), self.vec("cmlb")
        for c in range(KC):
            t, tk = self.ring("t32")
            self.dve(lambda e, t=t, c=c: e.tensor_tensor(out=t[:], in0=self.YB[:, c, :], in1=self.MU, op=ALU.subtract),
                     r=[("YB", c), "MU"], w=[tk])
            self.dve(lambda e, t=t: e.tensor_tensor(out=t[:], in0=t[:], in1=self.RSTD[:], op=ALU.mult), r=[tk, "RSTD"], w=[tk])
            self.act(lambda e, t=t, c=c: e.activation(out=self.HB[:, c, :], in_=t[:], func=AF.Silu,
                                                      scale=lg[:, c:c + 1], bias=lb[:, c:c + 1]), r=[tk, "VEC"], w=[("HB", c)])
        for p in range(8):
            wv, wk = self.wload("cmout", p)
            for j in range(2):
                oc = 2 * p + j
                ps, pk = self.psum()
                for kc in range(KC):
                    self.pe(lambda e, ps=ps, wv=wv, kc=kc, j=j: e.matmul(ps[:], wv[:, kc, j * 128:(j + 1) * 128], self.HB[:, kc, :],
                                                                        start=(kc == 0), stop=(kc == KC - 1)), r=[wk] + hk, w=[pk])
                self.act(lambda e, ps=ps, oc=oc: e.activation(out=self.YB[:, oc, :], in_=ps[:], func=AF.Copy),
                         r=[pk], w=[("YB", oc)])


def build_program(cfg, vec_off, nvec):
    b = Builder(cfg, vec_off, nvec)
    with b.st:
        b.EPSC = b.sb("EPSC", [128, 1], F32)
        b.S.add("dve", lambda e: e.memset(b.EPSC[:], EPS), (), ["EPSC"])
        nc = b.build()
    return nc


FULL_CFG = dict(layers=[0, 1, 2, 3], ntb=NTB, mixer=True)


def run(inp, cfg, core_ids=None, trace=False):
    shared, percore, voff = prep_inputs(inp, cfg)
    nvec = shared["vecs"].shape[1]
    nc = build_program(cfg, voff, nvec)
    if core_ids is None:
        core_ids = list(range(NB))
    in_maps = []
    for b in core_ids:
        m = dict(shared)
        m.update(percore[b])
        in_maps.append(m)
    res = run_bass_kernel_spmd(nc, in_maps, core_ids=list(range(len(core_ids))), trace=trace)
    return res


def kernel(**inp):
    inp = {k: np.asarray(v) for k, v in inp.items()}
    res = run(inp, FULL_CFG)
    out = np.stack([np.ascontiguousarray(res.results[b]["outT"].T) for b in range(NB)], 0)
    return out.astype(np.float32)
```

```python
import contextlib
import numpy as np
import concourse.bass as bass
import concourse.mybir as mybir
from concourse.bass_utils import run_bass_kernel_spmd

F32 = mybir.dt.float32
BF16 = mybir.dt.bfloat16
AF = mybir.ActivationFunctionType
ALU = mybir.AluOpType

D = 2048
L_SEQ = 4096
NB = 8
DEPTH = 4
TB = 512
NTB = L_SEQ // TB
KC = D // 128
FFH = 5632
FFC = FFH // 128
EPS = 1e-6
WSLOT = 4096
NWSLOT = 4
STRICT = True


class _Op:
    __slots__ = ("idx", "eng", "fn", "deps", "dma", "slot", "val", "sig", "cnt")


class Sched:
    ENGS = ("pe", "act", "dve", "pool", "sp")
    NSLOT = 12
    EPOCH = 30000

    def __init__(self):
        self.ops = []
        self.streams = {e: [] for e in self.ENGS}
        self.lastw = {}
        self.rdc = {}
        self.rdd = {}
        self.dcount = {e: 0 for e in self.ENGS}
        self.slot_last = {}

    def add(self, eng, fn, reads=(), writes=(), dma=False):
        op = _Op()
        op.idx = len(self.ops)
        op.eng = eng
        op.fn = fn
        op.dma = dma
        op.sig = False
        op.cnt = 0
        op.slot = -1
        op.val = 0
        deps = set()
        for k in reads:
            w = self.lastw.get(k)
            if w is not None:
                deps.add(w)
        for k in writes:
            w = self.lastw.get(k)
            if w is not None:
                deps.add(w)
            r = self.rdc.get(k)
            if r:
                deps.update(r.values())
            r = self.rdd.get(k)
            if r:
                deps.update(r)
        if dma:
            n = self.dcount[eng]
            self.dcount[eng] = n + 1
            op.slot = n % self.NSLOT
            op.val = 16 * (n // self.NSLOT + 1)
            prev = self.slot_last.get((eng, op.slot))
            if prev is not None:
                deps.add(prev)
            self.slot_last[(eng, op.slot)] = op.idx
        for k in reads:
            if dma:
                self.rdd.setdefault(k, []).append(op.idx)
            else:
                self.rdc.setdefault(k, {})[eng] = op.idx
        for k in writes:
            self.lastw[k] = op.idx
            self.rdc[k] = {}
            self.rdd[k] = []
        deps.discard(op.idx)
        op.deps = deps
        self.ops.append(op)
        self.streams[eng].append(op)
        return op

    def _needs_wait(self, op, d):
        if d.dma or op.dma:
            return True
        if d.eng == op.eng:
            if op.eng == "pe":
                return False
            return STRICT
        return True

    def finalize(self):
        for op in self.ops:
            for di in op.deps:
                d = self.ops[di]
                if not d.dma and self._needs_wait(op, d):
                    d.sig = True
        self.nepoch = {}
        for e in self.ENGS:
            c = 0
            for op in self.streams[e]:
                if op.sig and not op.dma:
                    c += 1
                    op.cnt = c
            self.nepoch[e] = max(1, (c + self.EPOCH - 1) // self.EPOCH)

    def emit(self, nc):
        self.finalize()
        with contextlib.ExitStack() as st:
            csem = {}
            for e in self.ENGS:
                csem[e] = [st.enter_context(nc.semaphore(f"c_{e}_{i}")) for i in range(self.nepoch[e])]
            dsem = {}
            for e in self.ENGS:
                if self.dcount[e]:
                    dsem[e] = [st.enter_context(nc.semaphore(f"d_{e}_{i}")) for i in range(self.NSLOT)]
            block = st.enter_context(nc.Block())
            ops = self.ops
            EP = self.EPOCH

            def run(ename, eng):
                known_c = {}
                known_d = {}
                for op in self.streams[ename]:
                    need_c = {}
                    need_d = {}
                    for di in op.deps:
                        d = ops[di]
                        if not self._needs_wait(op, d):
                            continue
                        if d.dma:
                            k = (d.eng, d.slot)
                            if need_d.get(k, 0) < d.val:
                                need_d[k] = d.val
                        else:
                            if need_c.get(d.eng, 0) < d.cnt:
                                need_c[d.eng] = d.cnt
                    for e2, c in need_c.items():
                        if known_c.get(e2, 0) >= c:
                            continue
                        known_c[e2] = c
                        eng.wait_ge(csem[e2][(c - 1) // EP], (c - 1) % EP + 1)
                    for k, v in need_d.items():
                        if known_d.get(k, 0) >= v:
                            continue
                        known_d[k] = v
                        eng.wait_ge(dsem[k[0]][k[1]], v)
                    ins = op.fn(eng)
                    if op.dma:
                        ins.then_inc(dsem[ename][op.slot], 16)
                    elif op.sig:
                        c = op.cnt
                        ins.then_inc(csem[ename][(c - 1) // EP], 1)

            @block.tensor
            def _(e):
                run("pe", e)

            @block.scalar
            def _(e):
                run("act", e)

            @block.vector
            def _(e):
                run("dve", e)

            @block.gpsimd
            def _(e):
                run("pool", e)

            @block.sync
            def _(e):
                run("sp", e)


def _tile_weight(W, nk, fw, col_groups=None):
    K, F = W.shape
    nkt = K // (128 * nk)
    if col_groups is None:
        col_groups = [(c, fw) for c in range(0, F, fw)]
    tiles = []
    for (c0, w) in col_groups:
        blk = W[:, c0:c0 + w]
        if w < fw:
            blk = np.concatenate([blk, np.zeros((K, fw - w), W.dtype)], axis=1)
        for kt in range(nkt):
            t = blk[kt * nk * 128:(kt + 1) * nk * 128].reshape(nk, 128, fw).transpose(1, 0, 2)
            tiles.append(t.reshape(128, nk * fw))
    return np.ascontiguousarray(np.stack(tiles, 0))


def _pvec(v):
    v = np.asarray(v, np.float32).reshape(-1, 128)
    return np.ascontiguousarray(v.T)


def _ret_consts():
    f32 = np.float32
    half = 128
    inv_freq = (np.float32(10000.0) ** (-np.arange(half, dtype=f32) / f32(half))).astype(f32)
    ang = (np.arange(L_SEQ, dtype=f32)[None, :] * inv_freq[:, None]).astype(f32)
    H = 8
    lg = np.log1p(-np.exp2(-5.0 - np.arange(H, dtype=np.float64)))
    tl = np.arange(TB, dtype=np.float64)
    gq = np.exp(lg[:, None] * tl[None, :])
    gk = np.exp(-lg[:, None] * tl[None, :]) * (256.0 ** -0.5)
    s_ = np.arange(128)[:, None]
    t_ = np.arange(128)[None, :]
    cs, ct = s_ // 64, t_ // 64
    dm = np.zeros((H, 128, 128))
    for h in range(H):
        same = np.where(s_ <= t_, 1.0, np.exp(lg[h] * 2.0 * (s_ - t_)))
        dm[h] = np.where(cs < ct, 1.0, np.where(cs > ct, 0.0, same))
    hc = np.zeros((H, 128, 1152), f32)
    hc[:, :, 0:512] = gq[:, None, :]
    hc[:, :, 512:1024] = gk[:, None, :]
    hc[:, :, 1024:1152] = dm
    return {
        "rot_cos": np.ascontiguousarray(np.cos(ang).astype(f32)),
        "rot_sin": np.ascontiguousarray(np.sin(ang).astype(f32)),
        "ret_hc": np.ascontiguousarray(hc),
        "ident": np.eye(128, dtype=f32),
    }


def _gla_consts():
    f32 = np.float32
    s_ = np.arange(128)[:, None]
    t_ = np.arange(128)[None, :]
    same = (s_ // 64) == (t_ // 64)
    tri = np.where(same & (s_ <= t_), -1.0 / 16.0, 0.0).astype(f32)
    tri2 = np.where(same & (s_ > t_), -1.0 / 16.0, 0.0).astype(f32)
    sl = (np.arange(128) % 64)[:, None]
    tl = np.arange(64)[None, :]
    um = (sl <= tl).astype(f32)
    lm = (sl > tl).astype(f32)
    return np.ascontiguousarray(np.concatenate([tri, tri2, um, lm], axis=1))


class _VecPack:
    def __init__(self):
        self.cols = []
        self.off = {}
        self.n = 0

    def add(self, name, arr):
        self.off[name] = (self.n, arr.shape[1])
        self.cols.append(arr)
        self.n += arr.shape[1]

    def build(self):
        return np.ascontiguousarray(np.concatenate(self.cols, axis=1).astype(np.float32))


def weight_specs():
    specs = {}
    for i in range(DEPTH):
        specs[f"ada{i}"] = dict(K=D, F=6 * D, nk=16, fw=256)
        specs[f"ffin{i}"] = dict(K=D, F=2 * FFH, nk=16, fw=256)
        specs[f"ffout{i}"] = dict(K=FFH, F=D, nk=22, fw=128)
    specs["cmin"] = dict(K=D, F=2 * D, nk=16, fw=256)
    specs["cmout"] = dict(K=D, F=D, nk=16, fw=256)
    specs["sguin"] = dict(K=D, F=4 * D, nk=16, fw=256)
    specs["retin"] = dict(K=D, F=6 * D, nk=16, fw=256)
    specs["glain"] = dict(K=D, F=6144, nk=16, fw=256)
    specs["glag"] = dict(K=D, F=16, nk=16, fw=16)
    specs["glaout"] = dict(K=D, F=D, nk=16, fw=256)
    specs["retout"] = dict(K=2 * D, F=D, nk=16, fw=256)
    specs["sguout"] = dict(K=2 * D, F=D, nk=16, fw=256)
    for s in specs.values():
        s["nkt"] = s["K"] // (128 * s["nk"])
        s["ng"] = s["F"] // s["fw"]
        s["nt"] = s["nkt"] * s["ng"]
        s["E"] = s["nk"] * s["fw"]
    return specs


def prep_inputs(inp, cfg):
    shared = {}
    for i in cfg["layers"]:
        shared[f"ada{i}"] = _tile_weight(inp["ada_w"][i], 16, 256)
        shared[f"ffin{i}"] = _tile_weight(inp["ffn_w_in"][i], 16, 256)
        shared[f"ffout{i}"] = _tile_weight(inp["ffn_w_out"][i], 22, 128)
    if 0 in cfg["layers"] and cfg.get("mixer", True):
        shared["cmin"] = _tile_weight(inp["cm_w_in"][0], 16, 256)
        shared["cmout"] = _tile_weight(inp["cm_w_out"][0], 16, 256)
    if 3 in cfg["layers"] and cfg.get("mixer", True):
        shared["sguin"] = _tile_weight(inp["sgu_w_in"][0], 16, 256)
        shared["sguout"] = _tile_weight(inp["sgu_w_out"][0], 16, 256)
    if 2 in cfg["layers"] and cfg.get("mixer", True):
        shared["retin"] = _tile_weight(inp["ret_w_in"][0], 16, 256)
        shared["retout"] = _tile_weight(inp["ret_w_out"][0], 16, 256)
    shared.update(_ret_consts())
    if 1 in cfg["layers"] and cfg.get("mixer", True):
        shared["glain"] = _tile_weight(inp["gla_w_in"][0][:, :6144], 16, 256)
        shared["glag"] = _tile_weight(inp["gla_w_in"][0][:, 6144:6160], 16, 16)
        shared["glaout"] = _tile_weight(inp["gla_w_out"][0], 16, 256)
    wg = np.zeros((32, 1024), np.float32)
    wg[0:16] = inp["gla_w_gate"][0]
    wg[16] = inp["gla_b_gate"][0]
    shared["gla_wg"] = wg
    shared["gla_c"] = _gla_consts()
    shared["sgu_wsT"] = np.ascontiguousarray(inp["sgu_w_s"][0].transpose(2, 0, 1).reshape(128, 1024).astype(np.float32))
    shared["sgu_bsb"] = np.ascontiguousarray(np.broadcast_to(inp["sgu_b_s"][0].reshape(1, 1024), (128, 1024)).astype(np.float32))
    vp = _VecPack()
    for i in range(DEPTH):
        vp.add(f"adab{i}", _pvec(inp["ada_b"][i]))
        for j in range(4):
            vp.add(f"ng{i}_{j}", _pvec(inp["norm_g"][i, j]))
        for k in range(3):
            vp.add(f"fcw{i}_{k}", _pvec(inp["ffn_conv_w"][i, k]))
        vp.add(f"fcb{i}", _pvec(inp["ffn_conv_b"][i]))
    for k in range(31):
        vp.add(f"cmw{k}", _pvec(inp["cm_dw_w"][0, k]))
    vp.add("cmb", _pvec(inp["cm_dw_b"][0]))
    vp.add("cmlg", _pvec(inp["cm_ln_g"][0]))
    vp.add("cmlb", _pvec(inp["cm_ln_b"][0]))
    vp.add("sglg", _pvec(inp["sgu_ln_g"][0]))
    vp.add("glng", _pvec(inp["gla_norm_g"][0]))
    vp.add("sglb", _pvec(inp["sgu_ln_b"][0]))
    vecs_common = vp
    percore = []
    for b in range(NB):
        d = {}
        d["xT"] = np.ascontiguousarray(inp["x"][b].T)
        d["cvec"] = _pvec(inp["c"][b])
        percore.append(d)
    shared["vecs"] = vp.build()
    return shared, percore, vp.off


class Builder:
    def __init__(self, cfg, vec_off, nvec):
        self.cfg = cfg
        self.voff = vec_off
        self.nvec = nvec
        self.nc = bass.Bass("TRN2", target_bir_lowering=False)
        self.S = Sched()
        self.st = contextlib.ExitStack()
        self.specs = weight_specs()
        self.wslot_i = 0
        self.dbg_done = set()
        self.ps_i = 0
        self.uid = 0

    def sb(self, name, shape, dt):
        return self.st.enter_context(self.nc.sbuf_tensor(name, list(shape), dt))

    def dram(self, name, shape, dt, kind):
        return self.nc.dram_tensor(name, list(shape), dt, kind=kind).ap()

    def pe(self, fn, r=(), w=()):
        return self.S.add("pe", fn, r, w)

    def act(self, fn, r=(), w=()):
        return self.S.add("act", fn, r, w)

    def dve(self, fn, r=(), w=()):
        return self.S.add("dve", fn, r, w)

    def pool(self, fn, r=(), w=()):
        return self.S.add("pool", fn, r, w)

    def dma(self, q, out, in_, r=(), w=()):
        return self.S.add(q, lambda e: e.dma_start(out=out, in_=in_), r, w, dma=True)

    def dbg(self, name, ap, keys, dt=F32):
        if not self.cfg.get("debug") or name in self.dbg_done:
            return
        self.dbg_done.add(name)
        d = self.dram("dbg_" + name, list(ap.shape), dt, "ExternalOutput")
        self.dma("pool", d, ap, r=list(keys), w=[("dbg", name)])

    def psum(self, i=None):
        if i is None:
            i = self.ps_i % 8
            self.ps_i += 1
        return self.PS[i], ("PS", i)

    def ring(self, name):
        bufs, st = self.rings[name]
        i = st[0] % len(bufs)
        st[0] += 1
        return bufs[i], (name, i)

    def mkring(self, name, n, shape, dt):
        self.rings[name] = ([self.sb(f"{name}{i}", shape, dt) for i in range(n)], [0])

    def wload(self, mat, t):
        sp = self.specs[mat]
        slot = self.wslot_i % NWSLOT
        self.wslot_i += 1
        buf = self.WR[slot]
        E = sp["E"]
        src = self.wb[mat][t]
        key = ("WR", slot)
        self.dma("sp", buf[:, 0:E], src, r=[("wb", mat, 0 if mat == "cmdg" else t // 16)], w=[key])
        view = buf[:, 0:E].rearrange("p (k f) -> p k f", f=sp["fw"])
        return view, key

    def cast_weights(self, mat):
        sp = self.specs[mat]
        nt, E = sp["nt"], sp["E"]
        a = 1
        while E // a > 2048 or E % a:
            a += 1
        src = self.wf[mat].rearrange("t p (a b) -> t p a b", a=a)
        dst = self.wb[mat].rearrange("t p (a b) -> t p a b", a=a)
        for g in range(0, nt, 16):
            n = min(16, nt - g)
            self.dma("pool", dst[g:g + n], src[g:g + n], w=[("wb", mat, g // 16)])

    def vec(self, name, c0=0, n=None):
        o, w = self.voff[name]
        if n is None:
            n = w - c0
        return self.VEC[:, o + c0:o + c0 + n]

    def build(self):
        nc, cfg = self.nc, self.cfg
        layers = cfg["layers"]
        ntb = cfg.get("ntb", NTB)
        self.ntb = ntb
        self.xT = self.dram("xT", [D, L_SEQ], F32, "ExternalInput")
        self.cvec = self.dram("cvec", [128, KC], F32, "ExternalInput")
        self.vecs = self.dram("vecs", [128, self.nvec], F32, "ExternalInput")
        self.outT = self.dram("outT", [D, L_SEQ], F32, "ExternalOutput")
        self.xs = [self.dram(f"xs{i}", [D, L_SEQ], F32, "Internal") for i in range(2)]
        self.sgu_wsT = self.dram("sgu_wsT", [128, 1024], F32, "ExternalInput")
        self.rot_cos = self.dram("rot_cos", [128, L_SEQ], F32, "ExternalInput")
        self.gla_wg = self.dram("gla_wg", [32, 1024], F32, "ExternalInput")
        self.gla_c = self.dram("gla_c", [128, 384], F32, "ExternalInput")
        self.rot_sin = self.dram("rot_sin", [128, L_SEQ], F32, "ExternalInput")
        self.ret_hc = self.dram("ret_hc", [8, 128, 1152], F32, "ExternalInput")
        self.ident_d = self.dram("ident", [128, 128], F32, "ExternalInput")
        self.sgu_bsb = self.dram("sgu_bsb", [128, 1024], F32, "ExternalInput")
        self.wf, self.wb = {}, {}
        mats = []
        for i in layers:
            mats.append(f"ada{i}")
        for i in layers:
            if cfg.get("mixer", True):
                mats += {0: ["cmin", "cmout"], 1: ["glain", "glag", "glaout"], 2: ["retin", "retout"], 3: ["sguin", "sguout"]}[i]
            mats += [f"ffin{i}", f"ffout{i}"]
        for m in mats:
            sp = self.specs[m]
            self.wf[m] = self.dram(m, [sp["nt"], 128, sp["E"]], F32, "ExternalInput")
            self.wb[m] = self.dram(m + "_b", [sp["nt"], 128, sp["E"]], BF16, "Internal")
        self.rings = {}
        self.VEC = self.sb("VEC", [128, self.nvec], F32)
        self.CV = self.sb("CV", [128, KC], F32)
        self.CA = self.sb("CA", [128, KC], BF16)
        self.MODS = [self.sb(f"MOD{i}", [128, 96], F32) for i in range(2)]
        self.PV = self.sb("PV", [128, 96], F32)
        self.ONES = self.sb("ONES", [128, 128], BF16)
        self.IDENT = self.sb("IDENT", [128, 128], BF16)
        self.XB = self.sb("XB", [128, KC, TB], F32)
        self.YB = self.sb("YB", [128, KC, TB], F32)
        self.HB = self.sb("HB", [128, KC, TB], BF16)
        self.MX = self.sb("MX", [128, 32 * TB // 2], F32)
        self.UB = self.MX[:, :].bitcast(BF16).rearrange("p (c t) -> p c t", t=TB)
        self.WR = [self.sb(f"WR{i}", [128, WSLOT], BF16) for i in range(NWSLOT)]
        self.RSTD = self.sb("RSTD", [128, TB], F32)
        self.LS = self.sb("LS", [128, 8192], F32)
        self.CTAIL = self.LS[:, 0:KC * 30].rearrange("p (c t) -> p c t", t=30)
        self.MU = self.LS[:, 512:1024]
        self.ST = self.sb("ST", [128, 256], F32)
        self.FTAIL = self.sb("FTAIL", [128, FFC, 2], F32)
        self.mkring("sq", 2, [128, TB], BF16)
        self.mkring("t32", 2, [128, TB], F32)
        self.mkring("ab", 2, [128, TB + 2], F32)
        self.mkring("acc", 2, [128, TB], F32)
        self.mkring("so", 2, [128, TB], F32)
        self.PS = [self.st.enter_context(nc.psum_tensor(f"PS{i}", [128, TB], F32)) for i in range(8)]

        for m in mats:
            self.cast_weights(m)
        self.dma("sp", self.VEC[:], self.vecs, w=["VEC"])
        self.dma("sp", self.CV[:], self.cvec, w=["CV"])
        self.dve(lambda e: e.memset(self.ONES[:], 1.0), w=["ONES"])
        it, itk = self.ring("t32")
        self.dma("sp", it[:, 0:128], self.ident_d, w=[itk])
        self.act(lambda e: e.activation(out=self.IDENT[:], in_=it[:, 0:128], func=AF.Copy), r=[itk], w=["IDENT"])
        self.act(lambda e: e.activation(out=self.CA[:], in_=self.CV[:], func=AF.Silu), r=["CV"], w=["CA"])

        xsrc = self.xT
        for g in range(0, 48, 6):
            self.ada_part(layers[0], g, g + 6)
        for li, i in enumerate(layers):
            last = li == len(layers) - 1
            xdst = self.outT if last else self.xs[li % 2]
            self.ada_fin(i)
            if cfg.get("mixer", True) and i == 0:
                self.conv_setup()
            if cfg.get("mixer", True) and i == 3:
                self.sgu_setup()
            if cfg.get("mixer", True) and i == 1:
                self.gla_setup()
            if cfg.get("mixer", True) and i == 2:
                self.pool(lambda e: e.memset(self.LS[:], 0.0), w=[("STATE", j) for j in range(16)])
            self.dve(lambda e: e.memset(self.FTAIL[:], 0.0), w=[("FT", fc) for fc in range(FFC)])
            for tb in range(ntb):
                self.load_x(xsrc, tb)
                if cfg.get("mixer", True):
                    self.prenorm(0)
                    self.mixer(i, tb)
                    self.postnorm(1, None, tb)
                self.prenorm(3)
                self.ffn(i, tb)
                self.postnorm(4, xdst, tb)
                if not last:
                    npart = 48 // ntb if 48 % ntb == 0 else 48
                    if npart == 48:
                        if tb == 0:
                            for g in range(0, 48, 6):
                                self.ada_part(layers[li + 1], g, g + 6)
                    else:
                        for g in range(tb * npart, (tb + 1) * npart, 6):
                            self.ada_part(layers[li + 1], g, min(g + 6, (tb + 1) * npart))
            xsrc = xdst
        self.S.add("sp", lambda e: e.nop(), [("xd", id(xdst), tb, c) for tb in range(ntb) for c in range(KC)]
                   + [("dbg", n) for n in self.dbg_done], [])
        self.S.emit(nc)
        return nc

    def ada_part(self, i, g0, g1):
        ps, pk = self.psum()
        mat = f"ada{i}"
        M = self.MODS[i % 2]
        for g in range(g0, g1):
            wv, wk = self.wload(mat, g)
            for j in range(2):
                col = g * 2 + j
                for kc in range(KC):
                    self.pe(lambda e, wv=wv, j=j, kc=kc, col=col, ps=ps: e.matmul(
                        ps[:, col:col + 1], wv[:, kc, j * 128:(j + 1) * 128], self.CA[:, kc:kc + 1],
                        start=(kc == 0), stop=(kc == KC - 1)),
                        r=[wk, "CA"], w=[pk])
        c0, c1 = 2 * g0, 2 * g1
        self.dve(lambda e, ps=ps: e.tensor_tensor(out=M[:, c0:c1], in0=ps[:, c0:c1], in1=self.vec(f"adab{i}")[:, c0:c1], op=ALU.add),
                 r=[pk, "VEC"], w=[("MOD", i % 2, g0)])

    def ada_fin(self, i):
        M, P = self.MODS[i % 2], self.PV
        mk = [("MOD", i % 2, g) for g in range(0, 48, 6)]

        def sl(t, j):
            return t[:, j * 16:(j + 1) * 16]
        self.dve(lambda e: e.scalar_tensor_tensor(out=sl(P, 0), in0=sl(M, 1), scalar=1.0, in1=self.vec(f"ng{i}_0"),
                                                  op0=ALU.add, op1=ALU.mult), r=mk + ["VEC"], w=["PV0"])
        self.dve(lambda e: e.tensor_copy(out=sl(P, 1), in_=sl(M, 0)), r=mk, w=["PV1"])
        self.dve(lambda e: e.tensor_tensor(out=sl(P, 2), in0=sl(M, 2), in1=self.vec(f"ng{i}_1"), op=ALU.mult),
                 r=mk + ["VEC"], w=["PV2"])
        self.dve(lambda e: e.scalar_tensor_tensor(out=sl(P, 3), in0=sl(M, 4), scalar=1.0, in1=self.vec(f"ng{i}_2"),
                                                  op0=ALU.add, op1=ALU.mult), r=mk + ["VEC"], w=["PV3"])
        self.dve(lambda e: e.tensor_copy(out=sl(P, 4), in_=sl(M, 3)), r=mk, w=["PV4"])
        self.dve(lambda e: e.tensor_tensor(out=sl(P, 5), in0=sl(M, 5), in1=self.vec(f"ng{i}_3"), op=ALU.mult),
                 r=mk + ["VEC"], w=["PV5"])

    def load_x(self, xsrc, tb):
        src = xsrc.rearrange("(c p) t -> p c t", p=128)
        for q in range(4):
            self.dma("sp", self.XB[:, q * 4:(q + 1) * 4, :], src[:, q * 4:(q + 1) * 4, tb * TB:(tb + 1) * TB],
                     r=[("xd", id(xsrc), tb, c) for c in range(q * 4, q * 4 + 4)],
                     w=[("XB", c) for c in range(q * 4, q * 4 + 4)])

    def rstd_from(self, srcbuf, skey):
        ps, pk = self.psum()
        for c in range(KC):
            sq, sk = self.ring("sq")
            self.act(lambda e, sq=sq, c=c: e.activation(out=sq[:], in_=srcbuf[:, c, :], func=AF.Square),
                     r=[(skey, c)], w=[sk])
            self.pe(lambda e, sq=sq, c=c: e.matmul(ps[:], self.ONES[:], sq[:], start=(c == 0), stop=(c == KC - 1)),
                    r=[sk, "ONES"], w=[pk])
        self.act(lambda e: e.activation(out=self.RSTD[:], in_=ps[:], func=AF.Sqrt, bias=self.EPSC[:, 0:1], scale=1.0 / D),
                 r=[pk, "EPSC"], w=["RSTD"])
        self.dve(lambda e: e.reciprocal(out=self.RSTD[:], in_=self.RSTD[:]), r=["RSTD"], w=["RSTD"])

    def prenorm(self, j):
        self.rstd_from(self.XB, "XB")
        A = self.PV[:, j * 16:(j + 1) * 16]
        Bv = self.PV[:, (j + 1) * 16:(j + 2) * 16]
        for c in range(KC):
            t, tk = self.ring("t32")
            self.dve(lambda e, t=t, c=c: e.tensor_tensor(out=t[:], in0=self.XB[:, c, :], in1=self.RSTD[:], op=ALU.mult),
                     r=[("XB", c), "RSTD"], w=[tk])
            self.act(lambda e, t=t, c=c: e.activation(out=self.HB[:, c, :], in_=t[:], func=AF.Identity,
                                                      scale=A[:, c:c + 1], bias=Bv[:, c:c + 1]),
                     r=[tk, f"PV{j}", f"PV{j + 1}"], w=[("HB", c)])

    def postnorm(self, j, xdst, tb):
        self.rstd_from(self.YB, "YB")
        G = self.PV[:, (j + 1) * 16:(j + 2) * 16]
        gk = f"PV{j + 1}"
        for c in range(KC):
            t, tk = self.ring("t32")
            self.dve(lambda e, t=t, c=c: e.tensor_tensor(out=t[:], in0=self.YB[:, c, :], in1=self.RSTD[:], op=ALU.mult),
                     r=[("YB", c), "RSTD"], w=[tk])
            if xdst is None:
                self.dve(lambda e, t=t, c=c: e.scalar_tensor_tensor(out=self.XB[:, c, :], in0=t[:], scalar=G[:, c:c + 1],
                                                                    in1=self.XB[:, c, :], op0=ALU.mult, op1=ALU.add),
                         r=[tk, gk, ("XB", c)], w=[("XB", c)])
            else:
                so, sk = self.ring("so")
                self.dve(lambda e, t=t, c=c, so=so: e.scalar_tensor_tensor(out=so[:], in0=t[:], scalar=G[:, c:c + 1],
                                                                           in1=self.XB[:, c, :], op0=ALU.mult, op1=ALU.add),
                         r=[tk, gk, ("XB", c)], w=[sk])
                self.dma("pool", xdst[c * 128:(c + 1) * 128, tb * TB:(tb + 1) * TB], so[:],
                         r=[sk], w=[("xd", id(xdst), tb, c)])

    def ffn(self, i, tb):
        mi, mo = f"ffin{i}", f"ffout{i}"
        hk = [("HB", c) for c in range(KC)]
        for half in range(2):
          for p in range(11 * half, 11 * half + 11):
                wa, wak = self.wload(mi, p)
                wb_, wbk = self.wload(mi, 22 + p)
                chunks = []
                for j in range(2):
                    fc = 2 * p + j
                    pa, pak = self.psum()
                    pb, pbk = self.psum()
                    for kc in range(KC):
                        self.pe(lambda e, pa=pa, j=j, kc=kc, wa=wa: e.matmul(pa[:], wa[:, kc, j * 128:(j + 1) * 128], self.HB[:, kc, :],
                                                                            start=(kc == 0), stop=(kc == KC - 1)),
                                r=[wak] + hk, w=[pak])
                    for kc in range(KC):
                        self.pe(lambda e, pb=pb, j=j, kc=kc, wb_=wb_: e.matmul(pb[:], wb_[:, kc, j * 128:(j + 1) * 128], self.HB[:, kc, :],
                                                                               start=(kc == 0), stop=(kc == KC - 1)),
                                r=[wbk] + hk, w=[pbk])
                    chunks.append((fc, pa, pak, pb, pbk))
                st = []
                for (fc, pa, pak, pb, pbk) in chunks:
                    ab, abk = self.ring("ab")
                    acc, ack = self.ring("acc")
                    self.pool(lambda e, ab=ab, fc=fc: e.tensor_copy(out=ab[:, 0:2], in_=self.FTAIL[:, fc, :]),
                              r=[("FT", fc)], w=[(abk, "t")])
                    self.act(lambda e, ab=ab, pa=pa: e.activation(out=ab[:, 2:TB + 2], in_=pa[:], func=AF.Copy),
                             r=[pak], w=[abk])
                    self.pool(lambda e, ab=ab, fc=fc: e.tensor_copy(out=self.FTAIL[:, fc, :], in_=ab[:, TB:TB + 2]),
                              r=[abk], w=[("FT", fc)])
                    st.append((fc, ab, abk, acc, ack, pb, pbk))
                w0, w1, w2, cb = (self.vec(f"fcw{i}_0"), self.vec(f"fcw{i}_1"), self.vec(f"fcw{i}_2"), self.vec(f"fcb{i}"))
                for (fc, ab, abk, acc, ack, pb, pbk) in st:
                    self.dve(lambda e, ab=ab, acc=acc, fc=fc: e.tensor_scalar(out=acc[:], in0=ab[:, 2:TB + 2], scalar1=w2[:, fc:fc + 1],
                                                                             scalar2=cb[:, fc:fc + 1], op0=ALU.mult, op1=ALU.add),
                             r=[abk, "VEC"], w=[ack])
                for (fc, ab, abk, acc, ack, pb, pbk) in st:
                    self.dve(lambda e, ab=ab, acc=acc, fc=fc: e.scalar_tensor_tensor(out=acc[:], in0=ab[:, 1:TB + 1], scalar=w1[:, fc:fc + 1],
                                                                                    in1=acc[:], op0=ALU.mult, op1=ALU.add),
                             r=[abk, (abk, "t"), ack, "VEC"], w=[ack])
                for (fc, ab, abk, acc, ack, pb, pbk) in st:
                    self.dve(lambda e, ab=ab, acc=acc, fc=fc: e.scalar_tensor_tensor(out=acc[:], in0=ab[:, 0:TB], scalar=w0[:, fc:fc + 1],
                                                                                    in1=acc[:], op0=ALU.mult, op1=ALU.add),
                             r=[abk, (abk, "t"), ack, "VEC"], w=[ack])
                for (fc, ab, abk, acc, ack, pb, pbk) in st:
                    self.act(lambda e, acc=acc: e.activation(out=acc[:], in_=acc[:], func=AF.Silu), r=[ack], w=[ack])
                for (fc, ab, abk, acc, ack, pb, pbk) in st:
                    self.dve(lambda e, acc=acc, pb=pb, fc=fc, half=half: e.tensor_tensor(out=self.UB[:, fc - 22 * half, :], in0=acc[:], in1=pb[:], op=ALU.mult),
                             r=[ack, pbk], w=[("UB", fc - 22 * half)])
          uk = [("UB", c) for c in range(22)]
          for oc in range(KC):
              ps, pk = self.psum()
              wv, wk = self.wload(mo, oc * 2 + half)
              for kc in range(22):
                  self.pe(lambda e, ps=ps, wv=wv, kc=kc: e.matmul(ps[:], wv[:, kc, :], self.UB[:, kc, :],
                                                                 start=(kc == 0), stop=(kc == 21)), r=[wk] + uk, w=[pk])
              if half == 0:
                  self.act(lambda e, ps=ps, oc=oc: e.activation(out=self.YB[:, oc, :], in_=ps[:], func=AF.Copy),
                           r=[pk], w=[("YB", oc)])
              else:
                  self.dve(lambda e, ps=ps, oc=oc: e.tensor_tensor(out=self.YB[:, oc, :], in0=self.YB[:, oc, :], in1=ps[:], op=ALU.add),
                           r=[pk, ("YB", oc)], w=[("YB", oc)])

    def mixer(self, i, tb):
        if i == 0:
            self.conv_mixer(tb)
        elif i == 3:
            self.sgu_mixer(tb)
        elif i == 2:
            self.ret_mixer(tb)
        elif i == 1:
            self.gla_mixer(tb)
        else:
            raise NotImplementedError

    def ysc(self, off, n, dt=None):
        v = self.YB[:, :, :].rearrange("p c t -> p (c t)")[:, off:off + n]
        if dt is not None:
            v = v.bitcast(dt)
        return v

    def out_proj32(self, mat, src, skeys):
        for g in range(8):
            pss = [self.psum() for _ in range(2)]
            for kt in range(2):
                wv, wk = self.wload(mat, g * 2 + kt)
                for jj in range(2):
                    ps, pk = pss[jj]
                    for kc in range(KC):
                        self.pe(lambda e, ps=ps, wv=wv, kc=kc, jj=jj, kt=kt: e.matmul(
                            ps[:], wv[:, kc, jj * 128:(jj + 1) * 128], src[:, kt * 16 + kc, :],
                            start=(kt == 0 and kc == 0), stop=(kt == 1 and kc == KC - 1)), r=[wk] + skeys[kt * 16:(kt + 1) * 16], w=[pk])
            for jj in range(2):
                ps, pk = pss[jj]
                oc = 2 * g + jj
                self.act(lambda e, ps=ps, oc=oc: e.activation(out=self.YB[:, oc, :], in_=ps[:], func=AF.Copy), r=[pk], w=[("YB", oc)])

    def ret_mixer(self, tb):
        H = 8
        lg = [float(np.log1p(-np.exp2(-5.0 - h))) for h in range(H)]
        STATE = self.LS[:, :].rearrange("p (j v) -> p j v", v=512)
        OALL = self.UB
        COS = self.ysc(0, 512)
        SIN = self.ysc(512, 512)
        QQ = self.ysc(1024, 512, BF16).rearrange("p (c t) -> p c t", t=TB)
        KK = self.ysc(1536, 512, BF16).rearrange("p (c t) -> p c t", t=TB)
        KT = self.ysc(2048, 512, BF16).rearrange("p (g d) -> p g d", d=256)
        VT = self.ysc(2560, 1024, BF16).rearrange("p (g v) -> p g v", v=512)
        SG = self.ysc(3584, 1024, BF16).rearrange("p (c t) -> p c t", t=TB)
        SB = self.ysc(4608, 512, BF16).rearrange("p (c v) -> p c v", v=512)
        HC = [self.ysc(5120 + i * 1152, 1152) for i in range(2)]
        SC = [self.ysc(7424 + i * 256, 256, BF16) for i in range(2)]
        hk = [("HB", c) for c in range(KC)]
        self.dma("sp", COS, self.rot_cos[:, tb * TB:(tb + 1) * TB], w=["COS"])
        self.dma("sp", SIN, self.rot_sin[:, tb * TB:(tb + 1) * TB], w=["SIN"])

        def proj_fm(wv, wk, j, bank):
            ps, pk = self.psum(bank)
            for kc in range(KC):
                self.pe(lambda e, ps=ps, wv=wv, kc=kc, j=j: e.matmul(ps[:], wv[:, kc, j * 128:(j + 1) * 128], self.HB[:, kc, :],
                                                                    start=(kc == 0), stop=(kc == KC - 1)), r=[wk] + hk, w=[pk])
            return ps, pk

        def rotary(p1, k1, p2, k2, G, gkey, DST, dkey):
            a, ak = self.ring("t32")
            b, bk = self.ring("t32")
            c, ck = self.ring("acc")
            d, dk_ = self.ring("acc")
            self.dve(lambda e: e.tensor_tensor(out=a[:], in0=p1[:], in1=COS, op=ALU.mult), r=[k1, "COS"], w=[ak])
            self.dve(lambda e: e.tensor_tensor(out=b[:], in0=p2[:], in1=SIN, op=ALU.mult), r=[k2, "SIN"], w=[bk])
            self.dve(lambda e: e.tensor_tensor(out=c[:], in0=p1[:], in1=SIN, op=ALU.mult), r=[k1, "SIN"], w=[ck])
            self.dve(lambda e: e.tensor_tensor(out=d[:], in0=p2[:], in1=COS, op=ALU.mult), r=[k2, "COS"], w=[dk_])
            self.dve(lambda e: e.tensor_tensor(out=a[:], in0=a[:], in1=b[:], op=ALU.subtract), r=[ak, bk], w=[ak])
            self.dve(lambda e: e.tensor_tensor(out=c[:], in0=c[:], in1=d[:], op=ALU.add), r=[ck, dk_], w=[ck])
            self.dve(lambda e: e.tensor_tensor(out=DST[:, 0, :], in0=a[:], in1=G, op=ALU.mult), r=[ak, gkey], w=[(dkey, 0)])
            self.dve(lambda e: e.tensor_tensor(out=DST[:, 1, :], in0=c[:], in1=G, op=ALU.mult), r=[ck, gkey], w=[(dkey, 1)])

        for h in range(H):
            g1 = float(np.exp(lg[h]))
            g511 = float(np.exp(511.0 * lg[h]))
            g512 = float(np.exp(512.0 * lg[h]))
            hc = HC[h % 2]
            hck = ("HC", h % 2)
            self.dma("sp", hc, self.ret_hc[h], w=[hck])
            GQ2, GK2, DM2 = hc[:, 0:512], hc[:, 512:1024], hc[:, 1024:1152]
            wq, wqk = self.wload("retin", h)
            p1, k1 = proj_fm(wq, wqk, 0, 4)
            p2, k2 = proj_fm(wq, wqk, 1, 5)
            rotary(p1, k1, p2, k2, GQ2, hck, QQ, "QQ")
            wkk, wkkk = self.wload("retin", 8 + h)
            p1, k1 = proj_fm(wkk, wkkk, 0, 6)
            p2, k2 = proj_fm(wkk, wkkk, 1, 7)
            rotary(p1, k1, p2, k2, GK2, hck, KK, "KK")
            pt, ptk = self.psum(4)
            ptb = pt[:].bitcast(BF16)
            for tt in range(4):
                for dc in range(2):
                    self.pe(lambda e, tt=tt, dc=dc, ptb=ptb: e.transpose(ptb[:, (tt * 2 + dc) * 128:(tt * 2 + dc + 1) * 128],
                                                                         KK[:, dc, tt * 128:(tt + 1) * 128], self.IDENT[:]),
                            r=[("KK", dc), "IDENT"], w=[ptk])
            self.act(lambda e, ptb=ptb: e.activation(out=KT.rearrange("p g d -> p (g d)"), in_=ptb, func=AF.Copy), r=[ptk], w=["KT"])
            for half in range(2):
                wv, wvk = self.wload("retin", 16 + 2 * h + half)
                for tt in range(4):
                    ps, pk = self.psum(5 + (tt % 2))
                    for kc in range(KC):
                        self.pe(lambda e, ps=ps, wv=wv, kc=kc, tt=tt: e.matmul(ps[:, 0:256], self.HB[:, kc, tt * 128:(tt + 1) * 128], wv[:, kc, :],
                                                                               start=(kc == 0), stop=(kc == KC - 1)), r=[wvk] + hk, w=[pk])
                    self.act(lambda e, ps=ps, tt=tt, half=half: e.activation(out=VT[:, tt, half * 256:(half + 1) * 256], in_=ps[:, 0:256], func=AF.Copy),
                             r=[pk], w=[("VT", tt)])
            for half in range(2):
                wg, wgk = self.wload("retin", 32 + 2 * h + half)
                for j in range(2):
                    ps, pk = proj_fm(wg, wgk, j, 6 + j)
                    vc = half * 2 + j
                    self.act(lambda e, ps=ps, vc=vc: e.activation(out=SG[:, vc, :], in_=ps[:], func=AF.Silu), r=[pk], w=[("SG", vc)])
            for dc in range(2):
                self.act(lambda e, dc=dc, h=h, g1=g1: e.activation(out=SB[:, dc, :], in_=STATE[:, h * 2 + dc, :], func=AF.Copy, scale=g1),
                         r=[("STATE", h * 2 + dc)], w=[("SB", dc)])
            po = [self.psum(vc) for vc in range(4)]
            for vc in range(4):
                ps, pk = po[vc]
                for dc in range(2):
                    self.pe(lambda e, ps=ps, vc=vc, dc=dc: e.matmul(ps[:], SB[:, dc, vc * 128:(vc + 1) * 128], QQ[:, dc, :],
                                                                    start=(dc == 0), stop=False), r=[("SB", dc), ("QQ", dc)], w=[pk])
            for j in range(4):
                pss, psk = self.psum(4 + j % 2)
                n = TB - 128 * j
                for dc in range(2):
                    self.pe(lambda e, pss=pss, dc=dc, j=j, n=n: e.matmul(pss[:, 0:n], KK[:, dc, 128 * j:128 * j + 128], QQ[:, dc, 128 * j:TB],
                                                                        start=(dc == 0), stop=(dc == 1)), r=[("KK", dc), ("QQ", dc)], w=[psk])
                sc, sck = SC[j % 2], ("SC", j % 2)
                self.dve(lambda e, pss=pss, sc=sc, DM2=DM2: e.tensor_tensor(out=sc[:, 0:128], in0=pss[:, 0:128], in1=DM2, op=ALU.mult),
                         r=[psk, hck], w=[(sck, 0)])
                if n > 128:
                    self.dve(lambda e, pss=pss, sc=sc, n=n: e.tensor_copy(out=sc[:, 128:n], in_=pss[:, 128:n]), r=[psk], w=[(sck, 1)])
                for vc in range(4):
                    ps, pk = po[vc]
                    self.pe(lambda e, ps=ps, vc=vc, j=j, n=n, sc=sc: e.matmul(ps[:, 128 * j:TB], VT[:, j, vc * 128:(vc + 1) * 128], sc[:, 0:n],
                                                                             start=False, stop=(j == 3)), r=[("VT", j), (sck, 0), (sck, 1)], w=[pk])
            for dc in range(2):
                ps, pk = self.psum(6 + dc)
                for j in range(4):
                    self.pe(lambda e, ps=ps, dc=dc, j=j: e.matmul(ps[:], KT[:, j, dc * 128:(dc + 1) * 128], VT[:, j, :],
                                                                  start=(j == 0), stop=(j == 3)), r=["KT", ("VT", j)], w=[pk])
                sk = ("STATE", h * 2 + dc)
                self.dve(lambda e, dc=dc, h=h, g512=g512: e.tensor_scalar(out=STATE[:, h * 2 + dc, :], in0=STATE[:, h * 2 + dc, :], scalar1=g512, scalar2=None,
                                                                          op0=ALU.mult), r=[sk], w=[sk])
                self.dve(lambda e, ps=ps, dc=dc, h=h, g511=g511: e.scalar_tensor_tensor(out=STATE[:, h * 2 + dc, :], in0=ps[:], scalar=g511, in1=STATE[:, h * 2 + dc, :],
                                                                                       op0=ALU.mult, op1=ALU.add), r=[pk, sk], w=[sk])
            pn, pnk = self.psum(5)
            for vc in range(4):
                ps, pk = po[vc]
                sq, sqk = self.ring("sq")
                self.act(lambda e, ps=ps, sq=sq: e.activation(out=sq[:], in_=ps[:], func=AF.Square), r=[pk], w=[sqk])
                self.pe(lambda e, sq=sq, vc=vc, pn=pn: e.matmul(pn[:], self.ONES[:], sq[:], start=(vc == 0), stop=(vc == 3)), r=[sqk, "ONES"], w=[pnk])
            self.act(lambda e, pn=pn: e.activation(out=self.RSTD[:], in_=pn[:], func=AF.Sqrt, bias=self.EPSC[:, 0:1], scale=1.0 / 512), r=[pnk, "EPSC"], w=["RSTD"])
            self.dve(lambda e: e.reciprocal(out=self.RSTD[:], in_=self.RSTD[:]), r=["RSTD"], w=["RSTD"])
            for vc in range(4):
                ps, pk = po[vc]
                t, tk = self.ring("t32")
                self.dve(lambda e, ps=ps, t=t: e.tensor_tensor(out=t[:], in0=ps[:], in1=self.RSTD[:], op=ALU.mult), r=[pk, "RSTD"], w=[tk])
                self.dve(lambda e, t=t, vc=vc, h=h: e.tensor_tensor(out=OALL[:, h * 4 + vc, :], in0=t[:], in1=SG[:, vc, :], op=ALU.mult),
                         r=[tk, ("SG", vc)], w=[("UB", h * 4 + vc)])
        self.out_proj32("retout", OALL, [("UB", c) for c in range(32)])

    def gla_setup(self):
        WG = self.LS[:, 4096:4608].bitcast(BF16)
        self.pool(lambda e: e.memset(self.LS[:, 0:4096], 0.0), w=[("STATE", j) for j in range(8)])
        tmp = self.YB[:, 14:16, :].rearrange("p c t -> p (c t)")
        yk = [("YB", 14), ("YB", 15)]
        self.dma("sp", tmp[0:32, :], self.gla_wg, w=yk)
        self.act(lambda e: e.activation(out=WG[0:32, :], in_=tmp[0:32, :], func=AF.Copy), r=yk, w=["WG"])

    def gla_mixer(self, tb):
        H = 4
        STATE = self.LS[:, 0:4096].rearrange("p (j v) -> p j v", v=512)
        WG = self.LS[:, 4096:4608].bitcast(BF16)
        OALL = self.UB
        mx = self.MX
        VT = mx[:, 4096:5120].bitcast(BF16).rearrange("p (g v) -> p g v", v=512)
        SG = mx[:, 5120:6144].bitcast(BF16).rearrange("p (c t) -> p c t", t=TB)
        KS = mx[:, 6144:6656].bitcast(BF16).rearrange("p (g d) -> p g d", d=256)
        SB = mx[:, 6656:7168].bitcast(BF16).rearrange("p (c v) -> p c v", v=512)
        GLOW = self.ysc(0, 256, BF16)
        LA = self.ysc(256, 1024).rearrange("p (g f) -> p g f", f=256)
        EP = self.ysc(1280, 1024).rearrange("p (c t) -> p c t", t=TB)
        EN = self.ysc(2304, 1024).rearrange("p (c t) -> p c t", t=TB)
        E2 = self.ysc(3328, 1024).rearrange("p (g f) -> p g f", f=256)
        QF = self.ysc(4352, 512, BF16).rearrange("p (c t) -> p c t", t=TB)
        QB = self.ysc(4864, 512, BF16).rearrange("p (c t) -> p c t", t=TB)
        KF = self.ysc(5376, 512, BF16).rearrange("p (c t) -> p c t", t=TB)
        KB = self.ysc(5888, 512, BF16).rearrange("p (c t) -> p c t", t=TB)
        GC = self.ysc(6400, 384)
        TRI, TRI2, UM, LM = GC[:, 0:128], GC[:, 128:256], GC[:, 256:320], GC[:, 320:384]
        SCT = self.ysc(6784, 32, BF16)
        T1 = self.ysc(6816, 64)
        T2 = self.ysc(6880, 64)
        hk = [("HB", c) for c in range(KC)]
        self.dma("sp", GC, self.gla_c, w=["GC"])
        self.pool(lambda e: e.memset(GLOW[0:32, :], 1.0), w=["GLOW"])
        wgl, wglk = self.wload("glag", 0)
        pg, pgk = self.psum(7)
        for kc in range(KC):
            self.pe(lambda e, kc=kc: e.matmul(pg[0:16, :], wgl[:, kc, :], self.HB[:, kc, :], start=(kc == 0), stop=(kc == KC - 1)),
                    r=[wglk] + hk, w=[pgk])
        self.act(lambda e: e.activation(out=GLOW[0:16, :], in_=pg[0:16, :], func=AF.Copy), r=[pgk, "GLOW"], w=["GLOW"])

        def proj_fm(wv, wk, j, bank):
            ps, pk = self.psum(bank)
            for kc in range(KC):
                self.pe(lambda e, ps=ps, wv=wv, kc=kc, j=j: e.matmul(ps[:], wv[:, kc, j * 128:(j + 1) * 128], self.HB[:, kc, :],
                                                                    start=(kc == 0), stop=(kc == KC - 1)), r=[wk] + hk, w=[pk])
            return ps, pk

        for h in range(H):
            for tt in range(4):
                ps, pk = self.psum(5 + tt % 2)
                self.pe(lambda e, ps=ps, tt=tt, h=h: e.matmul(ps[:, 0:256], GLOW[0:17, tt * 128:(tt + 1) * 128], WG[0:17, h * 256:(h + 1) * 256],
                                                              start=True, stop=True), r=["GLOW", "WG"], w=[pk])
                self.act(lambda e, ps=ps, tt=tt: e.activation(out=LA[:, tt, :], in_=ps[:, 0:256], func=AF.Exp, scale=-1.0), r=[pk], w=[("LA", tt)])
                self.act(lambda e, tt=tt: e.activation(out=LA[:, tt, :], in_=LA[:, tt, :], func=AF.Ln, bias=1.0), r=[("LA", tt)], w=[("LA", tt)])
            for dc in range(2):
                ps, pk = self.psum(5 + dc)
                for tt in range(4):
                    self.pe(lambda e, ps=ps, tt=tt, dc=dc: e.matmul(ps[:, tt * 128:(tt + 1) * 128], LA[:, tt, dc * 128:(dc + 1) * 128], TRI,
                                                                    start=True, stop=True), r=[("LA", tt), "GC"], w=[pk])
                self.act(lambda e, ps=ps, dc=dc: e.activation(out=EP[:, dc, :], in_=ps[:], func=AF.Exp), r=[pk], w=[("EP", dc)])
                self.act(lambda e, ps=ps, dc=dc: e.activation(out=EN[:, dc, :], in_=ps[:], func=AF.Exp, scale=-1.0), r=[pk], w=[("EN", dc)])
            for tt in range(4):
                ps, pk = self.psum(7)
                self.pe(lambda e, ps=ps, tt=tt: e.matmul(ps[:, 0:256], TRI2, LA[:, tt, :], start=True, stop=True), r=[("LA", tt), "GC"], w=[pk])
                self.act(lambda e, ps=ps, tt=tt: e.activation(out=E2[:, tt, :], in_=ps[:, 0:256], func=AF.Exp), r=[pk], w=[("E2", tt)])
            wq, wqk = self.wload("glain", h)
            for dc in range(2):
                ps, pk = proj_fm(wq, wqk, dc, 5 + dc)
                self.dve(lambda e, ps=ps, dc=dc: e.scalar_tensor_tensor(out=QF[:, dc, :], in0=ps[:], scalar=0.0625, in1=EP[:, dc, :], op0=ALU.mult, op1=ALU.mult),
                         r=[pk, ("EP", dc)], w=[("QF", dc)])
                self.dve(lambda e, ps=ps, dc=dc: e.scalar_tensor_tensor(out=QB[:, dc, :], in0=ps[:], scalar=0.0625, in1=EN[:, dc, :], op0=ALU.mult, op1=ALU.mult),
                         r=[pk, ("EN", dc)], w=[("QB", dc)])
            wkk, wkkk = self.wload("glain", 4 + h)
            for dc in range(2):
                ps, pk = proj_fm(wkk, wkkk, dc, 5 + dc)
                self.dve(lambda e, ps=ps, dc=dc: e.tensor_tensor(out=KF[:, dc, :], in0=ps[:], in1=EN[:, dc, :], op=ALU.mult), r=[pk, ("EN", dc)], w=[("KF", dc)])
                self.dve(lambda e, ps=ps, dc=dc: e.tensor_tensor(out=KB[:, dc, :], in0=ps[:], in1=EP[:, dc, :], op=ALU.mult), r=[pk, ("EP", dc)], w=[("KB", dc)])
            for tt in range(4):
                ps, pk = self.psum(5 + tt % 2)
                for kc in range(KC):
                    self.pe(lambda e, ps=ps, kc=kc, tt=tt, wkk=wkk: e.matmul(ps[:, 0:256], self.HB[:, kc, tt * 128:(tt + 1) * 128], wkk[:, kc, :],
                                                                    start=(kc == 0), stop=(kc == KC - 1)), r=[wkkk] + hk, w=[pk])
                self.dve(lambda e, ps=ps, tt=tt: e.tensor_tensor(out=KS[:, tt, :], in0=ps[:, 0:256], in1=E2[:, tt, :], op=ALU.mult), r=[pk, ("E2", tt)], w=[("KS", tt)])
            for half in range(2):
                wv, wvk = self.wload("glain", 8 + 2 * h + half)
                for tt in range(4):
                    ps, pk = self.psum(5 + tt % 2)
                    for kc in range(KC):
                        self.pe(lambda e, ps=ps, wv=wv, kc=kc, tt=tt: e.matmul(ps[:, 0:256], self.HB[:, kc, tt * 128:(tt + 1) * 128], wv[:, kc, :],
                                                                               start=(kc == 0), stop=(kc == KC - 1)), r=[wvk] + hk, w=[pk])
                    self.act(lambda e, ps=ps, tt=tt, half=half: e.activation(out=VT[:, tt, half * 256:(half + 1) * 256], in_=ps[:, 0:256], func=AF.Copy),
                             r=[pk], w=[("VT", tt)])
            for half in range(2):
                wr, wrk = self.wload("glain", 16 + 2 * h + half)
                for j in range(2):
                    ps, pk = proj_fm(wr, wrk, j, 5 + j)
                    vc = half * 2 + j
                    self.act(lambda e, ps=ps, vc=vc: e.activation(out=SG[:, vc, :], in_=ps[:], func=AF.Silu), r=[pk], w=[("SG", vc)])
            for dc in range(2):
                self.act(lambda e, dc=dc, h=h: e.activation(out=SB[:, dc, :], in_=STATE[:, h * 2 + dc, :], func=AF.Copy),
                         r=[("STATE", h * 2 + dc)], w=[("SB", dc)])
            if h == 0 and tb == 0:
                self.dbg("GLOW", GLOW[0:32, :], ["GLOW"], BF16)
                self.dbg("LA", LA, [("LA", t_) for t_ in range(4)])
                self.dbg("EP", EP, [("EP", 0), ("EP", 1)])
                self.dbg("EN", EN, [("EN", 0), ("EN", 1)])
                self.dbg("E2", E2, [("E2", t_) for t_ in range(4)])
                self.dbg("QF", QF, [("QF", 0), ("QF", 1)], BF16)
                self.dbg("KB", KB, [("KB", 0), ("KB", 1)], BF16)
                self.dbg("KS", KS, [("KS", t_) for t_ in range(4)], BF16)
                self.dbg("VT", VT, [("VT", t_) for t_ in range(4)], BF16)
                self.dbg("SG", SG, [("SG", t_) for t_ in range(4)], BF16)
            po = [self.psum(vc) for vc in range(4)]
            for n in range(8):
                tt, r0 = n // 2, (n % 2) * 64
                pss, psk = self.psum(4)
                cs = slice(n * 64, (n + 1) * 64)
                for dc in range(2):
                    self.pe(lambda e, dc=dc, cs=cs, r0=r0, pss=pss: e.matmul(pss[r0:r0 + 64, 0:64], KF[:, dc, cs], QF[:, dc, cs],
                                                                             start=(dc == 0), stop=(dc == 1)), r=[("KF", dc), ("QF", dc)], w=[psk])
                for dc in range(2):
                    self.pe(lambda e, dc=dc, cs=cs, r0=r0, pss=pss: e.matmul(pss[r0:r0 + 64, 64:128], KB[:, dc, cs], QB[:, dc, cs],
                                                                             start=(dc == 0), stop=(dc == 1)), r=[("KB", dc), ("QB", dc)], w=[psk])
                self.dve(lambda e, r0=r0, pss=pss: e.tensor_tensor(out=T1[r0:r0 + 64, :], in0=pss[r0:r0 + 64, 0:64], in1=UM[r0:r0 + 64, :], op=ALU.mult),
                         r=[psk, "GC"], w=["T1"])
                self.dve(lambda e, r0=r0, pss=pss: e.tensor_tensor(out=T2[r0:r0 + 64, :], in0=pss[r0:r0 + 64, 64:128], in1=LM[r0:r0 + 64, :], op=ALU.mult),
                         r=[psk, "GC"], w=["T2"])
                self.dve(lambda e, r0=r0: e.tensor_tensor(out=SCT[r0:r0 + 64, :], in0=T1[r0:r0 + 64, :], in1=T2[r0:r0 + 64, :], op=ALU.add),
                         r=["T1", "T2"], w=["SCT"])
                for vc in range(4):
                    ps, pk = po[vc]
                    self.pe(lambda e, ps=ps, vc=vc, cs=cs, tt=tt, r0=r0: e.matmul(ps[:, cs], VT[r0:r0 + 64, tt, vc * 128:(vc + 1) * 128], SCT[r0:r0 + 64, :],
                                                                                  start=True, stop=False), r=[("VT", tt), "SCT"], w=[pk])
                    for dc in range(2):
                        self.pe(lambda e, ps=ps, vc=vc, cs=cs, dc=dc: e.matmul(ps[:, cs], SB[:, dc, vc * 128:(vc + 1) * 128], QF[:, dc, cs],
                                                                               start=False, stop=(dc == 1)), r=[("SB", dc), ("QF", dc)], w=[pk])
                for dc in range(2):
                    ps, pk = self.psum(5 + dc)
                    self.pe(lambda e, ps=ps, dc=dc, tt=tt, r0=r0: e.matmul(ps[:], KS[r0:r0 + 64, tt, dc * 128:(dc + 1) * 128], VT[r0:r0 + 64, tt, :],
                                                                           start=True, stop=True), r=[("KS", tt), ("VT", tt)], w=[pk])
                    sk = ("STATE", h * 2 + dc)
                    self.dve(lambda e, ps=ps, dc=dc, h=h, n=n: e.scalar_tensor_tensor(
                        out=STATE[:, h * 2 + dc, :], in0=STATE[:, h * 2 + dc, :], scalar=EP[:, dc, n * 64 + 63:n * 64 + 64], in1=ps[:],
                        op0=ALU.mult, op1=ALU.add), r=[pk, sk, ("EP", dc)], w=[sk])
                    if n < 7:
                        self.act(lambda e, dc=dc, h=h: e.activation(out=SB[:, dc, :], in_=STATE[:, h * 2 + dc, :], func=AF.Copy), r=[sk], w=[("SB", dc)])
            pn, pnk = self.psum(7)
            for vc in range(4):
                ps, pk = po[vc]
                sq, sqk = self.ring("sq")
                self.act(lambda e, ps=ps, sq=sq: e.activation(out=sq[:], in_=ps[:], func=AF.Square), r=[pk], w=[sqk])
                self.pe(lambda e, sq=sq, vc=vc: e.matmul(pn[:], self.ONES[:], sq[:], start=(vc == 0), stop=(vc == 3)), r=[sqk, "ONES"], w=[pnk])
            self.act(lambda e: e.activation(out=self.RSTD[:], in_=pn[:], func=AF.Sqrt, bias=self.EPSC[:, 0:1], scale=1.0 / 512), r=[pnk, "EPSC"], w=["RSTD"])
            self.dve(lambda e: e.reciprocal(out=self.RSTD[:], in_=self.RSTD[:]), r=["RSTD"], w=["RSTD"])
            ng = self.vec("glng")
            for vc in range(4):
                ps, pk = po[vc]
                t, tk = self.ring("t32")
                self.dve(lambda e, ps=ps, t=t: e.tensor_tensor(out=t[:], in0=ps[:], in1=self.RSTD[:], op=ALU.mult), r=[pk, "RSTD"], w=[tk])
                self.dve(lambda e, t=t, vc=vc, h=h: e.scalar_tensor_tensor(out=OALL[:, h * 4 + vc, :], in0=t[:], scalar=ng[:, vc:vc + 1], in1=SG[:, vc, :],
                                                                           op0=ALU.mult, op1=ALU.mult), r=[tk, ("SG", vc), "VEC"], w=[("UB", h * 4 + vc)])
        if tb == 0:
            self.dbg("OALL", OALL[:, 0:16, :], [("UB", c) for c in range(16)], BF16)
        ok_ = [("UB", c) for c in range(KC)]
        for p in range(8):
            wv, wk = self.wload("glaout", p)
            for j in range(2):
                oc = 2 * p + j
                ps, pk = self.psum(5 + j)
                for kc in range(KC):
                    self.pe(lambda e, ps=ps, wv=wv, kc=kc, j=j: e.matmul(ps[:], wv[:, kc, j * 128:(j + 1) * 128], OALL[:, kc, :],
                                                                        start=(kc == 0), stop=(kc == KC - 1)), r=[wk] + ok_, w=[pk])
                self.act(lambda e, ps=ps, oc=oc: e.activation(out=self.YB[:, oc, :], in_=ps[:], func=AF.Copy), r=[pk], w=[("YB", oc)])

    def sgu_views(self):
        RWB = self.LS[:, 0:1024].rearrange("p (h i) -> p h i", i=128)
        BSB = self.LS[:, 1024:2048].rearrange("p (h i) -> p h i", i=128)
        WmT = self.LS[:, 2048:2560].bitcast(BF16).rearrange("p (h i) -> p h i", i=128)
        return RWB, BSB, WmT

    def sgu_setup(self):
        RWB, BSB, WmT = self.sgu_views()
        WS = self.YB[:, 0:2, :].rearrange("p c t -> p (c t)").rearrange("p (h i) -> p h i", i=128)
        yk = [("YB", 0), ("YB", 1)]
        self.dma("sp", self.YB[:, 0:2, :].rearrange("p c t -> p (c t)"), self.sgu_wsT, w=yk)
        self.dma("sp", self.LS[:, 1024:2048], self.sgu_bsb, w=["BSB"])
        self.pool(lambda e: e.memset(WS[64:128, :, 0:64], 0.0), r=yk, w=yk)
        self.act(lambda e: e.activation(out=WmT, in_=WS, func=AF.Copy), r=yk, w=["WmT"])
        for half in range(2):
            ps, pk = self.psum()
            self.pe(lambda e, ps=ps, half=half: e.matmul(ps[:], self.ONES[:], WmT[:, half * 4:(half + 1) * 4, :], start=True, stop=True),
                    r=["WmT", "ONES"], w=[pk])
            self.act(lambda e, ps=ps, half=half: e.activation(out=RWB[:, half * 4:(half + 1) * 4, :], in_=ps[:].rearrange("p (h i) -> p h i", i=128),
                                                              func=AF.Copy), r=[pk], w=["RWB"])

    def sgu_mixer(self, tb):
        RWB, BSB, WmT = self.sgu_views()
        U = self.UB
        Vb = self.YB[:, :, :].rearrange("p c t -> p (c t)").bitcast(BF16).rearrange("p (g f) -> p g f", f=4096)
        S1 = self.ST[:, 0:64].rearrange("p (g j) -> p g j", j=16)
        S2 = self.ST[:, 64:128].rearrange("p (g j) -> p g j", j=16)
        M1 = self.ST[:, 128:132]
        M2 = self.ST[:, 132:136]
        RS = self.ST[:, 136:140]
        NM = self.ST[:, 140:144]
        hk = [("HB", c) for c in range(KC)]

        def vbk(tt, j):
            return ("YB", 4 * tt + j // 4)
        for j in range(16):
            wv, wk = self.wload("sguin", 16 + j)
            for tt in range(4):
                ps, pk = self.psum()
                for kc in range(KC):
                    self.pe(lambda e, ps=ps, wv=wv, kc=kc, tt=tt: e.matmul(ps[:, 0:256], self.HB[:, kc, tt * 128:(tt + 1) * 128], wv[:, kc, :],
                                                                           start=(kc == 0), stop=(kc == KC - 1)), r=[wk] + hk, w=[pk])
                self.act(lambda e, ps=ps, tt=tt, j=j: e.activation(out=Vb[:, tt, j * 256:(j + 1) * 256], in_=ps[:, 0:256], func=AF.Gelu,
                                                                   accum_out=S1[:, tt, j:j + 1]), r=[pk], w=[vbk(tt, j), ("S1", tt, j)])
                jk, jkk = self.ring("sq")
                self.act(lambda e, jk=jk, tt=tt, j=j: e.activation(out=jk[:, 0:256], in_=Vb[:, tt, j * 256:(j + 1) * 256], func=AF.Square,
                                                                   accum_out=S2[:, tt, j:j + 1]), r=[vbk(tt, j)], w=[jkk, ("S2", tt, j)])
        for j in range(16):
            wu, wk = self.wload("sguin", j)
            for jj in range(2):
                c = 2 * j + jj
                ps, pk = self.psum()
                for kc in range(KC):
                    self.pe(lambda e, ps=ps, wu=wu, kc=kc, jj=jj: e.matmul(ps[:], wu[:, kc, jj * 128:(jj + 1) * 128], self.HB[:, kc, :],
                                                                           start=(kc == 0), stop=(kc == KC - 1)), r=[wk] + hk, w=[pk])
                self.act(lambda e, ps=ps, c=c: e.activation(out=U[:, c, :], in_=ps[:], func=AF.Gelu), r=[pk], w=[("UB", c)])
        s1k = [("S1", tt, j) for tt in range(4) for j in range(16)]
        s2k = [("S2", tt, j) for tt in range(4) for j in range(16)]
        X = mybir.AxisListType.X
        self.dve(lambda e: e.tensor_reduce(out=M1, in_=S1, axis=X, op=ALU.add), r=s1k, w=["M1"])
        self.dve(lambda e: e.tensor_reduce(out=M2, in_=S2, axis=X, op=ALU.add), r=s2k, w=["M2"])
        self.dve(lambda e: e.tensor_scalar(out=M1, in0=M1, scalar1=1.0 / 4096, scalar2=None, op0=ALU.mult), r=["M1"], w=["M1"])
        self.dve(lambda e: e.tensor_tensor(out=NM, in0=M1, in1=M1, op=ALU.mult), r=["M1"], w=["NM"])
        self.dve(lambda e: e.scalar_tensor_tensor(out=M2, in0=M2, scalar=1.0 / 4096, in1=NM, op0=ALU.mult, op1=ALU.subtract),
                 r=["M2", "NM"], w=["M2"])
        self.act(lambda e: e.activation(out=RS, in_=M2, func=AF.Sqrt, bias=self.EPSC[:, 0:1], scale=1.0), r=["M2", "EPSC"], w=["RS"])
        self.dve(lambda e: e.reciprocal(out=RS, in_=RS), r=["RS"], w=["RS"])
        self.dve(lambda e: e.scalar_tensor_tensor(out=NM, in0=M1, scalar=-1.0, in1=RS, op0=ALU.mult, op1=ALU.mult), r=["M1", "RS"], w=["NM"])
        for tt in range(4):
            ks = [("YB", 4 * tt + q) for q in range(4)]
            self.dve(lambda e, tt=tt: e.tensor_scalar(out=Vb[:, tt, :], in0=Vb[:, tt, :], scalar1=RS[:, tt:tt + 1], scalar2=NM[:, tt:tt + 1],
                                                      op0=ALU.mult, op1=ALU.add), r=ks + ["RS", "NM"], w=ks)
        lg, lb = self.vec("sglg"), self.vec("sglb")
        for c in range(32):
            h = c // 4
            ps, pk = self.psum()
            for tt in range(4):
                self.pe(lambda e, ps=ps, tt=tt, c=c, h=h: e.matmul(ps[:, tt * 128:(tt + 1) * 128], Vb[:, tt, c * 128:(c + 1) * 128], WmT[:, h, :],
                                                                    start=True, stop=True), r=[("YB", 4 * tt + c // 8), "WmT"], w=[pk])
            cb, cbk = self.ring("t32")
            self.dve(lambda e, cb=cb, c=c, h=h: e.scalar_tensor_tensor(out=cb[:, 0:128], in0=RWB[:, h, :], scalar=lb[:, c:c + 1], in1=BSB[:, h, :],
                                                                       op0=ALU.mult, op1=ALU.add), r=["RWB", "BSB", "VEC"], w=[cbk])
            sv, svk = self.ring("acc")
            self.dve(lambda e, cb=cb, sv=sv, ps=ps, c=c: e.scalar_tensor_tensor(
                out=sv[:].rearrange("p (g i) -> p g i", i=128), in0=ps[:].rearrange("p (g i) -> p g i", i=128), scalar=lg[:, c:c + 1],
                in1=cb[:, 0:128].unsqueeze(1).to_broadcast([128, 4, 128]), op0=ALU.mult, op1=ALU.add), r=[pk, cbk, "VEC"], w=[svk])
            self.dve(lambda e, sv=sv, c=c: e.tensor_tensor(out=U[:, c, :], in0=U[:, c, :], in1=sv[:], op=ALU.mult),
                     r=[svk, ("UB", c)], w=[("UB", c)])
        uk = [("UB", c) for c in range(32)]
        for g in range(8):
            pss = [self.psum() for _ in range(2)]
            for kt in range(2):
                wv, wk = self.wload("sguout", g * 2 + kt)
                for jj in range(2):
                    ps, pk = pss[jj]
                    for kc in range(KC):
                        self.pe(lambda e, ps=ps, wv=wv, kc=kc, jj=jj, kt=kt: e.matmul(
                            ps[:], wv[:, kc, jj * 128:(jj + 1) * 128], U[:, kt * 16 + kc, :],
                            start=(kt == 0 and kc == 0), stop=(kt == 1 and kc == KC - 1)), r=[wk] + uk[kt * 16:(kt + 1) * 16], w=[pk])
            for jj in range(2):
                ps, pk = pss[jj]
                oc = 2 * g + jj
                self.act(lambda e, ps=ps, oc=oc: e.activation(out=self.YB[:, oc, :], in_=ps[:], func=AF.Copy), r=[pk], w=[("YB", oc)])

    def ln_stats(self, src, skey, nch, nfeat):
        ps1, pk1 = self.psum()
        ps2, pk2 = self.psum()
        for c in range(nch):
            zb, zk = self.ring("sq")
            sq, sk = self.ring("sq")
            self.act(lambda e, zb=zb, c=c: e.activation(out=zb[:], in_=src[:, c, :], func=AF.Copy), r=[(skey, c)], w=[zk])
            self.act(lambda e, sq=sq, c=c: e.activation(out=sq[:], in_=src[:, c, :], func=AF.Square), r=[(skey, c)], w=[sk])
            self.pe(lambda e, zb=zb, c=c: e.matmul(ps1[:], self.ONES[:], zb[:], start=(c == 0), stop=(c == nch - 1)),
                    r=[zk, "ONES"], w=[pk1])
            self.pe(lambda e, sq=sq, c=c: e.matmul(ps2[:], self.ONES[:], sq[:], start=(c == 0), stop=(c == nch - 1)),
                    r=[sk, "ONES"], w=[pk2])
        MU, RS = self.MU, self.RSTD
        self.act(lambda e: e.activation(out=MU, in_=ps1[:], func=AF.Copy, scale=1.0 / nfeat), r=[pk1], w=["MU"])
        t, tk = self.ring("t32")
        self.dve(lambda e, t=t: e.tensor_tensor(out=t[:], in0=MU, in1=MU, op=ALU.mult), r=["MU"], w=[tk])
        self.dve(lambda e, t=t: e.scalar_tensor_tensor(out=t[:], in0=ps2[:], scalar=1.0 / nfeat, in1=t[:],
                                                       op0=ALU.mult, op1=ALU.subtract), r=[pk2, tk], w=[tk])
        self.act(lambda e, t=t: e.activation(out=RS[:], in_=t[:], func=AF.Sqrt, bias=self.EPSC[:, 0:1], scale=1.0),
                 r=[tk, "EPSC"], w=["RSTD"])
        self.dve(lambda e: e.reciprocal(out=RS[:], in_=RS[:]), r=["RSTD"], w=["RSTD"])

    def conv_setup(self):
        self.specs["cmdg"] = dict(nk=31, fw=128, E=31 * 128, nt=16)
        self.wb["cmdg"] = self.dram("cmdg_b", [16, 128, 31 * 128], BF16, "Internal")
        for fc in range(KC):
            slot = self.wslot_i % NWSLOT
            self.wslot_i += 1
            buf = self.WR[slot]
            key = ("WR", slot)
            for k in range(31):
                self.dve(lambda e, buf=buf, k=k, fc=fc: e.tensor_scalar(out=buf[:, k * 128:(k + 1) * 128], in0=self.IDENT[:],
                                                                        scalar1=self.vec(f"cmw{k}")[:, fc:fc + 1], scalar2=None, op0=ALU.mult),
                         r=["IDENT", "VEC"], w=[key])
            self.dma("sp", self.wb["cmdg"][fc], buf[:, 0:31 * 128], r=[key], w=[("wb", "cmdg", 0)])

    def conv_mixer(self, tb):
        W_ = TB + 30
        GL = self.MX[:, 0:KC * W_ // 2].bitcast(BF16).rearrange("p (c t) -> p c t", t=W_)
        CT = self.LS[:, 0:KC * 15].bitcast(BF16).rearrange("p (c t) -> p c t", t=30)
        hk = [("HB", c) for c in range(KC)]
        for p in range(8):
            wa, wak = self.wload("cmin", p)
            wg, wgk = self.wload("cmin", 8 + p)
            for j in range(2):
                fc = 2 * p + j
                pa, pak = self.psum()
                pg, pgk = self.psum()
                for kc in range(KC):
                    self.pe(lambda e, pa=pa, j=j, kc=kc, wa=wa: e.matmul(pa[:], wa[:, kc, j * 128:(j + 1) * 128], self.HB[:, kc, :],
                                                                        start=(kc == 0), stop=(kc == KC - 1)), r=[wak] + hk, w=[pak])
                for kc in range(KC):
                    self.pe(lambda e, pg=pg, j=j, kc=kc, wg=wg: e.matmul(pg[:], wg[:, kc, j * 128:(j + 1) * 128], self.HB[:, kc, :],
                                                                        start=(kc == 0), stop=(kc == KC - 1)), r=[wgk] + hk, w=[pgk])
                sg, sgk = self.ring("t32")
                self.act(lambda e, sg=sg, pg=pg: e.activation(out=sg[:], in_=pg[:], func=AF.Sigmoid), r=[pgk], w=[sgk])
                if tb == 0:
                    self.pool(lambda e, fc=fc: e.memset(GL[:, fc, 0:30], 0.0), w=[("GLt", fc)])
                else:
                    self.pool(lambda e, fc=fc: e.tensor_copy(out=GL[:, fc, 0:30], in_=CT[:, fc, :]),
                              r=[("CT", fc)], w=[("GLt", fc)])
                self.dve(lambda e, fc=fc, pa=pa, sg=sg: e.tensor_tensor(out=GL[:, fc, 30:TB + 30], in0=pa[:], in1=sg[:], op=ALU.mult),
                         r=[pak, sgk, ("GLt", fc)], w=[("GL", fc)])
                self.pool(lambda e, fc=fc: e.tensor_copy(out=CT[:, fc, :], in_=GL[:, fc, TB:TB + 30]),
                          r=[("GL", fc)], w=[("CT", fc)])
        for fc in range(KC):
            wd, wdk = self.wload("cmdg", fc)
            ps, pk = self.psum()
            for k in range(31):
                self.pe(lambda e, ps=ps, wd=wd, k=k, fc=fc: e.matmul(ps[:], wd[:, k, :], GL[:, fc, k:k + TB], start=(k == 0), stop=(k == 30)),
                        r=[wdk, ("GL", fc), ("GLt", fc)], w=[pk])
            self.act(lambda e, ps=ps, fc=fc: e.activation(out=self.YB[:, fc, :], in_=ps[:], func=AF.Identity,
                                                          bias=self.vec("cmb")[:, fc:fc + 1], scale=1.0), r=[pk, "VEC"], w=[("YB", fc)])
        self.ln_stats(self.YB, "YB", KC, D)
        lg, lb = self.vec("cmlg"), self.vec("cmlb")
        for c in range(KC):
            t, tk = self.ring("t32")
            self.dve(lambda e, t=t, c=c: e.tensor_tensor(out=t[:], in0=self.YB[:, c, :], in1=self.MU, op=ALU.subtract),
                     r=[("YB", c), "MU"], w=[tk])
            self.dve(lambda e, t=t: e.tensor_tensor(out=t[:], in0=t[:], in1=self.RSTD[:], op=ALU.mult), r=[tk, "RSTD"], w=[tk])
            self.act(lambda e, t=t, c=c: e.activation(out=self.HB[:, c, :], in_=t[:], func=AF.Silu,
                                                      scale=lg[:, c:c + 1], bias=lb[:, c:c + 1]), r=[tk, "VEC"], w=[("HB", c)])
        for p in range(8):
            wv, wk = self.wload("cmout", p)
            for j in range(2):
                oc = 2 * p + j
                ps, pk = self.psum()
                for kc in range(KC):
                    self.pe(lambda e, ps=ps, wv=wv, kc=kc, j=j: e.matmul(ps[:], wv[:, kc, j * 128:(j + 1) * 128], self.HB[:, kc, :],
                                                                        start=(kc == 0), stop=(kc == KC - 1)), r=[wk] + hk, w=[pk])
                self.act(lambda e, ps=ps, oc=oc: e.activation(out=self.YB[:, oc, :], in_=ps[:], func=AF.Copy),
                         r=[pk], w=[("YB", oc)])


def build_program(cfg, vec_off, nvec):
    b = Builder(cfg, vec_off, nvec)
    with b.st:
        b.EPSC = b.sb("EPSC", [128, 1], F32)
        b.S.add("dve", lambda e: e.memset(b.EPSC[:], EPS), (), ["EPSC"])
        nc = b.build()
    return nc


FULL_CFG = dict(layers=[0, 1, 2, 3], ntb=NTB, mixer=True)


def run(inp, cfg, core_ids=None, trace=False):
    shared, percore, voff = prep_inputs(inp, cfg)
    nvec = shared["vecs"].shape[1]
    nc = build_program(cfg, voff, nvec)
    if core_ids is None:
        core_ids = list(range(NB))
    in_maps = []
    for b in core_ids:
        m = dict(shared)
        m.update(percore[b])
        in_maps.append(m)
    res = run_bass_kernel_spmd(nc, in_maps, core_ids=list(range(len(core_ids))), trace=trace)
    return res


def kernel(**inp):
    inp = {k: np.asarray(v) for k, v in inp.items()}
    res = run(inp, FULL_CFG)
    out = np.stack([np.ascontiguousarray(res.results[b]["outT"].T) for b in range(NB)], 0)
    return out.astype(np.float32)
```

```python
import contextlib
import numpy as np
import concourse.bass as bass
import concourse.mybir as mybir
from concourse.bass_utils import run_bass_kernel_spmd

F32 = mybir.dt.float32
BF16 = mybir.dt.bfloat16
AF = mybir.ActivationFunctionType
ALU = mybir.AluOpType

D = 2048
L_SEQ = 4096
NB = 8
DEPTH = 4
TB = 512
NTB = L_SEQ // TB
KC = D // 128
FFH = 5632
FFC = FFH // 128
EPS = 1e-6
WSLOT = 4096
NWSLOT = 4
STRICT = True


class _Op:
    __slots__ = ("idx", "eng", "fn", "deps", "dma", "slot", "val", "sig", "cnt")


class Sched:
    ENGS = ("pe", "act", "dve", "pool", "sp")
    NSLOT = 12
    EPOCH = 30000

    def __init__(self):
        self.ops = []
        self.streams = {e: [] for e in self.ENGS}
        self.lastw = {}
        self.rdc = {}
        self.rdd = {}
        self.dcount = {e: 0 for e in self.ENGS}
        self.slot_last = {}

    def add(self, eng, fn, reads=(), writes=(), dma=False):
        op = _Op()
        op.idx = len(self.ops)
        op.eng = eng
        op.fn = fn
        op.dma = dma
        op.sig = False
        op.cnt = 0
        op.slot = -1
        op.val = 0
        deps = set()
        for k in reads:
            w = self.lastw.get(k)
            if w is not None:
                deps.add(w)
        for k in writes:
            w = self.lastw.get(k)
            if w is not None:
                deps.add(w)
            r = self.rdc.get(k)
            if r:
                deps.update(r.values())
            r = self.rdd.get(k)
            if r:
                deps.update(r)
        if dma:
            n = self.dcount[eng]
            self.dcount[eng] = n + 1
            op.slot = n % self.NSLOT
            op.val = 16 * (n // self.NSLOT + 1)
            prev = self.slot_last.get((eng, op.slot))
            if prev is not None:
                deps.add(prev)
            self.slot_last[(eng, op.slot)] = op.idx
        for k in reads:
            if dma:
                self.rdd.setdefault(k, []).append(op.idx)
            else:
                self.rdc.setdefault(k, {})[eng] = op.idx
        for k in writes:
            self.lastw[k] = op.idx
            self.rdc[k] = {}
            self.rdd[k] = []
        deps.discard(op.idx)
        op.deps = deps
        self.ops.append(op)
        self.streams[eng].append(op)
        return op

    def _needs_wait(self, op, d):
        if d.dma or op.dma:
            return True
        if d.eng == op.eng:
            if op.eng == "pe":
                return False
            return STRICT
        return True

    def finalize(self):
        for op in self.ops:
            for di in op.deps:
                d = self.ops[di]
                if not d.dma and self._needs_wait(op, d):
                    d.sig = True
        self.nepoch = {}
        for e in self.ENGS:
            c = 0
            for op in self.streams[e]:
                if op.sig and not op.dma:
                    c += 1
                    op.cnt = c
            self.nepoch[e] = max(1, (c + self.EPOCH - 1) // self.EPOCH)

    def emit(self, nc):
        self.finalize()
        with contextlib.ExitStack() as st:
            csem = {}
            for e in self.ENGS:
                csem[e] = [st.enter_context(nc.semaphore(f"c_{e}_{i}")) for i in range(self.nepoch[e])]
            dsem = {}
            for e in self.ENGS:
                if self.dcount[e]:
                    dsem[e] = [st.enter_context(nc.semaphore(f"d_{e}_{i}")) for i in range(self.NSLOT)]
            block = st.enter_context(nc.Block())
            ops = self.ops
            EP = self.EPOCH

            def run(ename, eng):
                known_c = {}
                known_d = {}
                for op in self.streams[ename]:
                    need_c = {}
                    need_d = {}
                    for di in op.deps:
                        d = ops[di]
                        if not self._needs_wait(op, d):
                            continue
                        if d.dma:
                            k = (d.eng, d.slot)
                            if need_d.get(k, 0) < d.val:
                                need_d[k] = d.val
                        else:
                            if need_c.get(d.eng, 0) < d.cnt:
                                need_c[d.eng] = d.cnt
                    for e2, c in need_c.items():
                        if known_c.get(e2, 0) >= c:
                            continue
                        known_c[e2] = c
                        eng.wait_ge(csem[e2][(c - 1) // EP], (c - 1) % EP + 1)
                    for k, v in need_d.items():
                        if known_d.get(k, 0) >= v:
                            continue
                        known_d[k] = v
                        eng.wait_ge(dsem[k[0]][k[1]], v)
                    ins = op.fn(eng)
                    if op.dma:
                        ins.then_inc(dsem[ename][op.slot], 16)
                    elif op.sig:
                        c = op.cnt
                        ins.then_inc(csem[ename][(c - 1) // EP], 1)

            @block.tensor
            def _(e):
                run("pe", e)

            @block.scalar
            def _(e):
                run("act", e)

            @block.vector
            def _(e):
                run("dve", e)

            @block.gpsimd
            def _(e):
                run("pool", e)

            @block.sync
            def _(e):
                run("sp", e)


def _tile_weight(W, nk, fw, col_groups=None):
    K, F = W.shape
    nkt = K // (128 * nk)
    if col_groups is None:
        col_groups = [(c, fw) for c in range(0, F, fw)]
    tiles = []
    for (c0, w) in col_groups:
        blk = W[:, c0:c0 + w]
        if w < fw:
            blk = np.concatenate([blk, np.zeros((K, fw - w), W.dtype)], axis=1)
        for kt in range(nkt):
            t = blk[kt * nk * 128:(kt + 1) * nk * 128].reshape(nk, 128, fw).transpose(1, 0, 2)
            tiles.append(t.reshape(128, nk * fw))
    return np.ascontiguousarray(np.stack(tiles, 0))


def _pvec(v):
    v = np.asarray(v, np.float32).reshape(-1, 128)
    return np.ascontiguousarray(v.T)


def _ret_consts():
    f32 = np.float32
    half = 128
    inv_freq = (np.float32(10000.0) ** (-np.arange(half, dtype=f32) / f32(half))).astype(f32)
    ang = (np.arange(L_SEQ, dtype=f32)[None, :] * inv_freq[:, None]).astype(f32)
    H = 8
    lg = np.log1p(-np.exp2(-5.0 - np.arange(H, dtype=np.float64)))
    tl = np.arange(TB, dtype=np.float64)
    gq = np.exp(lg[:, None] * tl[None, :])
    gk = np.exp(-lg[:, None] * tl[None, :]) * (256.0 ** -0.5)
    s_ = np.arange(128)[:, None]
    t_ = np.arange(128)[None, :]
    cs, ct = s_ // 64, t_ // 64
    dm = np.zeros((H, 128, 128))
    for h in range(H):
        same = np.where(s_ <= t_, 1.0, np.exp(lg[h] * 2.0 * (s_ - t_)))
        dm[h] = np.where(cs < ct, 1.0, np.where(cs > ct, 0.0, same))
    hc = np.zeros((H, 128, 1152), f32)
    hc[:, :, 0:512] = gq[:, None, :]
    hc[:, :, 512:1024] = gk[:, None, :]
    hc[:, :, 1024:1152] = dm
    return {
        "rot_cos": np.ascontiguousarray(np.cos(ang).astype(f32)),
        "rot_sin": np.ascontiguousarray(np.sin(ang).astype(f32)),
        "ret_hc": np.ascontiguousarray(hc),
        "ident": np.eye(128, dtype=f32),
    }


def _gla_consts():
    f32 = np.float32
    s_ = np.arange(128)[:, None]
    t_ = np.arange(128)[None, :]
    same = (s_ // 64) == (t_ // 64)
    tri = np.where(same & (s_ <= t_), -1.0 / 16.0, 0.0).astype(f32)
    tri2 = np.where(same & (s_ > t_), -1.0 / 16.0, 0.0).astype(f32)
    sl = (np.arange(128) % 64)[:, None]
    tl = np.arange(64)[None, :]
    um = (sl <= tl).astype(f32)
    lm = (sl > tl).astype(f32)
    return np.ascontiguousarray(np.concatenate([tri, tri2, um, lm], axis=1))


class _VecPack:
    def __init__(self):
        self.cols = []
        self.off = {}
        self.n = 0

    def add(self, name, arr):
        self.off[name] = (self.n, arr.shape[1])
        self.cols.append(arr)
        self.n += arr.shape[1]

    def build(self):
        return np.ascontiguousarray(np.concatenate(self.cols, axis=1).astype(np.float32))


def weight_specs():
    specs = {}
    for i in range(DEPTH):
        specs[f"ada{i}"] = dict(K=D, F=6 * D, nk=16, fw=256)
        specs[f"ffin{i}"] = dict(K=D, F=2 * FFH, nk=16, fw=256)
        specs[f"ffout{i}"] = dict(K=FFH, F=D, nk=22, fw=128)
    specs["cmin"] = dict(K=D, F=2 * D, nk=16, fw=256)
    specs["cmout"] = dict(K=D, F=D, nk=16, fw=256)
    specs["sguin"] = dict(K=D, F=4 * D, nk=16, fw=256)
    specs["retin"] = dict(K=D, F=6 * D, nk=16, fw=256)
    specs["glain"] = dict(K=D, F=6144, nk=16, fw=256)
    specs["glag"] = dict(K=D, F=16, nk=16, fw=16)
    specs["glaout"] = dict(K=D, F=D, nk=16, fw=256)
    specs["retout"] = dict(K=2 * D, F=D, nk=16, fw=256)
    specs["sguout"] = dict(K=2 * D, F=D, nk=16, fw=256)
    for s in specs.values():
        s["nkt"] = s["K"] // (128 * s["nk"])
        s["ng"] = s["F"] // s["fw"]
        s["nt"] = s["nkt"] * s["ng"]
        s["E"] = s["nk"] * s["fw"]
    return specs


def prep_inputs(inp, cfg):
    shared = {}
    for i in cfg["layers"]:
        shared[f"ada{i}"] = _tile_weight(inp["ada_w"][i], 16, 256)
        shared[f"ffin{i}"] = _tile_weight(inp["ffn_w_in"][i], 16, 256)
        shared[f"ffout{i}"] = _tile_weight(inp["ffn_w_out"][i], 22, 128)
    if 0 in cfg["layers"] and cfg.get("mixer", True):
        shared["cmin"] = _tile_weight(inp["cm_w_in"][0], 16, 256)
        shared["cmout"] = _tile_weight(inp["cm_w_out"][0], 16, 256)
    if 3 in cfg["layers"] and cfg.get("mixer", True):
        shared["sguin"] = _tile_weight(inp["sgu_w_in"][0], 16, 256)
        shared["sguout"] = _tile_weight(inp["sgu_w_out"][0], 16, 256)
    if 2 in cfg["layers"] and cfg.get("mixer", True):
        shared["retin"] = _tile_weight(inp["ret_w_in"][0], 16, 256)
        shared["retout"] = _tile_weight(inp["ret_w_out"][0], 16, 256)
    shared.update(_ret_consts())
    if 1 in cfg["layers"] and cfg.get("mixer", True):
        shared["glain"] = _tile_weight(inp["gla_w_in"][0][:, :6144], 16, 256)
        shared["glag"] = _tile_weight(inp["gla_w_in"][0][:, 6144:6160], 16, 16)
        shared["glaout"] = _tile_weight(inp["gla_w_out"][0], 16, 256)
    wg = np.zeros((32, 1024), np.float32)
    wg[0:16] = inp["gla_w_gate"][0]
    wg[16] = inp["gla_b_gate"][0]
    shared["gla_wg"] = wg
    shared["gla_c"] = _gla_consts()
    shared["sgu_wsT"] = np.ascontiguousarray(inp["sgu_w_s"][0].transpose(2, 0, 1).reshape(128, 1024).astype(np.float32))
    shared["sgu_bsb"] = np.ascontiguousarray(np.broadcast_to(inp["sgu_b_s"][0].reshape(1, 1024), (128, 1024)).astype(np.float32))
    vp = _VecPack()
    for i in range(DEPTH):
        vp.add(f"adab{i}", _pvec(inp["ada_b"][i]))
        for j in range(4):
            vp.add(f"ng{i}_{j}", _pvec(inp["norm_g"][i, j]))
        for k in range(3):
            vp.add(f"fcw{i}_{k}", _pvec(inp["ffn_conv_w"][i, k]))
        vp.add(f"fcb{i}", _pvec(inp["ffn_conv_b"][i]))
    for k in range(31):
        vp.add(f"cmw{k}", _pvec(inp["cm_dw_w"][0, k]))
    vp.add("cmb", _pvec(inp["cm_dw_b"][0]))
    vp.add("cmlg", _pvec(inp["cm_ln_g"][0]))
    vp.add("cmlb", _pvec(inp["cm_ln_b"][0]))
    vp.add("sglg", _pvec(inp["sgu_ln_g"][0]))
    vp.add("glng", _pvec(inp["gla_norm_g"][0]))
    vp.add("sglb", _pvec(inp["sgu_ln_b"][0]))
    vecs_common = vp
    percore = []
    for b in range(NB):
        d = {}
        d["xT"] = np.ascontiguousarray(inp["x"][b].T)
        d["cvec"] = _pvec(inp["c"][b])
        percore.append(d)
    shared["vecs"] = vp.build()
    return shared, percore, vp.off


class Builder:
    def __init__(self, cfg, vec_off, nvec):
        self.cfg = cfg
        self.voff = vec_off
        self.nvec = nvec
        self.nc = bass.Bass("TRN2", target_bir_lowering=False)
        self.S = Sched()
        self.st = contextlib.ExitStack()
        self.specs = weight_specs()
        self.wslot_i = 0
        self.dbg_done = set()
        self.ps_i = 0
        self.uid = 0

    def sb(self, name, shape, dt):
        return self.st.enter_context(self.nc.sbuf_tensor(name, list(shape), dt))

    def dram(self, name, shape, dt, kind):
        return self.nc.dram_tensor(name, list(shape), dt, kind=kind).ap()

    def pe(self, fn, r=(), w=()):
        return self.S.add("pe", fn, r, w)

    def act(self, fn, r=(), w=()):
        return self.S.add("act", fn, r, w)

    def dve(self, fn, r=(), w=()):
        return self.S.add("dve", fn, r, w)

    def pool(self, fn, r=(), w=()):
        return self.S.add("pool", fn, r, w)

    def dma(self, q, out, in_, r=(), w=()):
        return self.S.add(q, lambda e: e.dma_start(out=out, in_=in_), r, w, dma=True)

    def dbg(self, name, ap, keys, dt=F32):
        if not self.cfg.get("debug") or name in self.dbg_done:
            return
        self.dbg_done.add(name)
        d = self.dram("dbg_" + name, list(ap.shape), dt, "ExternalOutput")
        self.dma("pool", d, ap, r=list(keys), w=[("dbg", name)])

    def psum(self, i=None):
        if i is None:
            i = self.ps_i % 8
            self.ps_i += 1
        return self.PS[i], ("PS", i)

    def ring(self, name):
        bufs, st = self.rings[name]
        i = st[0] % len(bufs)
        st[0] += 1
        return bufs[i], (name, i)

    def mkring(self, name, n, shape, dt):
        self.rings[name] = ([self.sb(f"{name}{i}", shape, dt) for i in range(n)], [0])

    def wload(self, mat, t):
        sp = self.specs[mat]
        slot = self.wslot_i % NWSLOT
        self.wslot_i += 1
        buf = self.WR[slot]
        E = sp["E"]
        src = self.wb[mat][t]
        key = ("WR", slot)
        wkey = ("wb", mat, 0 if mat == "cmdg" else t // 16)
        assert wkey in self.S.lastw, wkey
        self.dma("sp", buf[:, 0:E], src, r=[wkey], w=[key])
        view = buf[:, 0:E].rearrange("p (k f) -> p k f", f=sp["fw"])
        return view, key

    def cast_jobs(self, mat):
        sp = self.specs[mat]
        return [(mat, g) for g in range(0, sp["nt"], 16)]

    def cast_job(self, job):
        mat, g = job
        sp = self.specs[mat]
        nt, E = sp["nt"], sp["E"]
        a = 1
        while E // a > 2048 or E % a:
            a += 1
        src = self.wf[mat].rearrange("t p (a b) -> t p a b", a=a)
        dst = self.wb[mat].rearrange("t p (a b) -> t p a b", a=a)
        n = min(16, nt - g)
        self.dma("pool", dst[g:g + n], src[g:g + n], w=[("wb", mat, g // 16)])

    def vec(self, name, c0=0, n=None):
        o, w = self.voff[name]
        if n is None:
            n = w - c0
        return self.VEC[:, o + c0:o + c0 + n]

    def build(self):
        nc, cfg = self.nc, self.cfg
        layers = cfg["layers"]
        ntb = cfg.get("ntb", NTB)
        self.ntb = ntb
        self.xT = self.dram("xT", [D, L_SEQ], F32, "ExternalInput")
        self.cvec = self.dram("cvec", [128, KC], F32, "ExternalInput")
        self.vecs = self.dram("vecs", [128, self.nvec], F32, "ExternalInput")
        self.outT = self.dram("outT", [D, L_SEQ], F32, "ExternalOutput")
        self.xs = [self.dram(f"xs{i}", [D, L_SEQ], F32, "Internal") for i in range(2)]
        self.sgu_wsT = self.dram("sgu_wsT", [128, 1024], F32, "ExternalInput")
        self.rot_cos = self.dram("rot_cos", [128, L_SEQ], F32, "ExternalInput")
        self.gla_wg = self.dram("gla_wg", [32, 1024], F32, "ExternalInput")
        self.gla_c = self.dram("gla_c", [128, 384], F32, "ExternalInput")
        self.rot_sin = self.dram("rot_sin", [128, L_SEQ], F32, "ExternalInput")
        self.ret_hc = self.dram("ret_hc", [8, 128, 1152], F32, "ExternalInput")
        self.ident_d = self.dram("ident", [128, 128], F32, "ExternalInput")
        self.sgu_bsb = self.dram("sgu_bsb", [128, 1024], F32, "ExternalInput")
        self.wf, self.wb = {}, {}
        mats = []
        self.layer_mats = {}
        for i in layers:
            lm = [f"ada{i}"]
            if cfg.get("mixer", True):
                lm += {0: ["cmin", "cmout"], 1: ["glag", "glain", "glaout"], 2: ["retin", "retout"], 3: ["sguin", "sguout"]}[i]
            lm += [f"ffin{i}", f"ffout{i}"]
            self.layer_mats[i] = lm
            mats += lm
        for m in mats:
            sp = self.specs[m]
            self.wf[m] = self.dram(m, [sp["nt"], 128, sp["E"]], F32, "ExternalInput")
            self.wb[m] = self.dram(m + "_b", [sp["nt"], 128, sp["E"]], BF16, "Internal")
        self.rings = {}
        self.VEC = self.sb("VEC", [128, self.nvec], F32)
        self.CV = self.sb("CV", [128, KC], F32)
        self.CA = self.sb("CA", [128, KC], BF16)
        self.MODS = [self.sb(f"MOD{i}", [128, 96], F32) for i in range(2)]
        self.PV = self.sb("PV", [128, 96], F32)
        self.ONES = self.sb("ONES", [128, 128], BF16)
        self.IDENT = self.sb("IDENT", [128, 128], BF16)
        self.XB = self.sb("XB", [128, KC, TB], F32)
        self.YB = self.sb("YB", [128, KC, TB], F32)
        self.HB = self.sb("HB", [128, KC, TB], BF16)
        self.MX = self.sb("MX", [128, 32 * TB // 2], F32)
        self.UB = self.MX[:, :].bitcast(BF16).rearrange("p (c t) -> p c t", t=TB)
        self.WR = [self.sb(f"WR{i}", [128, WSLOT], BF16) for i in range(NWSLOT)]
        self.RSTD = self.sb("RSTD", [128, TB], F32)
        self.LS = self.sb("LS", [128, 8192], F32)
        self.CTAIL = self.LS[:, 0:KC * 30].rearrange("p (c t) -> p c t", t=30)
        self.MU = self.LS[:, 512:1024]
        self.ST = self.sb("ST", [128, 256], F32)
        self.FTAIL = self.sb("FTAIL", [128, FFC, 2], F32)
        self.mkring("sq", 2, [128, TB], BF16)
        self.mkring("t32", 2, [128, TB], F32)
        self.mkring("ab", 2, [128, TB + 2], F32)
        self.mkring("acc", 2, [128, TB], F32)
        self.mkring("so", 2, [128, TB], F32)
        self.PS = [self.st.enter_context(nc.psum_tensor(f"PS{i}", [128, TB], F32)) for i in range(8)]

        for m in self.layer_mats[layers[0]]:
            for job in self.cast_jobs(m):
                self.cast_job(job)
        self.dma("sp", self.VEC[:], self.vecs, w=["VEC"])
        self.dma("sp", self.CV[:], self.cvec, w=["CV"])
        self.dve(lambda e: e.memset(self.ONES[:], 1.0), w=["ONES"])
        it, itk = self.ring("t32")
        self.dma("sp", it[:, 0:128], self.ident_d, w=[itk])
        self.act(lambda e: e.activation(out=self.IDENT[:], in_=it[:, 0:128], func=AF.Copy), r=[itk], w=["IDENT"])
        self.act(lambda e: e.activation(out=self.CA[:], in_=self.CV[:], func=AF.Silu), r=["CV"], w=["CA"])

        xsrc = self.xT
        for g in range(0, 48, 6):
            self.ada_part(layers[0], g, g + 6)
        for li, i in enumerate(layers):
            last = li == len(layers) - 1
            xdst = self.outT if last else self.xs[li % 2]
            self.ada_fin(i)
            if cfg.get("mixer", True) and i == 0:
                self.conv_setup()
            if cfg.get("mixer", True) and i == 3:
                self.sgu_setup()
            if cfg.get("mixer", True) and i == 1:
                self.gla_setup()
            if cfg.get("mixer", True) and i == 2:
                self.pool(lambda e: e.memset(self.LS[:], 0.0), w=[("STATE", j) for j in range(16)])
            self.dve(lambda e: e.memset(self.FTAIL[:], 0.0), w=[("FT", fc) for fc in range(FFC)])
            njobs = []
            if not last:
                for m in self.layer_mats[layers[li + 1]]:
                    njobs += self.cast_jobs(m)
            per = -(-len(njobs) // ntb) if njobs else 0
            for tb in range(ntb):
                for job in njobs[tb * per:(tb + 1) * per]:
                    self.cast_job(job)
                self.load_x(xsrc, tb)
                if cfg.get("mixer", True):
                    self.prenorm(0)
                    self.mixer(i, tb)
                    self.postnorm(1, None, tb)
                self.prenorm(3)
                self.ffn(i, tb)
                self.postnorm(4, xdst, tb)
                if not last:
                    npart = 48 // ntb if 48 % ntb == 0 else 48
                    if npart == 48:
                        if tb == 0:
                            for g in range(0, 48, 6):
                                self.ada_part(layers[li + 1], g, g + 6)
                    else:
                        for g in range(tb * npart, (tb + 1) * npart, 6):
                            self.ada_part(layers[li + 1], g, min(g + 6, (tb + 1) * npart))
            xsrc = xdst
        self.S.add("sp", lambda e: e.nop(), [("xd", id(xdst), tb, c) for tb in range(ntb) for c in range(KC)]
                   + [("dbg", n) for n in self.dbg_done], [])
        self.S.emit(nc)
        return nc

    def ada_part(self, i, g0, g1):
        ps, pk = self.psum()
        mat = f"ada{i}"
        M = self.MODS[i % 2]
        for g in range(g0, g1):
            wv, wk = self.wload(mat, g)
            for j in range(2):
                col = g * 2 + j
                for kc in range(KC):
                    self.pe(lambda e, wv=wv, j=j, kc=kc, col=col, ps=ps: e.matmul(
                        ps[:, col:col + 1], wv[:, kc, j * 128:(j + 1) * 128], self.CA[:, kc:kc + 1],
                        start=(kc == 0), stop=(kc == KC - 1)),
                        r=[wk, "CA"], w=[pk])
        c0, c1 = 2 * g0, 2 * g1
        self.dve(lambda e, ps=ps: e.tensor_tensor(out=M[:, c0:c1], in0=ps[:, c0:c1], in1=self.vec(f"adab{i}")[:, c0:c1], op=ALU.add),
                 r=[pk, "VEC"], w=[("MOD", i % 2, g0)])

    def ada_fin(self, i):
        M, P = self.MODS[i % 2], self.PV
        mk = [("MOD", i % 2, g) for g in range(0, 48, 6)]

        def sl(t, j):
            return t[:, j * 16:(j + 1) * 16]
        self.dve(lambda e: e.scalar_tensor_tensor(out=sl(P, 0), in0=sl(M, 1), scalar=1.0, in1=self.vec(f"ng{i}_0"),
                                                  op0=ALU.add, op1=ALU.mult), r=mk + ["VEC"], w=["PV0"])
        self.dve(lambda e: e.tensor_copy(out=sl(P, 1), in_=sl(M, 0)), r=mk, w=["PV1"])
        self.dve(lambda e: e.tensor_tensor(out=sl(P, 2), in0=sl(M, 2), in1=self.vec(f"ng{i}_1"), op=ALU.mult),
                 r=mk + ["VEC"], w=["PV2"])
        self.dve(lambda e: e.scalar_tensor_tensor(out=sl(P, 3), in0=sl(M, 4), scalar=1.0, in1=self.vec(f"ng{i}_2"),
                                                  op0=ALU.add, op1=ALU.mult), r=mk + ["VEC"], w=["PV3"])
        self.dve(lambda e: e.tensor_copy(out=sl(P, 4), in_=sl(M, 3)), r=mk, w=["PV4"])
        self.dve(lambda e: e.tensor_tensor(out=sl(P, 5), in0=sl(M, 5), in1=self.vec(f"ng{i}_3"), op=ALU.mult),
                 r=mk + ["VEC"], w=["PV5"])

    def load_x(self, xsrc, tb):
        src = xsrc.rearrange("(c p) t -> p c t", p=128)
        for q in range(4):
            self.dma("sp", self.XB[:, q * 4:(q + 1) * 4, :], src[:, q * 4:(q + 1) * 4, tb * TB:(tb + 1) * TB],
                     r=[("xd", id(xsrc), tb, c) for c in range(q * 4, q * 4 + 4)],
                     w=[("XB", c) for c in range(q * 4, q * 4 + 4)])

    def rstd_from(self, srcbuf, skey):
        ps, pk = self.psum()
        for c in range(KC):
            sq, sk = self.ring("sq")
            if c % 2 == 0:
                self.act(lambda e, sq=sq, c=c: e.activation(out=sq[:], in_=srcbuf[:, c, :], func=AF.Square),
                         r=[(skey, c)], w=[sk])
            else:
                self.dve(lambda e, sq=sq, c=c: e.tensor_tensor(out=sq[:], in0=srcbuf[:, c, :], in1=srcbuf[:, c, :], op=ALU.mult),
                         r=[(skey, c)], w=[sk])
            self.pe(lambda e, sq=sq, c=c, ps=ps: e.matmul(ps[:], self.ONES[:], sq[:], start=(c == 0), stop=(c == KC - 1)),
                    r=[sk, "ONES"], w=[pk])
        self.act(lambda e, ps=ps: e.activation(out=self.RSTD[:], in_=ps[:], func=AF.Sqrt, bias=self.EPSC[:, 0:1], scale=1.0 / D),
                 r=[pk, "EPSC"], w=["RSTD"])
        self.dve(lambda e, ps=ps: e.reciprocal(out=ps[:], in_=self.RSTD[:]), r=["RSTD", pk], w=[pk])
        return ps, pk

    def prenorm(self, j):
        ps, pk = self.rstd_from(self.XB, "XB")
        A = self.PV[:, j * 16:(j + 1) * 16]
        Bv = self.PV[:, (j + 1) * 16:(j + 2) * 16]
        for c in range(KC):
            t, tk = self.ring("t32")
            self.dve(lambda e, t=t, c=c, ps=ps: e.scalar_tensor_tensor(out=t[:], in0=self.XB[:, c, :], scalar=A[:, c:c + 1], in1=ps[:],
                                                                       op0=ALU.mult, op1=ALU.mult),
                     r=[("XB", c), pk, f"PV{j}"], w=[tk])
            self.act(lambda e, t=t, c=c: e.activation(out=self.HB[:, c, :], in_=t[:], func=AF.Identity,
                                                      scale=1.0, bias=Bv[:, c:c + 1]),
                     r=[tk, f"PV{j + 1}"], w=[("HB", c)])

    def postnorm(self, j, xdst, tb):
        ps, pk = self.rstd_from(self.YB, "YB")
        G = self.PV[:, (j + 1) * 16:(j + 2) * 16]
        gk = f"PV{j + 1}"
        for c in range(KC):
            t, tk = self.ring("t32")
            self.dve(lambda e, t=t, c=c, ps=ps: e.scalar_tensor_tensor(out=t[:], in0=self.YB[:, c, :], scalar=G[:, c:c + 1], in1=ps[:],
                                                                       op0=ALU.mult, op1=ALU.mult),
                     r=[("YB", c), pk, gk], w=[tk])
            addeng = self.pool if c % 2 == 0 else self.dve
            if xdst is None:
                addeng(lambda e, t=t, c=c: e.tensor_tensor(out=self.XB[:, c, :], in0=t[:], in1=self.XB[:, c, :], op=ALU.add),
                       r=[tk, ("XB", c)], w=[("XB", c)])
            else:
                so, sk = self.ring("so")
                addeng(lambda e, t=t, c=c, so=so: e.tensor_tensor(out=so[:], in0=t[:], in1=self.XB[:, c, :], op=ALU.add),
                       r=[tk, ("XB", c)], w=[sk])
                self.dma("pool", xdst[c * 128:(c + 1) * 128, tb * TB:(tb + 1) * TB], so[:],
                         r=[sk], w=[("xd", id(xdst), tb, c)])

    def ffn(self, i, tb):
        mi, mo = f"ffin{i}", f"ffout{i}"
        hk = [("HB", c) for c in range(KC)]
        for half in range(2):
          for p in range(11 * half, 11 * half + 11):
                wa, wak = self.wload(mi, p)
                wb_, wbk = self.wload(mi, 22 + p)
                chunks = []
                for j in range(2):
                    fc = 2 * p + j
                    pa, pak = self.psum()
                    pb, pbk = self.psum()
                    for kc in range(KC):
                        self.pe(lambda e, pa=pa, j=j, kc=kc, wa=wa: e.matmul(pa[:], wa[:, kc, j * 128:(j + 1) * 128], self.HB[:, kc, :],
                                                                            start=(kc == 0), stop=(kc == KC - 1)),
                                r=[wak] + hk, w=[pak])
                    for kc in range(KC):
                        self.pe(lambda e, pb=pb, j=j, kc=kc, wb_=wb_: e.matmul(pb[:], wb_[:, kc, j * 128:(j + 1) * 128], self.HB[:, kc, :],
                                                                               start=(kc == 0), stop=(kc == KC - 1)),
                                r=[wbk] + hk, w=[pbk])
                    chunks.append((fc, pa, pak, pb, pbk))
                st = []
                for (fc, pa, pak, pb, pbk) in chunks:
                    ab, abk = self.ring("ab")
                    acc, ack = self.ring("acc")
                    self.pool(lambda e, ab=ab, fc=fc: e.tensor_copy(out=ab[:, 0:2], in_=self.FTAIL[:, fc, :]),
                              r=[("FT", fc)], w=[(abk, "t")])
                    self.act(lambda e, ab=ab, pa=pa: e.activation(out=ab[:, 2:TB + 2], in_=pa[:], func=AF.Copy),
                             r=[pak], w=[abk])
                    self.pool(lambda e, ab=ab, fc=fc: e.tensor_copy(out=self.FTAIL[:, fc, :], in_=ab[:, TB:TB + 2]),
                              r=[abk], w=[("FT", fc)])
                    st.append((fc, ab, abk, acc, ack, pb, pbk))
                w0, w1, w2, cb = (self.vec(f"fcw{i}_0"), self.vec(f"fcw{i}_1"), self.vec(f"fcw{i}_2"), self.vec(f"fcb{i}"))
                for (fc, ab, abk, acc, ack, pb, pbk) in st:
                    self.dve(lambda e, ab=ab, acc=acc, fc=fc: e.tensor_scalar(out=acc[:], in0=ab[:, 2:TB + 2], scalar1=w2[:, fc:fc + 1],
                                                                             scalar2=cb[:, fc:fc + 1], op0=ALU.mult, op1=ALU.add),
                             r=[abk, "VEC"], w=[ack])
                for (fc, ab, abk, acc, ack, pb, pbk) in st:
                    self.dve(lambda e, ab=ab, acc=acc, fc=fc: e.scalar_tensor_tensor(out=acc[:], in0=ab[:, 1:TB + 1], scalar=w1[:, fc:fc + 1],
                                                                                    in1=acc[:], op0=ALU.mult, op1=ALU.add),
                             r=[abk, (abk, "t"), ack, "VEC"], w=[ack])
                for (fc, ab, abk, acc, ack, pb, pbk) in st:
                    self.dve(lambda e, ab=ab, acc=acc, fc=fc: e.scalar_tensor_tensor(out=acc[:], in0=ab[:, 0:TB], scalar=w0[:, fc:fc + 1],
                                                                                    in1=acc[:], op0=ALU.mult, op1=ALU.add),
                             r=[abk, (abk, "t"), ack, "VEC"], w=[ack])
                for (fc, ab, abk, acc, ack, pb, pbk) in st:
                    self.act(lambda e, acc=acc: e.activation(out=acc[:], in_=acc[:], func=AF.Silu), r=[ack], w=[ack])
                for (fc, ab, abk, acc, ack, pb, pbk) in st:
                    self.dve(lambda e, acc=acc, pb=pb, fc=fc, half=half: e.tensor_tensor(out=self.UB[:, fc - 22 * half, :], in0=acc[:], in1=pb[:], op=ALU.mult),
                             r=[ack, pbk], w=[("UB", fc - 22 * half)])
          uk = [("UB", c) for c in range(22)]
          for oc in range(KC):
              ps, pk = self.psum()
              wv, wk = self.wload(mo, oc * 2 + half)
              for kc in range(22):
                  self.pe(lambda e, ps=ps, wv=wv, kc=kc: e.matmul(ps[:], wv[:, kc, :], self.UB[:, kc, :],
                                                                 start=(kc == 0), stop=(kc == 21)), r=[wk] + uk, w=[pk])
              if half == 0:
                  self.act(lambda e, ps=ps, oc=oc: e.activation(out=self.YB[:, oc, :], in_=ps[:], func=AF.Copy),
                           r=[pk], w=[("YB", oc)])
              else:
                  self.dve(lambda e, ps=ps, oc=oc: e.tensor_tensor(out=self.YB[:, oc, :], in0=self.YB[:, oc, :], in1=ps[:], op=ALU.add),
                           r=[pk, ("YB", oc)], w=[("YB", oc)])

    def mixer(self, i, tb):
        if i == 0:
            self.conv_mixer(tb)
        elif i == 3:
            self.sgu_mixer(tb)
        elif i == 2:
            self.ret_mixer(tb)
        elif i == 1:
            self.gla_mixer(tb)
        else:
            raise NotImplementedError

    def ysc(self, off, n, dt=None):
        v = self.YB[:, :, :].rearrange("p c t -> p (c t)")[:, off:off + n]
        if dt is not None:
            v = v.bitcast(dt)
        return v

    def out_proj32(self, mat, src, skeys):
        for g in range(8):
            pss = [self.psum() for _ in range(2)]
            for kt in range(2):
                wv, wk = self.wload(mat, g * 2 + kt)
                for jj in range(2):
                    ps, pk = pss[jj]
                    for kc in range(KC):
                        self.pe(lambda e, ps=ps, wv=wv, kc=kc, jj=jj, kt=kt: e.matmul(
                            ps[:], wv[:, kc, jj * 128:(jj + 1) * 128], src[:, kt * 16 + kc, :],
                            start=(kt == 0 and kc == 0), stop=(kt == 1 and kc == KC - 1)), r=[wk] + skeys[kt * 16:(kt + 1) * 16], w=[pk])
            for jj in range(2):
                ps, pk = pss[jj]
                oc = 2 * g + jj
                self.act(lambda e, ps=ps, oc=oc: e.activation(out=self.YB[:, oc, :], in_=ps[:], func=AF.Copy), r=[pk], w=[("YB", oc)])

    def ret_mixer(self, tb):
        H = 8
        lg = [float(np.log1p(-np.exp2(-5.0 - h))) for h in range(H)]
        STATE = self.LS[:, :].rearrange("p (j v) -> p j v", v=512)
        OALL = self.UB
        COS = self.ysc(0, 512)
        SIN = self.ysc(512, 512)
        QQ = self.ysc(1024, 512, BF16).rearrange("p (c t) -> p c t", t=TB)
        KK = self.ysc(1536, 512, BF16).rearrange("p (c t) -> p c t", t=TB)
        KT = self.ysc(2048, 512, BF16).rearrange("p (g d) -> p g d", d=256)
        VT = self.ysc(2560, 1024, BF16).rearrange("p (g v) -> p g v", v=512)
        SG = self.ysc(3584, 1024, BF16).rearrange("p (c t) -> p c t", t=TB)
        SB = self.ysc(4608, 512, BF16).rearrange("p (c v) -> p c v", v=512)
        HC = [self.ysc(5120 + i * 1152, 1152) for i in range(2)]
        SC = [self.ysc(7424 + i * 256, 256, BF16) for i in range(2)]
        hk = [("HB", c) for c in range(KC)]
        self.dma("sp", COS, self.rot_cos[:, tb * TB:(tb + 1) * TB], w=["COS"])
        self.dma("sp", SIN, self.rot_sin[:, tb * TB:(tb + 1) * TB], w=["SIN"])

        def proj_fm(wv, wk, j, bank):
            ps, pk = self.psum(bank)
            for kc in range(KC):
                self.pe(lambda e, ps=ps, wv=wv, kc=kc, j=j: e.matmul(ps[:], wv[:, kc, j * 128:(j + 1) * 128], self.HB[:, kc, :],
                                                                    start=(kc == 0), stop=(kc == KC - 1)), r=[wk] + hk, w=[pk])
            return ps, pk

        def rotary(p1, k1, p2, k2, G, gkey, DST, dkey):
            a, ak = self.ring("t32")
            b, bk = self.ring("t32")
            c, ck = self.ring("acc")
            d, dk_ = self.ring("acc")
            self.dve(lambda e: e.tensor_tensor(out=a[:], in0=p1[:], in1=COS, op=ALU.mult), r=[k1, "COS"], w=[ak])
            self.dve(lambda e: e.tensor_tensor(out=b[:], in0=p2[:], in1=SIN, op=ALU.mult), r=[k2, "SIN"], w=[bk])
            self.dve(lambda e: e.tensor_tensor(out=c[:], in0=p1[:], in1=SIN, op=ALU.mult), r=[k1, "SIN"], w=[ck])
            self.dve(lambda e: e.tensor_tensor(out=d[:], in0=p2[:], in1=COS, op=ALU.mult), r=[k2, "COS"], w=[dk_])
            self.dve(lambda e: e.tensor_tensor(out=a[:], in0=a[:], in1=b[:], op=ALU.subtract), r=[ak, bk], w=[ak])
            self.dve(lambda e: e.tensor_tensor(out=c[:], in0=c[:], in1=d[:], op=ALU.add), r=[ck, dk_], w=[ck])
            self.dve(lambda e: e.tensor_tensor(out=DST[:, 0, :], in0=a[:], in1=G, op=ALU.mult), r=[ak, gkey], w=[(dkey, 0)])
            self.dve(lambda e: e.tensor_tensor(out=DST[:, 1, :], in0=c[:], in1=G, op=ALU.mult), r=[ck, gkey], w=[(dkey, 1)])

        for h in range(H):
            g1 = float(np.exp(lg[h]))
            g511 = float(np.exp(511.0 * lg[h]))
            g512 = float(np.exp(512.0 * lg[h]))
            hc = HC[h % 2]
            hck = ("HC", h % 2)
            self.dma("sp", hc, self.ret_hc[h], w=[hck])
            GQ2, GK2, DM2 = hc[:, 0:512], hc[:, 512:1024], hc[:, 1024:1152]
            wq, wqk = self.wload("retin", h)
            p1, k1 = proj_fm(wq, wqk, 0, 4)
            p2, k2 = proj_fm(wq, wqk, 1, 5)
            rotary(p1, k1, p2, k2, GQ2, hck, QQ, "QQ")
            wkk, wkkk = self.wload("retin", 8 + h)
            p1, k1 = proj_fm(wkk, wkkk, 0, 6)
            p2, k2 = proj_fm(wkk, wkkk, 1, 7)
            rotary(p1, k1, p2, k2, GK2, hck, KK, "KK")
            pt, ptk = self.psum(4)
            ptb = pt[:].bitcast(BF16)
            for tt in range(4):
                for dc in range(2):
                    self.pe(lambda e, tt=tt, dc=dc, ptb=ptb: e.transpose(ptb[:, (tt * 2 + dc) * 128:(tt * 2 + dc + 1) * 128],
                                                                         KK[:, dc, tt * 128:(tt + 1) * 128], self.IDENT[:]),
                            r=[("KK", dc), "IDENT"], w=[ptk])
            self.act(lambda e, ptb=ptb: e.activation(out=KT.rearrange("p g d -> p (g d)"), in_=ptb, func=AF.Copy), r=[ptk], w=["KT"])
            for half in range(2):
                wv, wvk = self.wload("retin", 16 + 2 * h + half)
                for tt in range(4):
                    ps, pk = self.psum(5 + (tt % 2))
                    for kc in range(KC):
                        self.pe(lambda e, ps=ps, wv=wv, kc=kc, tt=tt: e.matmul(ps[:, 0:256], self.HB[:, kc, tt * 128:(tt + 1) * 128], wv[:, kc, :],
                                                                               start=(kc == 0), stop=(kc == KC - 1)), r=[wvk] + hk, w=[pk])
                    self.act(lambda e, ps=ps, tt=tt, half=half: e.activation(out=VT[:, tt, half * 256:(half + 1) * 256], in_=ps[:, 0:256], func=AF.Copy),
                             r=[pk], w=[("VT", tt)])
            for half in range(2):
                wg, wgk = self.wload("retin", 32 + 2 * h + half)
                for j in range(2):
                    ps, pk = proj_fm(wg, wgk, j, 6 + j)
                    vc = half * 2 + j
                    self.act(lambda e, ps=ps, vc=vc: e.activation(out=SG[:, vc, :], in_=ps[:], func=AF.Silu), r=[pk], w=[("SG", vc)])
            for dc in range(2):
                self.act(lambda e, dc=dc, h=h, g1=g1: e.activation(out=SB[:, dc, :], in_=STATE[:, h * 2 + dc, :], func=AF.Copy, scale=g1),
                         r=[("STATE", h * 2 + dc)], w=[("SB", dc)])
            po = [self.psum(vc) for vc in range(4)]
            for vc in range(4):
                ps, pk = po[vc]
                for dc in range(2):
                    self.pe(lambda e, ps=ps, vc=vc, dc=dc: e.matmul(ps[:], SB[:, dc, vc * 128:(vc + 1) * 128], QQ[:, dc, :],
                                                                    start=(dc == 0), stop=False), r=[("SB", dc), ("QQ", dc)], w=[pk])
            for j in range(4):
                pss, psk = self.psum(4 + j % 2)
                n = TB - 128 * j
                for dc in range(2):
                    self.pe(lambda e, pss=pss, dc=dc, j=j, n=n: e.matmul(pss[:, 0:n], KK[:, dc, 128 * j:128 * j + 128], QQ[:, dc, 128 * j:TB],
                                                                        start=(dc == 0), stop=(dc == 1)), r=[("KK", dc), ("QQ", dc)], w=[psk])
                sc, sck = SC[j % 2], ("SC", j % 2)
                self.dve(lambda e, pss=pss, sc=sc, DM2=DM2: e.tensor_tensor(out=sc[:, 0:128], in0=pss[:, 0:128], in1=DM2, op=ALU.mult),
                         r=[psk, hck], w=[(sck, 0)])
                if n > 128:
                    self.dve(lambda e, pss=pss, sc=sc, n=n: e.tensor_copy(out=sc[:, 128:n], in_=pss[:, 128:n]), r=[psk], w=[(sck, 1)])
                for vc in range(4):
                    ps, pk = po[vc]
                    self.pe(lambda e, ps=ps, vc=vc, j=j, n=n, sc=sc: e.matmul(ps[:, 128 * j:TB], VT[:, j, vc * 128:(vc + 1) * 128], sc[:, 0:n],
                                                                             start=False, stop=(j == 3)), r=[("VT", j), (sck, 0), (sck, 1)], w=[pk])
            for dc in range(2):
                ps, pk = self.psum(6 + dc)
                for j in range(4):
                    self.pe(lambda e, ps=ps, dc=dc, j=j: e.matmul(ps[:], KT[:, j, dc * 128:(dc + 1) * 128], VT[:, j, :],
                                                                  start=(j == 0), stop=(j == 3)), r=["KT", ("VT", j)], w=[pk])
                sk = ("STATE", h * 2 + dc)
                self.dve(lambda e, dc=dc, h=h, g512=g512: e.tensor_scalar(out=STATE[:, h * 2 + dc, :], in0=STATE[:, h * 2 + dc, :], scalar1=g512, scalar2=None,
                                                                          op0=ALU.mult), r=[sk], w=[sk])
                self.dve(lambda e, ps=ps, dc=dc, h=h, g511=g511: e.scalar_tensor_tensor(out=STATE[:, h * 2 + dc, :], in0=ps[:], scalar=g511, in1=STATE[:, h * 2 + dc, :],
                                                                                       op0=ALU.mult, op1=ALU.add), r=[pk, sk], w=[sk])
            pn, pnk = self.psum(5)
            for vc in range(4):
                ps, pk = po[vc]
                sq, sqk = self.ring("sq")
                self.act(lambda e, ps=ps, sq=sq: e.activation(out=sq[:], in_=ps[:], func=AF.Square), r=[pk], w=[sqk])
                self.pe(lambda e, sq=sq, vc=vc, pn=pn: e.matmul(pn[:], self.ONES[:], sq[:], start=(vc == 0), stop=(vc == 3)), r=[sqk, "ONES"], w=[pnk])
            self.act(lambda e, pn=pn: e.activation(out=self.RSTD[:], in_=pn[:], func=AF.Sqrt, bias=self.EPSC[:, 0:1], scale=1.0 / 512), r=[pnk, "EPSC"], w=["RSTD"])
            self.dve(lambda e: e.reciprocal(out=self.RSTD[:], in_=self.RSTD[:]), r=["RSTD"], w=["RSTD"])
            for vc in range(4):
                ps, pk = po[vc]
                t, tk = self.ring("t32")
                self.dve(lambda e, ps=ps, t=t: e.tensor_tensor(out=t[:], in0=ps[:], in1=self.RSTD[:], op=ALU.mult), r=[pk, "RSTD"], w=[tk])
                self.dve(lambda e, t=t, vc=vc, h=h: e.tensor_tensor(out=OALL[:, h * 4 + vc, :], in0=t[:], in1=SG[:, vc, :], op=ALU.mult),
                         r=[tk, ("SG", vc)], w=[("UB", h * 4 + vc)])
        self.out_proj32("retout", OALL, [("UB", c) for c in range(32)])

    def gla_setup(self):
        WG = self.LS[:, 4096:4608].bitcast(BF16)
        self.pool(lambda e: e.memset(self.LS[:, 0:4096], 0.0), w=[("STATE", j) for j in range(8)])
        tmp = self.YB[:, 14:16, :].rearrange("p c t -> p (c t)")
        yk = [("YB", 14), ("YB", 15)]
        self.dma("sp", tmp[0:32, :], self.gla_wg, w=yk)
        self.act(lambda e: e.activation(out=WG[0:32, :], in_=tmp[0:32, :], func=AF.Copy), r=yk, w=["WG"])

    def gla_mixer(self, tb):
        H = 4
        STATE = self.LS[:, 0:4096].rearrange("p (j v) -> p j v", v=512)
        WG = self.LS[:, 4096:4608].bitcast(BF16)
        OALL = self.UB
        mx = self.MX
        VT = mx[:, 4096:5120].bitcast(BF16).rearrange("p (g v) -> p g v", v=512)
        SG = mx[:, 5120:6144].bitcast(BF16).rearrange("p (c t) -> p c t", t=TB)
        KS = mx[:, 6144:6656].bitcast(BF16).rearrange("p (g d) -> p g d", d=256)
        SB = mx[:, 6656:7168].bitcast(BF16).rearrange("p (c v) -> p c v", v=512)
        GLOW = self.ysc(0, 256, BF16)
        LA = self.ysc(256, 1024).rearrange("p (g f) -> p g f", f=256)
        EP = self.ysc(1280, 1024).rearrange("p (c t) -> p c t", t=TB)
        EN = self.ysc(2304, 1024).rearrange("p (c t) -> p c t", t=TB)
        E2 = self.ysc(3328, 1024).rearrange("p (g f) -> p g f", f=256)
        QF = self.ysc(4352, 512, BF16).rearrange("p (c t) -> p c t", t=TB)
        QB = self.ysc(4864, 512, BF16).rearrange("p (c t) -> p c t", t=TB)
        KF = self.ysc(5376, 512, BF16).rearrange("p (c t) -> p c t", t=TB)
        KB = self.ysc(5888, 512, BF16).rearrange("p (c t) -> p c t", t=TB)
        GC = self.ysc(6400, 384)
        TRI, TRI2, UM, LM = GC[:, 0:128], GC[:, 128:256], GC[:, 256:320], GC[:, 320:384]
        SCT = self.ysc(6784, 32, BF16)
        T1 = self.ysc(6816, 64)
        T2 = self.ysc(6880, 64)
        hk = [("HB", c) for c in range(KC)]
        self.dma("sp", GC, self.gla_c, w=["GC"])
        self.pool(lambda e: e.memset(GLOW[0:32, :], 1.0), w=["GLOW"])
        wgl, wglk = self.wload("glag", 0)
        pg, pgk = self.psum(7)
        for kc in range(KC):
            self.pe(lambda e, kc=kc: e.matmul(pg[0:16, :], wgl[:, kc, :], self.HB[:, kc, :], start=(kc == 0), stop=(kc == KC - 1)),
                    r=[wglk] + hk, w=[pgk])
        self.act(lambda e: e.activation(out=GLOW[0:16, :], in_=pg[0:16, :], func=AF.Copy), r=[pgk, "GLOW"], w=["GLOW"])

        def proj_fm(wv, wk, j, bank):
            ps, pk = self.psum(bank)
            for kc in range(KC):
                self.pe(lambda e, ps=ps, wv=wv, kc=kc, j=j: e.matmul(ps[:], wv[:, kc, j * 128:(j + 1) * 128], self.HB[:, kc, :],
                                                                    start=(kc == 0), stop=(kc == KC - 1)), r=[wk] + hk, w=[pk])
            return ps, pk

        for h in range(H):
            for tt in range(4):
                ps, pk = self.psum(5 + tt % 2)
                self.pe(lambda e, ps=ps, tt=tt, h=h: e.matmul(ps[:, 0:256], GLOW[0:17, tt * 128:(tt + 1) * 128], WG[0:17, h * 256:(h + 1) * 256],
                                                              start=True, stop=True), r=["GLOW", "WG"], w=[pk])
                self.act(lambda e, ps=ps, tt=tt: e.activation(out=LA[:, tt, :], in_=ps[:, 0:256], func=AF.Exp, scale=-1.0), r=[pk], w=[("LA", tt)])
                self.act(lambda e, tt=tt: e.activation(out=LA[:, tt, :], in_=LA[:, tt, :], func=AF.Ln, bias=1.0), r=[("LA", tt)], w=[("LA", tt)])
            for dc in range(2):
                ps, pk = self.psum(5 + dc)
                for tt in range(4):
                    self.pe(lambda e, ps=ps, tt=tt, dc=dc: e.matmul(ps[:, tt * 128:(tt + 1) * 128], LA[:, tt, dc * 128:(dc + 1) * 128], TRI,
                                                                    start=True, stop=True), r=[("LA", tt), "GC"], w=[pk])
                self.act(lambda e, ps=ps, dc=dc: e.activation(out=EP[:, dc, :], in_=ps[:], func=AF.Exp), r=[pk], w=[("EP", dc)])
                self.act(lambda e, ps=ps, dc=dc: e.activation(out=EN[:, dc, :], in_=ps[:], func=AF.Exp, scale=-1.0), r=[pk], w=[("EN", dc)])
            for tt in range(4):
                ps, pk = self.psum(7)
                self.pe(lambda e, ps=ps, tt=tt: e.matmul(ps[:, 0:256], TRI2, LA[:, tt, :], start=True, stop=True), r=[("LA", tt), "GC"], w=[pk])
                self.act(lambda e, ps=ps, tt=tt: e.activation(out=E2[:, tt, :], in_=ps[:, 0:256], func=AF.Exp), r=[pk], w=[("E2", tt)])
            wq, wqk = self.wload("glain", h)
            for dc in range(2):
                ps, pk = proj_fm(wq, wqk, dc, 5 + dc)
                self.dve(lambda e, ps=ps, dc=dc: e.scalar_tensor_tensor(out=QF[:, dc, :], in0=ps[:], scalar=0.0625, in1=EP[:, dc, :], op0=ALU.mult, op1=ALU.mult),
                         r=[pk, ("EP", dc)], w=[("QF", dc)])
                self.dve(lambda e, ps=ps, dc=dc: e.scalar_tensor_tensor(out=QB[:, dc, :], in0=ps[:], scalar=0.0625, in1=EN[:, dc, :], op0=ALU.mult, op1=ALU.mult),
                         r=[pk, ("EN", dc)], w=[("QB", dc)])
            wkk, wkkk = self.wload("glain", 4 + h)
            for dc in range(2):
                ps, pk = proj_fm(wkk, wkkk, dc, 5 + dc)
                self.dve(lambda e, ps=ps, dc=dc: e.tensor_tensor(out=KF[:, dc, :], in0=ps[:], in1=EN[:, dc, :], op=ALU.mult), r=[pk, ("EN", dc)], w=[("KF", dc)])
                self.dve(lambda e, ps=ps, dc=dc: e.tensor_tensor(out=KB[:, dc, :], in0=ps[:], in1=EP[:, dc, :], op=ALU.mult), r=[pk, ("EP", dc)], w=[("KB", dc)])
            for tt in range(4):
                ps, pk = self.psum(5 + tt % 2)
                for kc in range(KC):
                    self.pe(lambda e, ps=ps, kc=kc, tt=tt, wkk=wkk: e.matmul(ps[:, 0:256], self.HB[:, kc, tt * 128:(tt + 1) * 128], wkk[:, kc, :],
                                                                    start=(kc == 0), stop=(kc == KC - 1)), r=[wkkk] + hk, w=[pk])
                self.dve(lambda e, ps=ps, tt=tt: e.tensor_tensor(out=KS[:, tt, :], in0=ps[:, 0:256], in1=E2[:, tt, :], op=ALU.mult), r=[pk, ("E2", tt)], w=[("KS", tt)])
            for half in range(2):
                wv, wvk = self.wload("glain", 8 + 2 * h + half)
                for tt in range(4):
                    ps, pk = self.psum(5 + tt % 2)
                    for kc in range(KC):
                        self.pe(lambda e, ps=ps, wv=wv, kc=kc, tt=tt: e.matmul(ps[:, 0:256], self.HB[:, kc, tt * 128:(tt + 1) * 128], wv[:, kc, :],
                                                                               start=(kc == 0), stop=(kc == KC - 1)), r=[wvk] + hk, w=[pk])
                    self.act(lambda e, ps=ps, tt=tt, half=half: e.activation(out=VT[:, tt, half * 256:(half + 1) * 256], in_=ps[:, 0:256], func=AF.Copy),
                             r=[pk], w=[("VT", tt)])
            for half in range(2):
                wr, wrk = self.wload("glain", 16 + 2 * h + half)
                for j in range(2):
                    ps, pk = proj_fm(wr, wrk, j, 5 + j)
                    vc = half * 2 + j
                    self.act(lambda e, ps=ps, vc=vc: e.activation(out=SG[:, vc, :], in_=ps[:], func=AF.Silu), r=[pk], w=[("SG", vc)])
            for dc in range(2):
                self.act(lambda e, dc=dc, h=h: e.activation(out=SB[:, dc, :], in_=STATE[:, h * 2 + dc, :], func=AF.Copy),
                         r=[("STATE", h * 2 + dc)], w=[("SB", dc)])
            if h == 0 and tb == 0:
                self.dbg("GLOW", GLOW[0:32, :], ["GLOW"], BF16)
                self.dbg("LA", LA, [("LA", t_) for t_ in range(4)])
                self.dbg("EP", EP, [("EP", 0), ("EP", 1)])
                self.dbg("EN", EN, [("EN", 0), ("EN", 1)])
                self.dbg("E2", E2, [("E2", t_) for t_ in range(4)])
                self.dbg("QF", QF, [("QF", 0), ("QF", 1)], BF16)
                self.dbg("KB", KB, [("KB", 0), ("KB", 1)], BF16)
                self.dbg("KS", KS, [("KS", t_) for t_ in range(4)], BF16)
                self.dbg("VT", VT, [("VT", t_) for t_ in range(4)], BF16)
                self.dbg("SG", SG, [("SG", t_) for t_ in range(4)], BF16)
            po = [self.psum(vc) for vc in range(4)]
            for n in range(8):
                tt, r0 = n // 2, (n % 2) * 64
                pss, psk = self.psum(4)
                cs = slice(n * 64, (n + 1) * 64)
                for dc in range(2):
                    self.pe(lambda e, dc=dc, cs=cs, r0=r0, pss=pss: e.matmul(pss[r0:r0 + 64, 0:64], KF[:, dc, cs], QF[:, dc, cs],
                                                                             start=(dc == 0), stop=(dc == 1)), r=[("KF", dc), ("QF", dc)], w=[psk])
                for dc in range(2):
                    self.pe(lambda e, dc=dc, cs=cs, r0=r0, pss=pss: e.matmul(pss[r0:r0 + 64, 64:128], KB[:, dc, cs], QB[:, dc, cs],
                                                                             start=(dc == 0), stop=(dc == 1)), r=[("KB", dc), ("QB", dc)], w=[psk])
                self.dve(lambda e, r0=r0, pss=pss: e.tensor_tensor(out=T1[r0:r0 + 64, :], in0=pss[r0:r0 + 64, 0:64], in1=UM[r0:r0 + 64, :], op=ALU.mult),
                         r=[psk, "GC"], w=["T1"])
                self.dve(lambda e, r0=r0, pss=pss: e.tensor_tensor(out=T2[r0:r0 + 64, :], in0=pss[r0:r0 + 64, 64:128], in1=LM[r0:r0 + 64, :], op=ALU.mult),
                         r=[psk, "GC"], w=["T2"])
                self.dve(lambda e, r0=r0: e.tensor_tensor(out=SCT[r0:r0 + 64, :], in0=T1[r0:r0 + 64, :], in1=T2[r0:r0 + 64, :], op=ALU.add),
                         r=["T1", "T2"], w=["SCT"])
                for vc in range(4):
                    ps, pk = po[vc]
                    self.pe(lambda e, ps=ps, vc=vc, cs=cs, tt=tt, r0=r0: e.matmul(ps[:, cs], VT[r0:r0 + 64, tt, vc * 128:(vc + 1) * 128], SCT[r0:r0 + 64, :],
                                                                                  start=True, stop=False), r=[("VT", tt), "SCT"], w=[pk])
                    for dc in range(2):
                        self.pe(lambda e, ps=ps, vc=vc, cs=cs, dc=dc: e.matmul(ps[:, cs], SB[:, dc, vc * 128:(vc + 1) * 128], QF[:, dc, cs],
                                                                               start=False, stop=(dc == 1)), r=[("SB", dc), ("QF", dc)], w=[pk])
                for dc in range(2):
                    ps, pk = self.psum(5 + dc)
                    self.pe(lambda e, ps=ps, dc=dc, tt=tt, r0=r0: e.matmul(ps[:], KS[r0:r0 + 64, tt, dc * 128:(dc + 1) * 128], VT[r0:r0 + 64, tt, :],
                                                                           start=True, stop=True), r=[("KS", tt), ("VT", tt)], w=[pk])
                    sk = ("STATE", h * 2 + dc)
                    self.dve(lambda e, ps=ps, dc=dc, h=h, n=n: e.scalar_tensor_tensor(
                        out=STATE[:, h * 2 + dc, :], in0=STATE[:, h * 2 + dc, :], scalar=EP[:, dc, n * 64 + 63:n * 64 + 64], in1=ps[:],
                        op0=ALU.mult, op1=ALU.add), r=[pk, sk, ("EP", dc)], w=[sk])
                    if n < 7:
                        self.act(lambda e, dc=dc, h=h: e.activation(out=SB[:, dc, :], in_=STATE[:, h * 2 + dc, :], func=AF.Copy), r=[sk], w=[("SB", dc)])
            pn, pnk = self.psum(7)
            for vc in range(4):
                ps, pk = po[vc]
                sq, sqk = self.ring("sq")
                self.act(lambda e, ps=ps, sq=sq: e.activation(out=sq[:], in_=ps[:], func=AF.Square), r=[pk], w=[sqk])
                self.pe(lambda e, sq=sq, vc=vc: e.matmul(pn[:], self.ONES[:], sq[:], start=(vc == 0), stop=(vc == 3)), r=[sqk, "ONES"], w=[pnk])
            self.act(lambda e: e.activation(out=self.RSTD[:], in_=pn[:], func=AF.Sqrt, bias=self.EPSC[:, 0:1], scale=1.0 / 512), r=[pnk, "EPSC"], w=["RSTD"])
            self.dve(lambda e: e.reciprocal(out=self.RSTD[:], in_=self.RSTD[:]), r=["RSTD"], w=["RSTD"])
            ng = self.vec("glng")
            for vc in range(4):
                ps, pk = po[vc]
                t, tk = self.ring("t32")
                self.dve(lambda e, ps=ps, t=t: e.tensor_tensor(out=t[:], in0=ps[:], in1=self.RSTD[:], op=ALU.mult), r=[pk, "RSTD"], w=[tk])
                self.dve(lambda e, t=t, vc=vc, h=h: e.scalar_tensor_tensor(out=OALL[:, h * 4 + vc, :], in0=t[:], scalar=ng[:, vc:vc + 1], in1=SG[:, vc, :],
                                                                           op0=ALU.mult, op1=ALU.mult), r=[tk, ("SG", vc), "VEC"], w=[("UB", h * 4 + vc)])
        if tb == 0:
            self.dbg("OALL", OALL[:, 0:16, :], [("UB", c) for c in range(16)], BF16)
        ok_ = [("UB", c) for c in range(KC)]
        for p in range(8):
            wv, wk = self.wload("glaout", p)
            for j in range(2):
                oc = 2 * p + j
                ps, pk = self.psum(5 + j)
                for kc in range(KC):
                    self.pe(lambda e, ps=ps, wv=wv, kc=kc, j=j: e.matmul(ps[:], wv[:, kc, j * 128:(j + 1) * 128], OALL[:, kc, :],
                                                                        start=(kc == 0), stop=(kc == KC - 1)), r=[wk] + ok_, w=[pk])
                self.act(lambda e, ps=ps, oc=oc: e.activation(out=self.YB[:, oc, :], in_=ps[:], func=AF.Copy), r=[pk], w=[("YB", oc)])

    def sgu_views(self):
        RWB = self.LS[:, 0:1024].rearrange("p (h i) -> p h i", i=128)
        BSB = self.LS[:, 1024:2048].rearrange("p (h i) -> p h i", i=128)
        WmT = self.LS[:, 2048:2560].bitcast(BF16).rearrange("p (h i) -> p h i", i=128)
        return RWB, BSB, WmT

    def sgu_setup(self):
        RWB, BSB, WmT = self.sgu_views()
        WS = self.YB[:, 0:2, :].rearrange("p c t -> p (c t)").rearrange("p (h i) -> p h i", i=128)
        yk = [("YB", 0), ("YB", 1)]
        self.dma("sp", self.YB[:, 0:2, :].rearrange("p c t -> p (c t)"), self.sgu_wsT, w=yk)
        self.dma("sp", self.LS[:, 1024:2048], self.sgu_bsb, w=["BSB"])
        self.pool(lambda e: e.memset(WS[64:128, :, 0:64], 0.0), r=yk, w=yk)
        self.act(lambda e: e.activation(out=WmT, in_=WS, func=AF.Copy), r=yk, w=["WmT"])
        for half in range(2):
            ps, pk = self.psum()
            self.pe(lambda e, ps=ps, half=half: e.matmul(ps[:], self.ONES[:], WmT[:, half * 4:(half + 1) * 4, :], start=True, stop=True),
                    r=["WmT", "ONES"], w=[pk])
            self.act(lambda e, ps=ps, half=half: e.activation(out=RWB[:, half * 4:(half + 1) * 4, :], in_=ps[:].rearrange("p (h i) -> p h i", i=128),
                                                              func=AF.Copy), r=[pk], w=["RWB"])

    def sgu_mixer(self, tb):
        RWB, BSB, WmT = self.sgu_views()
        U = self.UB
        Vb = self.YB[:, :, :].rearrange("p c t -> p (c t)").bitcast(BF16).rearrange("p (g f) -> p g f", f=4096)
        S1 = self.ST[:, 0:64].rearrange("p (g j) -> p g j", j=16)
        S2 = self.ST[:, 64:128].rearrange("p (g j) -> p g j", j=16)
        M1 = self.ST[:, 128:132]
        M2 = self.ST[:, 132:136]
        RS = self.ST[:, 136:140]
        NM = self.ST[:, 140:144]
        hk = [("HB", c) for c in range(KC)]

        def vbk(tt, j):
            return ("YB", 4 * tt + j // 4)
        for j in range(16):
            wv, wk = self.wload("sguin", 16 + j)
            for tt in range(4):
                ps, pk = self.psum()
                for kc in range(KC):
                    self.pe(lambda e, ps=ps, wv=wv, kc=kc, tt=tt: e.matmul(ps[:, 0:256], self.HB[:, kc, tt * 128:(tt + 1) * 128], wv[:, kc, :],
                                                                           start=(kc == 0), stop=(kc == KC - 1)), r=[wk] + hk, w=[pk])
                self.act(lambda e, ps=ps, tt=tt, j=j: e.activation(out=Vb[:, tt, j * 256:(j + 1) * 256], in_=ps[:, 0:256], func=AF.Gelu,
                                                                   accum_out=S1[:, tt, j:j + 1]), r=[pk], w=[vbk(tt, j), ("S1", tt, j)])
                jk, jkk = self.ring("sq")
                self.act(lambda e, jk=jk, tt=tt, j=j: e.activation(out=jk[:, 0:256], in_=Vb[:, tt, j * 256:(j + 1) * 256], func=AF.Square,
                                                                   accum_out=S2[:, tt, j:j + 1]), r=[vbk(tt, j)], w=[jkk, ("S2", tt, j)])
        for j in range(16):
            wu, wk = self.wload("sguin", j)
            for jj in range(2):
                c = 2 * j + jj
                ps, pk = self.psum()
                for kc in range(KC):
                    self.pe(lambda e, ps=ps, wu=wu, kc=kc, jj=jj: e.matmul(ps[:], wu[:, kc, jj * 128:(jj + 1) * 128], self.HB[:, kc, :],
                                                                           start=(kc == 0), stop=(kc == KC - 1)), r=[wk] + hk, w=[pk])
                self.act(lambda e, ps=ps, c=c: e.activation(out=U[:, c, :], in_=ps[:], func=AF.Gelu), r=[pk], w=[("UB", c)])
        s1k = [("S1", tt, j) for tt in range(4) for j in range(16)]
        s2k = [("S2", tt, j) for tt in range(4) for j in range(16)]
        X = mybir.AxisListType.X
        self.dve(lambda e: e.tensor_reduce(out=M1, in_=S1, axis=X, op=ALU.add), r=s1k, w=["M1"])
        self.dve(lambda e: e.tensor_reduce(out=M2, in_=S2, axis=X, op=ALU.add), r=s2k, w=["M2"])
        self.dve(lambda e: e.tensor_scalar(out=M1, in0=M1, scalar1=1.0 / 4096, scalar2=None, op0=ALU.mult), r=["M1"], w=["M1"])
        self.dve(lambda e: e.tensor_tensor(out=NM, in0=M1, in1=M1, op=ALU.mult), r=["M1"], w=["NM"])
        self.dve(lambda e: e.scalar_tensor_tensor(out=M2, in0=M2, scalar=1.0 / 4096, in1=NM, op0=ALU.mult, op1=ALU.subtract),
                 r=["M2", "NM"], w=["M2"])
        self.act(lambda e: e.activation(out=RS, in_=M2, func=AF.Sqrt, bias=self.EPSC[:, 0:1], scale=1.0), r=["M2", "EPSC"], w=["RS"])
        self.dve(lambda e: e.reciprocal(out=RS, in_=RS), r=["RS"], w=["RS"])
        self.dve(lambda e: e.scalar_tensor_tensor(out=NM, in0=M1, scalar=-1.0, in1=RS, op0=ALU.mult, op1=ALU.mult), r=["M1", "RS"], w=["NM"])
        for tt in range(4):
            ks = [("YB", 4 * tt + q) for q in range(4)]
            self.dve(lambda e, tt=tt: e.tensor_scalar(out=Vb[:, tt, :], in0=Vb[:, tt, :], scalar1=RS[:, tt:tt + 1], scalar2=NM[:, tt:tt + 1],
                                                      op0=ALU.mult, op1=ALU.add), r=ks + ["RS", "NM"], w=ks)
        lg, lb = self.vec("sglg"), self.vec("sglb")
        for c in range(32):
            h = c // 4
            ps, pk = self.psum()
            for tt in range(4):
                self.pe(lambda e, ps=ps, tt=tt, c=c, h=h: e.matmul(ps[:, tt * 128:(tt + 1) * 128], Vb[:, tt, c * 128:(c + 1) * 128], WmT[:, h, :],
                                                                    start=True, stop=True), r=[("YB", 4 * tt + c // 8), "WmT"], w=[pk])
            cb, cbk = self.ring("t32")
            self.dve(lambda e, cb=cb, c=c, h=h: e.scalar_tensor_tensor(out=cb[:, 0:128], in0=RWB[:, h, :], scalar=lb[:, c:c + 1], in1=BSB[:, h, :],
                                                                       op0=ALU.mult, op1=ALU.add), r=["RWB", "BSB", "VEC"], w=[cbk])
            sv, svk = self.ring("acc")
            self.dve(lambda e, cb=cb, sv=sv, ps=ps, c=c: e.scalar_tensor_tensor(
                out=sv[:].rearrange("p (g i) -> p g i", i=128), in0=ps[:].rearrange("p (g i) -> p g i", i=128), scalar=lg[:, c:c + 1],
                in1=cb[:, 0:128].unsqueeze(1).to_broadcast([128, 4, 128]), op0=ALU.mult, op1=ALU.add), r=[pk, cbk, "VEC"], w=[svk])
            self.dve(lambda e, sv=sv, c=c: e.tensor_tensor(out=U[:, c, :], in0=U[:, c, :], in1=sv[:], op=ALU.mult),
                     r=[svk, ("UB", c)], w=[("UB", c)])
        uk = [("UB", c) for c in range(32)]
        for g in range(8):
            pss = [self.psum() for _ in range(2)]
            for kt in range(2):
                wv, wk = self.wload("sguout", g * 2 + kt)
                for jj in range(2):
                    ps, pk = pss[jj]
                    for kc in range(KC):
                        self.pe(lambda e, ps=ps, wv=wv, kc=kc, jj=jj, kt=kt: e.matmul(
                            ps[:], wv[:, kc, jj * 128:(jj + 1) * 128], U[:, kt * 16 + kc, :],
                            start=(kt == 0 and kc == 0), stop=(kt == 1 and kc == KC - 1)), r=[wk] + uk[kt * 16:(kt + 1) * 16], w=[pk])
            for jj in range(2):
                ps, pk = pss[jj]
                oc = 2 * g + jj
                self.act(lambda e, ps=ps, oc=oc: e.activation(out=self.YB[:, oc, :], in_=ps[:], func=AF.Copy), r=[pk], w=[("YB", oc)])

    def ln_stats(self, src, skey, nch, nfeat):
        ps1, pk1 = self.psum()
        ps2, pk2 = self.psum()
        for c in range(nch):
            zb, zk = self.ring("sq")
            sq, sk = self.ring("sq")
            self.act(lambda e, zb=zb, c=c: e.activation(out=zb[:], in_=src[:, c, :], func=AF.Copy), r=[(skey, c)], w=[zk])
            self.act(lambda e, sq=sq, c=c: e.activation(out=sq[:], in_=src[:, c, :], func=AF.Square), r=[(skey, c)], w=[sk])
            self.pe(lambda e, zb=zb, c=c: e.matmul(ps1[:], self.ONES[:], zb[:], start=(c == 0), stop=(c == nch - 1)),
                    r=[zk, "ONES"], w=[pk1])
            self.pe(lambda e, sq=sq, c=c: e.matmul(ps2[:], self.ONES[:], sq[:], start=(c == 0), stop=(c == nch - 1)),
                    r=[sk, "ONES"], w=[pk2])
        MU, RS = self.MU, self.RSTD
        self.act(lambda e: e.activation(out=MU, in_=ps1[:], func=AF.Copy, scale=1.0 / nfeat), r=[pk1], w=["MU"])
        t, tk = self.ring("t32")
        self.dve(lambda e, t=t: e.tensor_tensor(out=t[:], in0=MU, in1=MU, op=ALU.mult), r=["MU"], w=[tk])
        self.dve(lambda e, t=t: e.scalar_tensor_tensor(out=t[:], in0=ps2[:], scalar=1.0 / nfeat, in1=t[:],
                                                       op0=ALU.mult, op1=ALU.subtract), r=[pk2, tk], w=[tk])
        self.act(lambda e, t=t: e.activation(out=RS[:], in_=t[:], func=AF.Sqrt, bias=self.EPSC[:, 0:1], scale=1.0),
                 r=[tk, "EPSC"], w=["RSTD"])
        self.dve(lambda e: e.reciprocal(out=RS[:], in_=RS[:]), r=["RSTD"], w=["RSTD"])

    def conv_setup(self):
        self.specs["cmdg"] = dict(nk=31, fw=128, E=31 * 128, nt=16)
        self.wb["cmdg"] = self.dram("cmdg_b", [16, 128, 31 * 128], BF16, "Internal")
        for fc in range(KC):
            slot = self.wslot_i % NWSLOT
            self.wslot_i += 1
            buf = self.WR[slot]
            key = ("WR", slot)
            for k in range(31):
                self.dve(lambda e, buf=buf, k=k, fc=fc: e.tensor_scalar(out=buf[:, k * 128:(k + 1) * 128], in0=self.IDENT[:],
                                                                        scalar1=self.vec(f"cmw{k}")[:, fc:fc + 1], scalar2=None, op0=ALU.mult),
                         r=["IDENT", "VEC"], w=[key])
            self.dma("sp", self.wb["cmdg"][fc], buf[:, 0:31 * 128], r=[key], w=[("wb", "cmdg", 0)])

    def conv_mixer(self, tb):
        W_ = TB + 30
        GL = self.MX[:, 0:KC * W_ // 2].bitcast(BF16).rearrange("p (c t) -> p c t", t=W_)
        CT = self.LS[:, 0:KC * 15].bitcast(BF16).rearrange("p (c t) -> p c t", t=30)
        hk = [("HB", c) for c in range(KC)]
        for p in range(8):
            wa, wak = self.wload("cmin", p)
            wg, wgk = self.wload("cmin", 8 + p)
            for j in range(2):
                fc = 2 * p + j
                pa, pak = self.psum()
                pg, pgk = self.psum()
                for kc in range(KC):
                    self.pe(lambda e, pa=pa, j=j, kc=kc, wa=wa: e.matmul(pa[:], wa[:, kc, j * 128:(j + 1) * 128], self.HB[:, kc, :],
                                                                        start=(kc == 0), stop=(kc == KC - 1)), r=[wak] + hk, w=[pak])
                for kc in range(KC):
                    self.pe(lambda e, pg=pg, j=j, kc=kc, wg=wg: e.matmul(pg[:], wg[:, kc, j * 128:(j + 1) * 128], self.HB[:, kc, :],
                                                                        start=(kc == 0), stop=(kc == KC - 1)), r=[wgk] + hk, w=[pgk])
                sg, sgk = self.ring("t32")
                self.act(lambda e, sg=sg, pg=pg: e.activation(out=sg[:], in_=pg[:], func=AF.Sigmoid), r=[pgk], w=[sgk])
                if tb == 0:
                    self.pool(lambda e, fc=fc: e.memset(GL[:, fc, 0:30], 0.0), w=[("GLt", fc)])
                else:
                    self.pool(lambda e, fc=fc: e.tensor_copy(out=GL[:, fc, 0:30], in_=CT[:, fc, :]),
                              r=[("CT", fc)], w=[("GLt", fc)])
                self.dve(lambda e, fc=fc, pa=pa, sg=sg: e.tensor_tensor(out=GL[:, fc, 30:TB + 30], in0=pa[:], in1=sg[:], op=ALU.mult),
                         r=[pak, sgk, ("GLt", fc)], w=[("GL", fc)])
                self.pool(lambda e, fc=fc: e.tensor_copy(out=CT[:, fc, :], in_=GL[:, fc, TB:TB + 30]),
                          r=[("GL", fc)], w=[("CT", fc)])
        for fc in range(KC):
            wd, wdk = self.wload("cmdg", fc)
            ps, pk = self.psum()
            for k in range(31):
                self.pe(lambda e, ps=ps, wd=wd, k=k, fc=fc: e.matmul(ps[:], wd[:, k, :], GL[:, fc, k:k + TB], start=(k == 0), stop=(k == 30)),
                        r=[wdk, ("GL", fc), ("GLt", fc)], w=[pk])
            self.act(lambda e, ps=ps, fc=fc: e.activation(out=self.YB[:, fc, :], in_=ps[:], func=AF.Identity,
                                                          bias=self.vec("cmb")[:, fc:fc + 1], scale=1.0), r=[pk, "VEC"], w=[("YB", fc)])
        self.ln_stats(self.YB, "YB", KC, D)
        lg, lb = self.vec("cmlg"), self.vec("cmlb")
        for c in range(KC):
            t, tk = self.ring("t32")
            self.dve(lambda e, t=t, c=c: e.tensor_tensor(out=t[:], in0=self.YB[:, c, :], in1=self.MU, op=ALU.subtract),
                     r=[("YB", c), "MU"], w=[tk])
            self.dve(lambda e, t=t: e.tensor_tensor(out=t[:], in0=t[:], in1=self.RSTD[:], op=ALU.mult), r=[tk, "RSTD"], w=[tk])
            self.act(lambda e, t=t, c=c: e.activation(out=self.HB[:, c, :], in_=t[:], func=AF.Silu,
                                                      scale=lg[:, c:c + 1], bias=lb[:, c:c + 1]), r=[tk, "VEC"], w=[("HB", c)])
        for p in range(8):
            wv, wk = self.wload("cmout", p)
            for j in range(2):
                oc = 2 * p + j
                ps, pk = self.psum()
                for kc in range(KC):
                    self.pe(lambda e, ps=ps, wv=wv, kc=kc, j=j: e.matmul(ps[:], wv[:, kc, j * 128:(j + 1) * 128], self.HB[:, kc, :],
                                                                        start=(kc == 0), stop=(kc == KC - 1)), r=[wk] + hk, w=[pk])
                self.act(lambda e, ps=ps, oc=oc: e.activation(out=self.YB[:, oc, :], in_=ps[:], func=AF.Copy),
                         r=[pk], w=[("YB", oc)])


def build_program(cfg, vec_off, nvec):
    b = Builder(cfg, vec_off, nvec)
    with b.st:
        b.EPSC = b.sb("EPSC", [128, 1], F32)
        b.S.add("dve", lambda e: e.memset(b.EPSC[:], EPS), (), ["EPSC"])
        nc = b.build()
    return nc


FULL_CFG = dict(layers=[0, 1, 2, 3], ntb=NTB, mixer=True)


def run(inp, cfg, core_ids=None, trace=False):
    shared, percore, voff = prep_inputs(inp, cfg)
    nvec = shared["vecs"].shape[1]
    nc = build_program(cfg, voff, nvec)
    if core_ids is None:
        core_ids = list(range(NB))
    in_maps = []
    for b in core_ids:
        m = dict(shared)
        m.update(percore[b])
        in_maps.append(m)
    res = run_bass_kernel_spmd(nc, in_maps, core_ids=list(range(len(core_ids))), trace=trace)
    return res


def kernel(**inp):
    inp = {k: np.asarray(v) for k, v in inp.items()}
    res = run(inp, FULL_CFG)
    out = np.stack([np.ascontiguousarray(res.results[b]["outT"].T) for b in range(NB)], 0)
    return out.astype(np.float32)
```
